# Optimizing a Trainium2 kernel written in Bass

```python
import jax, jax.numpy as jnp
from jax import lax
import numpy as np

D_MODEL = 4096
BATCH = 4
SEQ = 2048
DEPTH = 2
DEC_BATCH = 8
DEC_SEQ = 8
PAST_LEN = 16384
PAGE_SIZE = 128

N_A_LAYERS = DEPTH // 2
N_B_LAYERS = DEPTH - N_A_LAYERS
RWKV_HEAD = 64
RWKV_HEADS = D_MODEL // RWKV_HEAD
RWKV_LORA_W = 128
RWKV_LORA_A = 128
GN_EPS = 64e-5
HEAD_DIM = 128
N_Q_HEADS = D_MODEL // HEAD_DIM
N_KV_HEADS = 8
GQA = N_Q_HEADS // N_KV_HEADS
WINDOWS = (128, 512, 2048)
DILATIONS = (1, 4, 16)
N_GROUPS = len(WINDOWS)
MAX_WINDOW = max(WINDOWS)
NORM_EPS = 1e-6

kernel_name = 'rwkv7_yoco_dilated_window_step'


def rms_norm(x, g):
    xf = x.astype(jnp.float32)
    y = xf * lax.rsqrt(jnp.mean(xf * xf, axis=-1, keepdims=True) + NORM_EPS)
    return (y * g.astype(jnp.float32)).astype(x.dtype)


def rwkv7_mixer(h, x_prev, s0, g_norm, mu, w_in, w0, w1, w2, a0, a1, a2,
                k_k, k_a, r_k, lnx_g, lnx_b, w_out):
    b, t, d = h.shape
    H, N = RWKV_HEADS, RWKV_HEAD
    f32 = jnp.float32
    xn = rms_norm(h, g_norm)
    xx = jnp.concatenate([x_prev[:, None].astype(xn.dtype), xn[:, :-1]], axis=1) - xn
    mixed = xn[:, :, None, :] + xx[:, :, None, :] * mu
    rkvg = jnp.einsum('btjd,djc->btjc', mixed[:, :, :4], w_in.reshape(d, 4, d))
    r = rkvg[:, :, 0].astype(f32)
    k = rkvg[:, :, 1].astype(f32)
    v = rkvg[:, :, 2].astype(f32)
    gp = rkvg[:, :, 3]
    w_log = -jax.nn.softplus(-(w0 + jnp.tanh(mixed[:, :, 4] @ w1) @ w2)) - 0.5
    decay = jnp.exp(-jnp.exp(w_log.astype(f32)))
    a = jax.nn.sigmoid(a0 + (mixed[:, :, 5] @ a1) @ a2).astype(f32)
    kk = (k * k_k).reshape(b, t, H, N)
    kk = kk / jnp.maximum(jnp.sqrt(jnp.sum(kk * kk, axis=-1, keepdims=True)), 1e-12)
    k = k * (1.0 + (a - 1.0) * k_a)
    heads = lambda z: z.reshape(b, t, H, N)
    r, k, v, a, decay = heads(r), heads(k), heads(v), heads(a), heads(decay)

    def step(S, inp):
        r_t, w_t, k_t, v_t, kk_t, a_t = inp
        sk = jnp.einsum('bhvk,bhk->bhv', S, kk_t)
        S = (S * w_t[:, :, None, :] - sk[..., None] * (kk_t * a_t)[:, :, None, :]
             + v_t[..., None] * k_t[:, :, None, :])
        return S, jnp.einsum('bhvk,bhk->bhv', S, r_t)

    xs = tuple(jnp.moveaxis(z, 1, 0) for z in (r, decay, k, v, kk, a))
    s_t, ys = lax.scan(step, s0.astype(f32), xs)
    y = jnp.moveaxis(ys, 0, 1)
    mean = jnp.mean(y, axis=-1, keepdims=True)
    var = jnp.mean(jnp.square(y - mean), axis=-1, keepdims=True)
    y = (y - mean) * lax.rsqrt(var + GN_EPS) * lnx_g.reshape(H, N) + lnx_b.reshape(H, N)
    y = y + jnp.sum(r * k * r_k, axis=-1, keepdims=True) * v
    out = (y.reshape(b, t, d).astype(h.dtype) * jax.nn.silu(gp)) @ w_out
    return h + out, s_t, xn[:, -1]


def shared_kv(h, g_norm, w_kv, g_k):
    b, t = h.shape[:2]
    kv = rms_norm(h, g_norm) @ w_kv
    k, v = jnp.split(kv, 2, axis=-1)
    k = rms_norm(k.reshape(b, t, N_KV_HEADS, HEAD_DIM), g_k)
    return k, v.reshape(b, t, N_KV_HEADS, HEAD_DIM)


def dsw_queries(h, g_norm, w_in, g_q):
    b, t = h.shape[:2]
    z = rms_norm(h, g_norm) @ w_in
    qw = N_GROUPS * N_Q_HEADS * HEAD_DIM
    q = z[..., :qw].reshape(b, t, N_GROUPS, N_KV_HEADS, GQA, HEAD_DIM)
    q = rms_norm(q, g_q[:, None, None, :]).astype(jnp.float32) * HEAD_DIM ** -0.5
    return q, z[..., qw:]


def dilated_band_attention(q, k, v, dil, nback):
    b, s = q.shape[:2]
    s_sub = s // dil
    nb = -(-s_sub // nback)
    pad = nb * nback - s_sub

    def by_residue(x, front):
        x = x.reshape((b, s_sub, dil) + x.shape[2:])
        return jnp.pad(x, [(0, 0), (front, pad)] + [(0, 0)] * (x.ndim - 2))

    qb = by_residue(q, 0).reshape((b, nb, nback, dil) + q.shape[2:])

    def key_band(x):
        x = by_residue(x, nback).reshape((b, nb + 1, nback, dil) + x.shape[2:])
        return jnp.concatenate([x[:, :-1], x[:, 1:]], axis=2)

    kb, vb = key_band(k), key_band(v)
    sc = jnp.einsum('bnqrhgc,bnkrhc->bnrhgqk', qb, kb)
    qi = jnp.arange(nback)[:, None]
    kj = jnp.arange(2 * nback)[None, :]
    dist = qi + nback - kj
    blk = jnp.arange(nb)[:, None, None]
    valid = (dist >= 0) & (dist <= nback) & (blk * nback + kj >= nback)
    sc = jnp.where(valid[None, :, None, None, None], sc, -jnp.inf)
    m = jnp.max(sc, axis=-1, keepdims=True)
    p = jnp.exp(sc - m)
    den = jnp.moveaxis(jnp.sum(p, axis=-1), -1, 2)
    lse = jnp.moveaxis(m[..., 0], -1, 2) + jnp.log(den)
    o = jnp.einsum('bnrhgqk,bnkrhc->bnqrhgc', p, vb) / den[..., None]
    o = o.reshape((b, nb * nback, dil) + o.shape[4:])[:, :s_sub].reshape((b, s) + o.shape[4:])
    lse = lse.reshape((b, nb * nback, dil) + lse.shape[4:])[:, :s_sub].reshape((b, s) + lse.shape[4:])
    return o, lse


def dilated_gather_attention(q, kc, vc, buf_len, dil, nback):
    t = q.shape[1]
    idx = buf_len + jnp.arange(t)[:, None] - dil * jnp.arange(nback + 1)[None, :]
    valid = idx >= 0
    idx = jnp.maximum(idx, 0)
    kg, vg = kc[:, idx], vc[:, idx]
    sc = jnp.einsum('bthgc,btmhc->bthgm', q, kg)
    sc = jnp.where(valid[None, :, None, None, :], sc, -jnp.inf)
    lse = jax.nn.logsumexp(sc, axis=-1)
    p = jnp.exp(sc - lse[..., None])
    return jnp.einsum('bthgm,btmhc->bthgc', p, vg), lse


def merge_groups(h, outs, gate, w_out):
    b, t = h.shape[:2]
    o = jnp.stack([o_g for o_g, _ in outs])
    lse = jnp.stack([l_g for _, l_g in outs])
    wts = jax.nn.softmax(lse, axis=0)
    o = jnp.sum(wts[..., None] * o, axis=0).reshape(b, t, -1).astype(h.dtype)
    return h + (o * jax.nn.silu(gate)) @ w_out


def dsw_prompt_layer(h, k, v, g_norm, w_in, g_q, w_out):
    q, gate = dsw_queries(h, g_norm, w_in, g_q)
    kf, vf = k.astype(jnp.float32), v.astype(jnp.float32)
    outs = [dilated_band_attention(q[:, :, gi], kf, vf, dil, win // dil)
            for gi, (win, dil) in enumerate(zip(WINDOWS, DILATIONS))]
    return merge_groups(h, outs, gate, w_out)


def dsw_sample_layer(h, kc, vc, buf_len, g_norm, w_in, g_q, w_out):
    q, gate = dsw_queries(h, g_norm, w_in, g_q)
    kf, vf = kc.astype(jnp.float32), vc.astype(jnp.float32)
    outs = [dilated_gather_attention(q[:, :, gi], kf, vf, buf_len, dil, win // dil)
            for gi, (win, dil) in enumerate(zip(WINDOWS, DILATIONS))]
    return merge_groups(h, outs, gate, w_out)


def setup_inputs(seed: int = 0) -> dict:
    key = jax.random.key(seed)
    keys = iter(jax.random.split(key, 32))
    f32 = jnp.float32

    def nrm(shape, scale):
        return jax.random.normal(next(keys), shape, f32) * scale

    def gain(shape):
        return 1.0 + nrm(shape, 0.02)

    D = D_MODEL
    NA, NB = N_A_LAYERS, N_B_LAYERS
    H, N = RWKV_HEADS, RWKV_HEAD
    buf = min(MAX_WINDOW, PAST_LEN)
    q_width = N_GROUPS * N_Q_HEADS * HEAD_DIM
    attn_width = N_Q_HEADS * HEAD_DIM
    return {
        'x_prompt': nrm((BATCH, SEQ, D), 1.0),
        'x_sample': nrm((DEC_BATCH, DEC_SEQ, D), 1.0),
        'state_wkv': nrm((NA, DEC_BATCH, H, N, N), 0.1),
        'state_shift': nrm((NA, DEC_BATCH, D), 1.0),
        'cache_k': nrm((DEC_BATCH, buf, N_KV_HEADS, HEAD_DIM), 1.0),
        'cache_v': nrm((DEC_BATCH, buf, N_KV_HEADS, HEAD_DIM), 1.0),
        'a_norm_g': gain((NA, D)),
        'a_mu': jax.random.uniform(next(keys), (NA, 6, D), f32),
        'a_w_in': nrm((NA, D, 4 * D), D ** -0.5),
        'a_w0': jax.random.uniform(next(keys), (NA, D), f32, -6.0, -1.0),
        'a_w1': nrm((NA, D, RWKV_LORA_W), D ** -0.5),
        'a_w2': nrm((NA, RWKV_LORA_W, D), 0.1 * RWKV_LORA_W ** -0.5),
        'a_a0': nrm((NA, D), 0.1),
        'a_a1': nrm((NA, D, RWKV_LORA_A), D ** -0.5),
        'a_a2': nrm((NA, RWKV_LORA_A, D), 0.1 * RWKV_LORA_A ** -0.5),
        'a_k_k': 0.85 + nrm((NA, D), 0.02),
        'a_k_a': gain((NA, D)),
        'a_r_k': nrm((NA, H, N), 0.1),
        'a_lnx_g': gain((NA, D)),
        'a_lnx_b': nrm((NA, D), 0.01),
        'a_w_out': nrm((NA, D, D), D ** -0.5),
        'kv_norm_g': gain((D,)),
        'w_kv': nrm((D, 2 * N_KV_HEADS * HEAD_DIM), D ** -0.5),
        'k_norm_g': gain((HEAD_DIM,)),
        'b_norm_g': gain((NB, D)),
        'b_w_in': nrm((NB, D, q_width + attn_width), D ** -0.5),
        'q_norm_g': gain((NB, N_GROUPS, HEAD_DIM)),
        'b_w_out': nrm((NB, attn_width, D), attn_width ** -0.5),
    }


def reference(x_prompt, x_sample, state_wkv, state_shift, cache_k, cache_v,
              a_norm_g, a_mu, a_w_in, a_w0, a_w1, a_w2, a_a0, a_a1, a_a2,
              a_k_k, a_k_a, a_r_k, a_lnx_g, a_lnx_b, a_w_out,
              kv_norm_g, w_kv, k_norm_g,
              b_norm_g, b_w_in, q_norm_g, b_w_out):
    bp = x_prompt.shape[0]
    buf_len = cache_k.shape[1]
    hp, hs = x_prompt, x_sample
    wkv_p, shift_p, wkv_s, shift_s = [], [], [], []
    k_pr = v_pr = k_sm = v_sm = kc = vc = None
    for layer in range(DEPTH):
        if layer < N_A_LAYERS:
            par = (a_norm_g[layer], a_mu[layer], a_w_in[layer], a_w0[layer], a_w1[layer],
                   a_w2[layer], a_a0[layer], a_a1[layer], a_a2[layer], a_k_k[layer],
                   a_k_a[layer], a_r_k[layer], a_lnx_g[layer], a_lnx_b[layer], a_w_out[layer])
            zero_shift = jnp.zeros((bp, D_MODEL), hp.dtype)
            zero_wkv = jnp.zeros((bp, RWKV_HEADS, RWKV_HEAD, RWKV_HEAD), jnp.float32)
            hp, sp, xp = rwkv7_mixer(hp, zero_shift, zero_wkv, *par)
            hs, ss, xs = rwkv7_mixer(hs, state_shift[layer], state_wkv[layer], *par)
            wkv_p.append(sp)
            shift_p.append(xp)
            wkv_s.append(ss)
            shift_s.append(xs)
            if layer == N_A_LAYERS - 1:
                k_pr, v_pr = shared_kv(hp, kv_norm_g, w_kv, k_norm_g)
                k_sm, v_sm = shared_kv(hs, kv_norm_g, w_kv, k_norm_g)
                kc = jnp.concatenate([cache_k.astype(k_sm.dtype), k_sm], axis=1)
                vc = jnp.concatenate([cache_v.astype(v_sm.dtype), v_sm], axis=1)
        else:
            j = layer - N_A_LAYERS
            hp = dsw_prompt_layer(hp, k_pr, v_pr, b_norm_g[j], b_w_in[j], q_norm_g[j], b_w_out[j])
            hs = dsw_sample_layer(hs, kc, vc, buf_len, b_norm_g[j], b_w_in[j], q_norm_g[j], b_w_out[j])
    tail = min(MAX_WINDOW, x_prompt.shape[1])
    sd, hd, cd = state_wkv.dtype, state_shift.dtype, cache_k.dtype
    return (hp, hs,
            jnp.stack(wkv_p).astype(sd), jnp.stack(shift_p).astype(hd),
            k_pr[:, -tail:].astype(cd), v_pr[:, -tail:].astype(cd),
            jnp.stack(wkv_s).astype(sd), jnp.stack(shift_s).astype(hd),
            k_sm.astype(cd), v_sm.astype(cd))
```

```python
import os
import numpy as np
import ml_dtypes
A2CUT = int(os.environ.get("A2CUT", "9"))
from contextlib import ExitStack
import concourse.bass as bass
import concourse.mybir as mybir
from concourse.bass_utils import run_bass_kernel_spmd

F32 = mybir.dt.float32
BF16 = mybir.dt.bfloat16
AF = mybir.ActivationFunctionType
ALU = mybir.AluOpType
AX = mybir.AxisListType

D = 4096
NKC = 32
C0 = float(np.exp(-0.5))
V_G, V_MU, V_W0, V_A0, V_KK, V_KA, V_RK, V_LG, V_LB, V_KVG, V_BG = 0, 1, 7, 8, 9, 10, 11, 12, 13, 14, 15
NV = 16
DILS = (1, 4, 16)
NDELTA = (2, 5, 17)
MOFF = (0, 2, 7)


class T:
    __slots__ = ("h", "name", "lw", "rd", "dsem", "dcnt")

    def __init__(self, h, name):
        self.h = h
        self.name = name
        self.lw = None
        self.rd = {}
        self.dsem = None
        self.dcnt = 0

    def __getitem__(self, k):
        return self.h[k]


class Eng:
    def __init__(self, name, sem):
        self.name = name
        self.sem = sem
        self.cnt = 0
        self.epoch = 0
        self.waited = {}
        self.prog = []


class _Rec:
    def __init__(self):
        self.call = None

    def __getattr__(self, name):
        def f(*args, **kw):
            self.call = (name, args, kw)
        return f


class Scope(ExitStack):
    def __init__(self):
        super().__init__()
        self.tiles = []


class Ctx:
    def __init__(self, nc, es):
        self.free_dsems = []
        self.nc = nc
        self.es = es
        self.eng = {}
        for name in ("pe", "act", "dve", "pool", "sp"):
            sem = es.enter_context(nc.semaphore("s_" + name))
            self.eng[name] = Eng(name, sem)
        self.nt = 0
        self.dsems = []

    def sb(self, es, shape, dt, name):
        self.nt += 1
        h = es.enter_context(self.nc.sbuf_tensor(f"{name}_{self.nt}", list(shape), dt))
        t = T(h, f"{name}_{self.nt}")
        if isinstance(es, Scope):
            es.tiles.append(t)
        return t

    def ps(self, es, shape, dt, name):
        self.nt += 1
        h = es.enter_context(self.nc.psum_tensor(f"{name}_{self.nt}", list(shape), dt))
        return T(h, f"{name}_{self.nt}")

    def view(self, h, name):
        self.nt += 1
        return T(h, f"{name}_{self.nt}")

    def _need(self, e, reads, writes):
        need = {}

        def add(ev):
            if ev is None:
                return
            kind, key, val = ev
            k = (kind, key if kind == "e" else key.num)
            if k not in need or need[k][2] < val:
                need[k] = ev
        for t in reads:
            add(t.lw)
        for t in writes:
            add(t.lw)
            for ev in t.rd.values():
                add(ev)
        for k, ev in need.items():
            kind, key, val = ev
            if kind == "e":
                if key[1] != self.eng[key[0]].epoch:
                    continue
                if key[0] == "pe" and e.name == "pe":
                    continue
                if e.waited.get(k, 0) >= val:
                    continue
                e.waited[k] = val
                e.prog.append(("w", self.eng[key[0]].sem, val))
                continue
            if e.waited.get(k, 0) >= val:
                continue
            e.waited[k] = val
            e.prog.append(("w", self.eng[key].sem if kind == "e" else key, val))

    def op(self, ename, fn, reads=(), writes=()):
        e = self.eng[ename]
        self._need(e, reads, writes)
        e.cnt += 1
        rec = _Rec()
        fn(rec)
        name, args, kw = rec.call
        e.prog.append(("i", (lambda h, name=name, args=args, kw=kw: getattr(h, name)(*args, **kw)), e.sem, 1))
        ev = ("e", (ename, e.epoch), e.cnt)
        for t in reads:
            t.rd[("e", ename)] = ev
        for t in writes:
            t.lw = ev
            t.rd = {}

    def dma(self, qname, out, in_, anchor, reads=(), writes=()):
        e = self.eng[qname]
        self._need(e, reads, writes)
        if anchor.dsem is None:
            if self.free_dsems:
                anchor.dsem, anchor.dcnt = self.free_dsems.pop()
            else:
                anchor.dsem = self.es.enter_context(self.nc.semaphore("d_" + anchor.name))
            self.dsems.append(anchor)
        e.prog.append(("i", (lambda h, out=out, in_=in_: h.dma_start(out=out, in_=in_)), anchor.dsem, 16))
        anchor.dcnt += 16
        assert anchor.dcnt < 60000
        ev = ("d", anchor.dsem, anchor.dcnt)
        for t in reads:
            t.rd[("d", anchor.dsem.num)] = ev
        for t in writes:
            t.lw = ev
            t.rd = {}

    def barrier(self):
        for e in self.eng.values():
            for o in self.eng.values():
                if o is e or o.cnt == 0:
                    continue
                k = ("e", (o.name, o.epoch))
                if e.waited.get(k, 0) >= o.cnt:
                    continue
                e.waited[k] = o.cnt
                e.prog.append(("w", o.sem, o.cnt))
            for a in self.dsems:
                k = ("d", a.dsem.num)
                if e.waited.get(k, 0) >= a.dcnt:
                    continue
                e.waited[k] = a.dcnt
                e.prog.append(("w", a.dsem, a.dcnt))

    def new_epoch(self):
        for e in self.eng.values():
            e.sem = self.es.enter_context(self.nc.semaphore(f"s_{e.name}_{e.epoch + 1}"))
            e.cnt = 0
            e.epoch += 1
            e.waited = {k: v for k, v in e.waited.items() if k[0] != "e"}

    def end_scope(self, sc):
        self.barrier()
        if max(e.cnt for e in self.eng.values()) > 30000:
            self.new_epoch()
        for t in sc.tiles:
            if t.dsem is not None:
                self.free_dsems.append((t.dsem, t.dcnt))
                self.dsems.remove(t)
                t.dsem = None

    def emit(self):
        nc = self.nc
        handles = {"pe": "tensor", "act": "scalar", "dve": "vector", "pool": "gpsimd", "sp": "sync"}
        with nc.Block() as block:
            def mk(e):
                def body(h):
                    pend = []
                    for a in e.prog:
                        if a[0] == "w":
                            pend.append(a)
                            continue
                        for w in pend[:-1]:
                            h.wait_ge(w[1], w[2])
                        ins = a[1](h)
                        if pend:
                            ins._wait_ge(pend[-1][1], pend[-1][2])
                        ins.then_inc(a[2], a[3])
                        pend = []
                    for w in pend:
                        h.wait_ge(w[1], w[2])
                return body
            for n, attr in handles.items():
                getattr(block, attr)(mk(self.eng[n]))


class Job:
    pass


def build_program(SP, do_sample=True, SC=2048, phases="AB0123456789", do_prompt=True):
    nc = bass.Bass("TRN2", target_bir_lowering=False)

    def din(name, shape, dt=F32):
        return nc.dram_tensor(name, list(shape), dt, kind="ExternalInput").ap()

    def dout(name, shape, dt=F32):
        return nc.dram_tensor(name, list(shape), dt, kind="ExternalOutput").ap()

    def dscr(name, shape, dt=F32):
        return nc.dram_tensor(name, list(shape), dt).ap()

    A = {}
    A["xp"] = din("xp", [SP, D])
    A["xs"] = din("xs", [8, D])
    A["swkv"] = din("swkv", [D, 64])
    A["sshift"] = din("sshift", [128, NKC])
    A["ck"] = din("ck", [SC, 1024])
    A["cv"] = din("cv", [SC, 1024])
    A["a_w_in"] = din("a_w_in", [D, 4 * D])
    A["a_w1"] = din("a_w1", [D, 128])
    A["a_w2"] = din("a_w2", [128, D])
    A["a_a1"] = din("a_a1", [D, 128])
    A["a_a2"] = din("a_a2", [128, D])
    A["a_w_out"] = din("a_w_out", [D, D])
    A["w_kv"] = din("w_kv", [D, 2048])
    A["b_w_in"] = din("b_w_in", [D, 4 * D])
    A["b_w_out"] = din("b_w_out", [D, D])
    A["vecs"] = din("vecs", [128, NV, NKC])
    A["grow"] = din("grow", [D])
    A["gq"] = din("gq", [128, 3])
    A["gkb"] = din("gkb", [128, 128])
    A["identf"] = din("identf", [128, 128])
    A["bones"] = din("bones", [128, 128])
    A["rmask"] = din("rmask", [128, 3, 4, 128])
    A["amask"] = din("amask", [128, 24, 128])

    O = {}
    O["yp"] = dout("yp", [SP, D])
    O["ys"] = dout("ys", [8, D])
    O["wkvp"] = dout("wkvp", [64, 64, 64])
    O["shiftp"] = dout("shiftp", [D])
    O["kp"] = dout("kp", [SP, 1024])
    O["vp"] = dout("vp", [SP, 1024])
    O["wkvs"] = dout("wkvs", [64, 64, 64])
    O["shifts"] = dout("shifts", [D])
    O["ks"] = dout("ks", [8, 1024])
    O["vs"] = dout("vs", [8, 1024])

    WSC.clear()
    WDONE.clear()
    for wn in ("a_w_in", "a_w_out", "w_kv", "b_w_in", "b_w_out"):
        WSC[wn] = dscr(wn + "_bf16", [int(A[wn].shape[0]) * int(A[wn].shape[1]) // (128 * 4096), 128, 4096], BF16)
    with ExitStack() as es:
        cx = Ctx(nc, es)
        vecs = cx.sb(es, [128, NV, NKC], F32, "vecs")
        identf = cx.sb(es, [128, 128], F32, "identf")
        identb = cx.sb(es, [128, 128], BF16, "identb")
        bones = cx.sb(es, [128, 128], BF16, "bones")
        onesb = cx.sb(es, [128, 128], BF16, "onesb")
        onesf = cx.sb(es, [128, 128], F32, "onesf")
        rmask = cx.sb(es, [128, 3, 4, 128], BF16, "rmask")
        amask = cx.sb(es, [128, 24, 128], BF16, "amask")
        gqs = cx.sb(es, [128, 3], F32, "gqs")
        gkb = cx.sb(es, [128, 128], F32, "gkb")
        ident2 = cx.sb(es, [128, 4, 128], BF16, "ident2")
        cx.dma("sp", vecs[:], A["vecs"], vecs, writes=[vecs])
        cx.dma("sp", identf[:], A["identf"], identf, writes=[identf])
        cx.dma("pool", bones[:], A["bones"], bones, writes=[bones])
        cx.dma("pool", rmask[:], A["rmask"], rmask, writes=[rmask])
        cx.dma("pool", amask[:], A["amask"], amask, writes=[amask])
        cx.dma("sp", gqs[:], A["gq"], gqs, writes=[gqs])
        cx.dma("sp", gkb[:], A["gkb"], gkb, writes=[gkb])
        cx.op("dve", lambda h: h.tensor_copy(out=identb[:], in_=identf[:]), [identf], [identb])
        cx.op("dve", lambda h: h.memset(onesb[:], 1.0), [], [onesb])
        cx.op("dve", lambda h: h.memset(onesf[:], 1.0), [], [onesf])
        cx.op("dve", lambda h: h.tensor_scalar(out=gqs[:], in0=gqs[:], scalar1=float(128 ** -0.5), scalar2=None, op0=ALU.mult), [gqs], [gqs])
        for i in range(4):
            cx.op("dve", lambda h, i=i: h.tensor_copy(out=ident2[:, i, :], in_=identf[:]), [identf], [ident2])

        def vec(idx, c):
            return vecs[:, idx, c:c + 1]

        scr_h_p = cx.view(dscr("scr_h_p", [SP, D]), "scr_h_p")
        scr_h_s = cx.view(dscr("scr_h_s", [8, D]), "scr_h_s")
        scr_p = cx.view(dscr("scr_p", [4, D, 512]), "scr_p")
        ktscr = cx.view(dscr("ktscr", [8, 128, max(SP, SC + 128)], BF16), "ktscr")
        yp_t = cx.view(O["yp"], "yp")
        ys_t = cx.view(O["ys"], "ys")
        kp_t = cx.view(O["kp"], "kp")
        vp_t = cx.view(O["vp"], "vp")
        ks_t = cx.view(O["ks"], "ks")
        vs_t = cx.view(O["vs"], "vs")
        outs_misc = cx.view(O["wkvp"], "misc")

        jobs = []
        jp = Job()
        jp.name, jp.S, jp.TT, jp.C = "p", SP, 512, 128
        jp.x, jp.h_scr, jp.y = A["xp"], scr_h_p, yp_t
        jp.k_out, jp.v_out = kp_t, vp_t
        jp.wkv_out, jp.shift_out = O["wkvp"], O["shiftp"]
        jp.has_state = False
        jp.ncache = 0
        if do_prompt:
            jobs.append(jp)
        if do_sample:
            js = Job()
            js.name, js.S, js.TT, js.C = "s", 8, 8, 8
            js.x, js.h_scr, js.y = A["xs"], scr_h_s, ys_t
            js.k_out, js.v_out = ks_t, vs_t
            js.wkv_out, js.shift_out = O["wkvs"], O["shifts"]
            js.has_state = True
            js.ncache = SC // 128
            jobs.append(js)

        wq = [0]

        for job in jobs:
            run_job(cx, nc, es, job, A, O, vec, dict(
                vecs=vecs, identf=identf, identb=identb, bones=bones, onesb=onesb, onesf=onesf,
                rmask=rmask, amask=amask, gqs=gqs, gkb=gkb, ident2=ident2, scr_p=scr_p, ktscr=ktscr,
                outs_misc=outs_misc), phases)

        cx.barrier()
        cx.emit()
    return nc


def run_job(cx, nc, es_glob, job, A, O, vec, K, phases):
    S, TT, C = job.S, job.TT, job.C
    ntile = S // TT
    nsub = (TT + 127) // 128
    P = min(TT, 128)
    nch = TT // C
    nst = {128: 6, 8: 2}[C]
    identb, identf, bones, onesb, onesf = K["identb"], K["identf"], K["bones"], K["onesb"], K["onesf"]
    rmask, amask, gqs, gkb, ident2, scr_p, vecs = K["rmask"], K["amask"], K["gqs"], K["gkb"], K["ident2"], K["scr_p"], K["vecs"]

    with Scope() as ej:
        Sst = cx.sb(ej, [128, NKC, 64], F32, "Sst")
        hrstd = cx.sb(ej, [128, 16], F32, "hrstd")
        prevcol = cx.sb(ej, [128, NKC, 1], BF16, "prevcol")
        if job.has_state:
            with Scope() as e0:
                zp = cx.sb(e0, [128, NKC, 128], F32, "zp")
                pst = cx.ps(e0, [128, 4, 128], F32, "pst")
                sh = cx.sb(e0, [128, NKC], F32, "sh")
                cx.op("dve", lambda h: h.memset(zp[:], 0.0), [], [zp])
                src = A["swkv"].rearrange("(hp e v) k -> e v hp k", e=2, v=64)
                cx.dma("sp", zp[0:64, :, 0:64], src[0], zp, writes=[zp])
                cx.dma("sp", zp[64:128, :, 64:128], src[1], zp, writes=[zp])
                for hp in range(NKC):
                    j = hp % 4
                    cx.op("pe", lambda h, hp=hp, j=j: h.transpose(out=pst[:, j, :], in_=zp[:, hp, :], identity=identf[:]), [zp, identf], [pst])
                    if j == 3:
                        h0 = hp - 3
                        cx.op("dve", lambda h, h0=h0: h.tensor_copy(out=Sst[0:64, h0:h0 + 4, :], in_=pst[0:64, :, 0:64]), [pst], [Sst])
                        cx.op("act", lambda h, h0=h0: h.activation(out=Sst[64:128, h0:h0 + 4, :], in_=pst[64:128, :, 64:128], func=AF.Copy), [pst], [Sst])
                cx.dma("sp", sh[:], A["sshift"], sh, writes=[sh])
                cx.op("dve", lambda h: h.tensor_copy(out=prevcol[:, :, 0], in_=sh[:]), [sh], [prevcol])
                cx.end_scope(e0)
        else:
            cx.op("dve", lambda h: h.memset(Sst[:], 0.0), [], [Sst])
            cx.op("dve", lambda h: h.memset(prevcol[:], 0.0), [], [prevcol])

        if "A" in phases:
            for t in range(ntile):
                layer_a_tile(cx, nc, job, t, A, O, vec, K, Sst, hrstd, prevcol, phases)
            with Scope() as e0:
                pst = cx.ps(e0, [64, 4, 128], F32, "pso")
                so = cx.sb(e0, [64, NKC, 128], F32, "so")
                for hp in range(NKC):
                    j = hp % 4
                    cx.op("pe", lambda h, hp=hp, j=j: h.transpose(out=pst[:, j, :], in_=Sst[:, hp, :], identity=identf[:]), [Sst, identf], [pst])
                    if j == 3:
                        h0 = hp - 3
                        cx.op("dve", lambda h, h0=h0: h.tensor_copy(out=so[:, h0:h0 + 4, :], in_=pst[:]), [pst], [so])
                dst = job.wkv_out.rearrange("(hp e) v k -> v hp e k", e=2)
                cx.dma("sp", dst, so[:].rearrange("v hp (e k) -> v hp e k", e=2), so, reads=[so], writes=[K["outs_misc"]])
                cx.end_scope(e0)

        if "B" in phases:
            layer_b(cx, nc, job, A, O, vec, K, hrstd, phases)
        cx.end_scope(ej)


def rms_rstd(cx, ss_in, out, P, scale, eps, reads_extra=()):
    st, sap = ss_in
    ot, oap = out
    cx.op("dve", lambda h: h.tensor_scalar(out=oap, in0=sap, scalar1=scale, scalar2=eps, op0=ALU.mult, op1=ALU.add), [st], [ot])
    cx.op("act", lambda h: h.activation(out=oap, in_=oap, func=AF.Sqrt), [ot], [ot])
    cx.op("dve", lambda h: h.reciprocal(out=oap, in_=oap), [ot], [ot])


def norm_transpose(cx, es, job, src_rows, rstd_ap_fn, gidx, xnT, col0, vecs, identb, last_out=None, grow=None, compute_rstd=True, hrstd=None, t=0):
    TT = job.TT
    nsub = (TT + 127) // 128
    P = min(TT, 128)
    nb = min(nsub, 4)
    xt2 = [cx.sb(es, [128, D], F32, "xt") for _ in range(nb)]
    xs2 = [cx.sb(es, [128, D], BF16, "xs") for _ in range(2)]
    junk = cx.sb(es, [128, D], BF16, "junk")
    ss = cx.sb(es, [128, 4], F32, "ss")
    rs = cx.sb(es, [128, 4], F32, "rs")
    pt2 = [cx.ps(es, [128, 4, 128], BF16, "pt") for _ in range(2)]
    for s in range(nsub):
        xt, xs = xt2[s % nb], xs2[s % 2]
        cx.dma("sp", xt[:P, :], src_rows(s, P), xt, writes=[xt])
        if compute_rstd:
            cx.op("act", lambda h, xt=xt, s=s: h.activation(out=junk[:P, :], in_=xt[:P, :], func=AF.Square, accum_out=ss[:P, s:s + 1]), [xt], [junk, ss])
            rms_rstd(cx, (ss, ss[:P, s:s + 1]), (rs, rs[:P, s:s + 1]), P, 1.0 / D, 1e-6)
            rap, rt = rs[:P, s:s + 1], rs
        else:
            rap, rt = hrstd[:P, t * 4 + s:t * 4 + s + 1], hrstd
        cx.op("act", lambda h, xt=xt, xs=xs, rap=rap: h.activation(out=xs[:P, :], in_=xt[:P, :], func=AF.Identity, scale=rap), [xt, rt], [xs])
        if last_out is not None and s == nsub - 1:
            xnf = cx.sb(es, [128, D], F32, "xnf")
            gbc = cx.sb(es, [128, D], F32, "gbc")
            cx.dma("sp", gbc[:P, :], grow.partition_broadcast(P), gbc, writes=[gbc])
            cx.op("act", lambda h, xt=xt, rap=rap: h.activation(out=xnf[:P, :], in_=xt[:P, :], func=AF.Identity, scale=rap), [xt, rt], [xnf])
            cx.op("dve", lambda h: h.tensor_tensor(out=xnf[:P, :], in0=xnf[:P, :], in1=gbc[:P, :], op=ALU.mult), [xnf, gbc], [xnf])
            cx.dma("sp", last_out.rearrange("(o d) -> o d", o=1), xnf[P - 1:P, :], xnf, reads=[xnf])
        for c4 in range(8):
            p = pt2[c4 % 2]
            for j in range(4):
                c = c4 * 4 + j
                cx.op("pe", lambda h, c=c, j=j, p=p, xs=xs: h.transpose(out=p[:, j, :P], in_=xs[:P, c * 128:(c + 1) * 128], identity=identb[:P, :P]), [xs, identb], [p])
            cx.op("dve", lambda h, c4=c4, p=p, s=s: h.tensor_tensor(
                out=xnT[:, c4 * 4:c4 * 4 + 4, col0 + s * 128:col0 + s * 128 + P], in0=p[:, :, :P],
                in1=vecs[:, gidx, c4 * 4:c4 * 4 + 4].unsqueeze(2).to_broadcast([128, 4, P]), op=ALU.mult), [p, vecs], [xnT])


WSC = {}
WDONE = {}


def stream_weights(cx, es, W, row0, nrows, col0, ncols, ring, ctr):
    slot = ring[ctr[0] % len(ring)]
    ctr[0] += 1
    nk = nrows // 128
    name = W.tensor.name
    key = (name, row0, nrows, col0, ncols)
    w16 = WSC[name]
    assert nk * ncols == 4096 and tuple(slot.h.shape) == (128, nk, ncols)
    sflat = slot.h[:].rearrange("p a b -> p (a b)")
    if key not in WDONE:
        gid = sum(1 for k in WDONE if k[0] == name)
        WDONE[key] = gid
        cx.dma("pool", slot[:, :nk, :ncols], W[row0:row0 + nrows, col0:col0 + ncols].rearrange("(kc p) n -> p kc n", p=128), slot, writes=[slot])
        cx.dma("act", w16[gid], sflat, slot, reads=[slot])
    else:
        cx.dma("pool", sflat, w16[WDONE[key]], slot, writes=[slot])
    return slot


def layer_a_tile(cx, nc, job, t, A, O, vec, K, Sst, hrstd, prevcol, phases):
    S, TT, C = job.S, job.TT, job.C
    nsub = (TT + 127) // 128
    P = min(TT, 128)
    nch = TT // C
    nst = {128: 6, 8: 2}[C]
    tok0 = t * TT
    ntile = S // TT
    identb, identf, bones, onesb, onesf = K["identb"], K["identf"], K["bones"], K["onesb"], K["onesf"]
    rmask, ident2, scr_p, vecs = K["rmask"], K["ident2"], K["scr_p"], K["vecs"]
    last = (t == ntile - 1)

    with Scope() as et:
      tanhT = cx.sb(et, [128, TT], BF16, "tanhT")
      ahT = cx.sb(et, [128, TT], BF16, "ahT")
      with Scope() as ex:
        xnT = cx.sb(ex, [128, NKC, TT + 1], BF16, "xnT")
        with Scope() as e0:
          if "0" in phases:
            cx.op("dve", lambda h: h.tensor_copy(out=xnT[:, :, 0:1], in_=prevcol[:]), [prevcol], [xnT])
            norm_transpose(cx, e0, job, lambda s, P: job.x[tok0 + s * 128:tok0 + s * 128 + P, :], None, V_G, xnT, 1,
                           vecs, identb, last_out=(job.shift_out if last else None), grow=A["grow"])
            cx.op("dve", lambda h: h.tensor_copy(out=prevcol[:], in_=xnT[:, :, TT:TT + 1]), [xnT], [prevcol])
          cx.end_scope(e0)
        with Scope() as e1:
            mixb = [cx.sb(e1, [128, NKC, TT], BF16, "mix") for _ in range(2)]
            tmpd = [cx.sb(e1, [128, TT], F32, "tmpd") for _ in range(3)]
            pp = [cx.ps(e1, [128, 2, 512], F32, "pp") for _ in range(3)]
            pl = cx.ps(e1, [128, 512], F32, "pl")
            ctr = [0]
            ncg = 0
            e1a = Scope()
            e1a.__enter__()
            w1sb = cx.sb(e1a, [128, NKC, 128], BF16, "w1sb")
            a1sb = cx.sb(e1a, [128, NKC, 128], BF16, "a1sb")
            cx.dma("pool", w1sb[:], A["a_w1"].rearrange("(kc p) n -> p kc n", p=128), w1sb, writes=[w1sb])
            cx.dma("pool", a1sb[:], A["a_a1"].rearrange("(kc p) n -> p kc n", p=128), a1sb, writes=[a1sb])
            ring = stg = None
            for j in ((4, 5, 0, 1, 2, 3) if "1" in phases else ()):
                mix = mixb[j % 2]
                if j == 0:
                    cx.end_scope(e1a)
                    e1a.__exit__(None, None, None)
                    ring = [cx.sb(e1, [128, 16, 256], BF16, "wr") for _ in range(4)]
                    stg = [cx.sb(e1, [128, 2, TT], F32, "stg") for _ in range(2)]
                for c in range(NKC):
                    td = tmpd[c % 3]
                    cx.op("pool", lambda h, c=c, td=td: h.tensor_tensor(out=td[:, :TT], in0=xnT[:, c, 0:TT], in1=xnT[:, c, 1:TT + 1], op=ALU.subtract), [xnT], [td])
                    cx.op("dve", lambda h, c=c, td=td, mix=mix, j=j: h.scalar_tensor_tensor(
                        out=mix[:, c, :], in0=td[:, :TT], scalar=vecs[:, V_MU + j, c:c + 1], in1=xnT[:, c, 1:TT + 1],
                        op0=ALU.mult, op1=ALU.add), [td, xnT, vecs], [mix])
                if j >= 4:
                    wsb = w1sb if j == 4 else a1sb
                    for kc in range(NKC):
                        cx.op("pe", lambda h, kc=kc, wsb=wsb, mix=mix: h.matmul(pl[:, :TT], lhsT=wsb[:, kc, :], rhs=mix[:, kc, :], start=(kc == 0), stop=(kc == NKC - 1)), [wsb, mix], [pl])
                    if j == 4:
                        cx.op("act", lambda h: h.activation(out=tanhT[:, :], in_=pl[:, :TT], func=AF.Tanh), [pl], [tanhT])
                    else:
                        cx.op("act", lambda h: h.activation(out=ahT[:, :], in_=pl[:, :TT], func=AF.Copy), [pl], [ahT])
                    continue
                for cg in range(16):
                    ppt = pp[ncg % 3]
                    sg = stg[ncg % 2]
                    ncg += 1
                    for k2 in range(2):
                        slot = stream_weights(cx, e1, A["a_w_in"], k2 * 2048, 2048, j * D + cg * 256, 256, ring, ctr)
                        for oc in range(2):
                            for kc in range(16):
                                kk = k2 * 16 + kc
                                cx.op("pe", lambda h, ppt=ppt, slot=slot, oc=oc, kc=kc, kk=kk, mix=mix: h.matmul(
                                    ppt[:, oc, :TT], lhsT=slot[:, kc, oc * 128:(oc + 1) * 128], rhs=mix[:, kk, :],
                                    start=(kk == 0), stop=(kk == NKC - 1)), [slot, mix], [ppt])
                    if j == 3:
                        cx.op("act", lambda h, ppt=ppt, sg=sg: h.activation(out=sg[:, :, :], in_=ppt[:, :, :TT], func=AF.Silu), [ppt], [sg])
                    elif cg % 2 == 0:
                        cx.op("act", lambda h, ppt=ppt, sg=sg: h.activation(out=sg[:, :, :], in_=ppt[:, :, :TT], func=AF.Copy), [ppt], [sg])
                    else:
                        cx.op("dve", lambda h, ppt=ppt, sg=sg: h.tensor_copy(out=sg[:, :, :], in_=ppt[:, :, :TT]), [ppt], [sg])
                    cx.dma("sp", scr_p[j, cg * 256:(cg + 1) * 256, 0:TT].rearrange("(o p) t -> p o t", p=128), sg[:, :, :], sg, reads=[sg], writes=[scr_p])
            if ring is None:
                cx.end_scope(e1a)
                e1a.__exit__(None, None, None)
            cx.end_scope(e1)
        cx.end_scope(ex)
      with Scope() as ey:
        oT = cx.sb(ey, [128, NKC, TT], BF16, "oT")
        with Scope() as e2:
            if "2" in phases:
                phase_a2(cx, e2, job, A, K, Sst, tanhT, ahT, oT)
            cx.end_scope(e2)
        with Scope() as e3:
            ring = [cx.sb(e3, [128, 8, 512], BF16, "wr3") for _ in range(4)]
            xres = [cx.sb(e3, [128, 512], F32, "xres") for _ in range(4)]
            hsb = [cx.sb(e3, [128, 512], F32, "hsb") for _ in range(4)]
            junk = cx.sb(e3, [128, 512], BF16, "junk3")
            ssq = cx.sb(e3, [128, 4, 8], F32, "ssq")
            sst = cx.sb(e3, [128, 4], F32, "sst")
            pp = [cx.ps(e3, [128, 4, 512], F32, "pp3") for _ in range(2)]
            ctr = [0]
            n = 0
            for cg in (range(8) if "3" in phases else ()):
                ppt = pp[cg % 2]
                for k4 in range(4):
                    slot = stream_weights(cx, e3, A["a_w_out"], k4 * 1024, 1024, cg * 512, 512, ring, ctr)
                    for s in range(nsub):
                        for kc in range(8):
                            kk = k4 * 8 + kc
                            cx.op("pe", lambda h, ppt=ppt, slot=slot, s=s, kc=kc, kk=kk: h.matmul(
                                ppt[:P, s, :], lhsT=oT[:, kk, s * 128:s * 128 + P], rhs=slot[:, kc, :],
                                start=(kk == 0), stop=(kk == NKC - 1)), [slot, oT], [ppt])
                for s in range(nsub):
                    xr, hb = xres[n % 4], hsb[n % 4]
                    n += 1
                    cx.dma("sp", xr[:P, :], job.x[tok0 + s * 128:tok0 + s * 128 + P, cg * 512:(cg + 1) * 512], xr, writes=[xr])
                    cx.op("dve", lambda h, ppt=ppt, s=s, xr=xr, hb=hb: h.tensor_tensor(out=hb[:P, :], in0=ppt[:P, s, :], in1=xr[:P, :], op=ALU.add), [ppt, xr], [hb])
                    cx.op("act", lambda h, hb=hb, s=s, cg=cg: h.activation(out=junk[:P, :], in_=hb[:P, :], func=AF.Square, accum_out=ssq[:P, s, cg:cg + 1]), [hb], [junk, ssq])
                    cx.dma("sp", job.h_scr[tok0 + s * 128:tok0 + s * 128 + P, cg * 512:(cg + 1) * 512], hb[:P, :], hb, reads=[hb], writes=[job.h_scr])
            cx.op("dve", lambda h: h.tensor_reduce(out=sst[:P, :nsub], in_=ssq[:P, :nsub, :], axis=AX.X, op=ALU.add), [ssq], [sst])
            rms_rstd(cx, (sst, sst[:P, :nsub]), (hrstd, hrstd[:P, t * 4:t * 4 + nsub]), P, 1.0 / D, 1e-6)
            cx.end_scope(e3)
        cx.end_scope(ey)
      with Scope() as ez:
        xnT = cx.sb(ez, [128, NKC, TT + 1], BF16, "hnTa")
        with Scope() as e4:
          if "4" in phases:
            norm_transpose(cx, e4, job, lambda s, P: job.h_scr[tok0 + s * 128:tok0 + s * 128 + P, :], None, V_KVG, xnT, 0,
                           vecs, identb, compute_rstd=False, hrstd=hrstd, t=t)
          cx.end_scope(e4)
        with Scope() as e5:
            ring = [cx.sb(e5, [128, 8, 512], BF16, "wr5") for _ in range(4)]
            pp = [cx.ps(e5, [128, 4, 512], F32, "pp5") for _ in range(2)]
            ksb = [cx.sb(e5, [128, 4, 128], F32, "ksb") for _ in range(3)]
            ksq = cx.sb(e5, [128, 4, 128], F32, "ksq")
            kss = [cx.sb(e5, [128, 4], F32, "kss") for _ in range(2)]
            gkb = K["gkb"]
            ctr = [0]
            n = 0
            for cg in (range(4) if "5" in phases else ()):
                ppt = pp[cg % 2]
                for k4 in range(4):
                    slot = stream_weights(cx, e5, A["w_kv"], k4 * 1024, 1024, cg * 512, 512, ring, ctr)
                    for s in range(nsub):
                        for kc in range(8):
                            kk = k4 * 8 + kc
                            cx.op("pe", lambda h, ppt=ppt, slot=slot, s=s, kc=kc, kk=kk: h.matmul(
                                ppt[:P, s, :], lhsT=xnT[:, kk, s * 128:s * 128 + P], rhs=slot[:, kc, :],
                                start=(kk == 0), stop=(kk == NKC - 1)), [slot, xnT], [ppt])
                for s in range(nsub):
                    kb = ksb[n % 3]
                    ks_ = kss[n % 2]
                    n += 1
                    rows = slice(tok0 + s * 128, tok0 + s * 128 + P)
                    if cg < 2:
                        cx.op("act", lambda h, ppt=ppt, s=s, kb=kb: h.activation(out=kb[:P].rearrange("p a b -> p (a b)"), in_=ppt[:P, s, :], func=AF.Copy), [ppt], [kb])
                        cx.op("dve", lambda h, kb=kb: h.tensor_tensor(out=ksq[:P], in0=kb[:P], in1=kb[:P], op=ALU.mult), [kb], [ksq])
                        cx.op("dve", lambda h, ks_=ks_: h.tensor_reduce(out=ks_[:P, :], in_=ksq[:P], axis=AX.X, op=ALU.add), [ksq], [ks_])
                        rms_rstd(cx, (ks_, ks_[:P, :]), (ks_, ks_[:P, :]), P, 1.0 / 128, 1e-6)
                        cx.op("dve", lambda h, kb=kb, ks_=ks_: h.tensor_tensor(out=kb[:P], in0=kb[:P], in1=ks_[:P, :].unsqueeze(2).to_broadcast([P, 4, 128]), op=ALU.mult), [kb, ks_], [kb])
                        cx.op("dve", lambda h, kb=kb: h.tensor_tensor(out=kb[:P], in0=kb[:P], in1=gkb[:P, :].unsqueeze(1).to_broadcast([P, 4, 128]), op=ALU.mult), [kb, gkb], [kb])
                        cx.dma("sp", job.k_out[rows, cg * 512:(cg + 1) * 512], kb[:P].rearrange("p a b -> p (a b)"), kb, reads=[kb], writes=[job.k_out])
                    else:
                        cx.op("act", lambda h, ppt=ppt, s=s, kb=kb: h.activation(out=kb[:P].rearrange("p a b -> p (a b)"), in_=ppt[:P, s, :], func=AF.Copy), [ppt], [kb])
                        cx.dma("sp", job.v_out[rows, (cg - 2) * 512:(cg - 1) * 512], kb[:P].rearrange("p a b -> p (a b)"), kb, reads=[kb], writes=[job.v_out])
            cx.end_scope(e5)
        cx.end_scope(ez)
      cx.end_scope(et)


def _rr(*gens):
    gens = list(gens)
    while gens:
        for g in list(gens):
            try:
                next(g)
            except StopIteration:
                gens.remove(g)


def phase_a2(cx, es, job, A, K, Sst, tanhT, ahT, oT):
    TT, C = job.TT, job.C
    nch = TT // C
    nst = {128: 6, 8: 2}[C]
    identb, identf, bones = K["identb"], K["identf"], K["bones"]
    rmask, ident2, scr_p, vecs = K["rmask"], K["ident2"], K["scr_p"], K["vecs"]
    w2sb = cx.sb(es, [128, D], BF16, "w2sb")
    a2sb = cx.sb(es, [128, D], BF16, "a2sb")
    cx.dma("pool", w2sb[:], A["a_w2"], w2sb, writes=[w2sb])
    cx.dma("pool", a2sb[:], A["a_a2"], a2sb, writes=[a2sb])
    rin = [cx.sb(es, [128, 4, TT], F32, "rin") for _ in range(2)]

    def f32t(name):
        return cx.sb(es, [128, TT], F32, name)

    def bft(name):
        return cx.sb(es, [128, TT], BF16, name)
    lwp, al, clp, egi, egm, kkr, rn, kk, t1, kp, tmp, mean, m2, var, yc = [f32t(n) for n in (
        "lwp", "al", "clp", "egi", "egm", "kkr", "rn", "kk", "t1", "kp", "tmp", "mean", "m2", "var", "yc")]
    sq, rkb, ysq, ysbb = [bft(n) for n in ("sq", "rkb", "ysq", "ysbb")]
    eg2 = [f32t("eg") for _ in range(2)]
    bon2 = [f32t("bon") for _ in range(2)]
    ysb2 = [f32t("ysb") for _ in range(2)]
    bt2 = [bft("bt") for _ in range(2)]
    kt2 = [bft("kt") for _ in range(2)]
    vbf2 = [bft("vbf") for _ in range(2)]
    at22 = [[bft("at0"), bft("at1")] for _ in range(2)]
    rt22 = [[bft("rt0"), bft("rt1")] for _ in range(2)]
    for par_ in range(2):
        for z in at22[par_] + rt22[par_]:
            cx.op("dve", lambda h, z=z: h.memset(z[:], 0.0), [], [z])
    onesT = f32t("onesT")
    cx.op("dve", lambda h: h.memset(onesT[:], 1.0), [], [onesT])
    tokm = [cx.sb(es, [128, 3, 128], BF16, "tokm") for _ in range(2)]
    UA = [cx.sb(es, [128, 4, 128], BF16, "UA") for _ in range(2)]
    UB = [cx.sb(es, [128, 4, 128], BF16, "UB") for _ in range(2)]
    NTs = [cx.sb(es, [128, 2, 128], BF16, "NTs") for _ in range(2)]
    PPs = [[cx.sb(es, [128, 4, 128], BF16, "PP") for _ in range(3)] for _ in range(2)]
    RRs = [[cx.sb(es, [128, 4, 128], BF16, "RR") for _ in range(3)] for _ in range(2)]
    Sbf = cx.sb(es, [128, 64], BF16, "Sbf")
    Zb = cx.sb(es, [128, 2, 64], BF16, "Zb")
    Ub = cx.sb(es, [128, 2, 64], BF16, "Ub")
    tmpS = cx.sb(es, [128, 64], F32, "tmpS")
    pA = cx.ps(es, [128, 4, 128], F32, "pA")
    pPPs = [cx.ps(es, [128, 4, 128], F32, "pPP") for _ in range(2)]
    pRRs = [cx.ps(es, [128, 4, 128], F32, "pRR") for _ in range(2)]
    pX0, pX1 = pPPs[1], pRRs[1]
    pXv = [pX0.h[:].rearrange("p a b -> p (a b)"), pX1.h[:].rearrange("p a b -> p (a b)")]
    pm1 = cx.ps(es, [128, 512], F32, "pm1")
    pC = pZ = pU = pm1
    pm2 = cx.ps(es, [128, 512], F32, "pm2")
    pY = pS = pm2
    ptr = cx.ps(es, [128, 3, 128], BF16, "ptr")

    def V(idx, hp):
        return vecs[:, idx, hp:hp + 1]

    def make(hp):
        par = hp % 2
        r_in = rin[par]
        eg, bon, ysb, bt, kt, vbf, at2, rt2 = eg2[par], bon2[par], ysb2[par], bt2[par], kt2[par], vbf2[par], at22[par], rt22[par]

        def S1():
            if True:
                pass
                cx.dma("sp", r_in[:, :, :], scr_p[:, hp * 128:(hp + 1) * 128, 0:TT].rearrange("j p t -> p j t"), r_in, reads=[scr_p], writes=[r_in])
                r_ap, k_ap, v_ap, g_ap = r_in[:, 0, :], r_in[:, 1, :], r_in[:, 2, :], r_in[:, 3, :]
                fs = slice(hp * 128, (hp + 1) * 128)
                cx.op("pe", lambda h, fs=fs: h.matmul(pXv[0][:, :TT], lhsT=w2sb[:, fs], rhs=tanhT[:, :], start=True, stop=True), [w2sb, tanhT], [pX0])
                cx.op("pe", lambda h, fs=fs: h.matmul(pXv[1][:, :TT], lhsT=a2sb[:, fs], rhs=ahT[:, :], start=True, stop=True), [a2sb, ahT], [pX1])
                cx.op("act", lambda h, hp=hp: h.activation(out=lwp[:], in_=pXv[0][:, :TT], func=AF.Sigmoid, bias=V(V_W0, hp)), [pX0, vecs], [lwp])
                yield
                cx.op("act", lambda h, hp=hp: h.activation(out=al[:], in_=pXv[1][:, :TT], func=AF.Sigmoid, bias=V(V_A0, hp)), [pX1, vecs], [al])
                for ch in range(nch):
                    cs = slice(ch * C, (ch + 1) * C)
                    cx.op("dve", lambda h, cs=cs: h.tensor_tensor_scan(out=clp[:, cs], data0=onesT[:, cs], data1=lwp[:, cs], initial=0.0, op0=ALU.mult, op1=ALU.add), [onesT, lwp], [clp])
                cx.op("act", lambda h: h.activation(out=eg[:], in_=clp[:], func=AF.Exp, scale=-C0), [clp], [eg])
                yield
                cx.op("act", lambda h: h.activation(out=egi[:], in_=clp[:], func=AF.Exp, scale=C0), [clp], [egi])
                cx.op("dve", lambda h: h.tensor_tensor(out=tmp[:], in0=clp[:], in1=lwp[:], op=ALU.subtract), [clp, lwp], [tmp])
                cx.op("act", lambda h: h.activation(out=egm[:], in_=tmp[:], func=AF.Exp, scale=-C0), [tmp], [egm])
                yield
                cx.op("dve", lambda h, hp=hp, k_ap=k_ap: h.tensor_scalar(out=kkr[:], in0=k_ap, scalar1=V(V_KK, hp), scalar2=None, op0=ALU.mult), [r_in, vecs], [kkr])
                cx.op("act", lambda h: h.activation(out=sq[:], in_=kkr[:], func=AF.Square), [kkr], [sq])
                cx.op("pe", lambda h: h.matmul(pXv[0][:, :TT], lhsT=bones[:], rhs=sq[:], start=True, stop=True), [bones, sq], [pX0])
                yield
                cx.op("dve", lambda h: h.tensor_scalar(out=rn[:], in0=pXv[0][:, :TT], scalar1=1e-24, scalar2=None, op0=ALU.max), [pX0], [rn])
                cx.op("act", lambda h: h.activation(out=rn[:], in_=rn[:], func=AF.Ln), [rn], [rn])
                cx.op("act", lambda h: h.activation(out=rn[:], in_=rn[:], func=AF.Exp, scale=-0.5), [rn], [rn])
                yield
                cx.op("dve", lambda h: h.tensor_tensor(out=kk[:], in0=kkr[:], in1=rn[:], op=ALU.mult), [kkr, rn], [kk])
                cx.op("dve", lambda h, hp=hp: h.tensor_scalar(out=t1[:], in0=al[:], scalar1=-1.0, scalar2=V(V_KA, hp), op0=ALU.add, op1=ALU.mult), [al, vecs], [t1])
                cx.op("dve", lambda h, k_ap=k_ap: h.scalar_tensor_tensor(out=kp[:], in0=t1[:], scalar=1.0, in1=k_ap, op0=ALU.add, op1=ALU.mult), [t1, r_in], [kp])
                yield
                for e in range(2):
                    po = slice(64 * e, 64 * e + 64)
                    cx.op("dve", lambda h, e=e, po=po: h.scalar_tensor_tensor(out=at2[e][po, :], in0=kk[po, :], scalar=-1.0, in1=egm[po, :], op0=ALU.mult, op1=ALU.mult), [kk, egm], [at2[e]])
                cx.op("pool", lambda h: h.tensor_tensor(out=tmp[:], in0=kk[:], in1=al[:], op=ALU.mult), [kk, al], [tmp])
                cx.op("pool", lambda h: h.tensor_tensor(out=bt[:], in0=tmp[:], in1=egi[:], op=ALU.mult), [tmp, egi], [bt])
                yield
                cx.op("pool", lambda h: h.tensor_tensor(out=kt[:], in0=kp[:], in1=egi[:], op=ALU.mult), [kp, egi], [kt])
                for e in range(2):
                    po = slice(64 * e, 64 * e + 64)
                    cx.op("pool", lambda h, e=e, po=po, r_in=r_in: h.tensor_tensor(out=rt2[e][po, :], in0=r_in[po, 0, :], in1=eg[po, :], op=ALU.mult), [r_in, eg], [rt2[e]])
                cx.op("dve", lambda h, hp=hp, r_ap=r_ap: h.scalar_tensor_tensor(out=rkb[:], in0=r_ap, scalar=V(V_RK, hp), in1=kp[:], op0=ALU.mult, op1=ALU.mult), [r_in, vecs, kp], [rkb])
                yield
                cx.op("pe", lambda h: h.matmul(pXv[1][:, :TT], lhsT=bones[:], rhs=rkb[:], start=True, stop=True), [bones, rkb], [pX1])
                cx.op("dve", lambda h, v_ap=v_ap: h.tensor_tensor(out=bon[:], in0=pXv[1][:, :TT], in1=v_ap, op=ALU.mult), [pX1, r_in], [bon])
                cx.op("act", lambda h, v_ap=v_ap: h.activation(out=vbf[:], in_=v_ap, func=AF.Copy), [r_in], [vbf])
                yield
            yield

        if True:
            def s2(ch, idx):
                cs = slice(ch * C, (ch + 1) * C)
                tk = tokm[idx]
                ua, ub, nts = UA[idx], UB[idx], NTs[idx]
                for i, srcT in enumerate((vbf, bt, kt)):
                    cx.op("pe", lambda h, i=i, srcT=srcT, cs=cs: h.transpose(out=ptr[:C, i, :], in_=srcT[:, cs], identity=identb[:]), [srcT, identb], [ptr])
                cx.op("act", lambda h, tk=tk: h.activation(out=tk[:C], in_=ptr[:C], func=AF.Copy), [ptr], [tk])
                for e in range(2):
                    cx.op("pe", lambda h, e=e, cs=cs: h.matmul(pA[:C, e, :C], lhsT=bt[:, cs], rhs=at2[e][:, cs], start=True, stop=True), [bt, at2[e]], [pA])
                    cx.op("pe", lambda h, e=e, cs=cs: h.matmul(pA[:C, 2 + e, :C], lhsT=kt[:, cs], rhs=at2[e][:, cs], start=True, stop=True), [kt, at2[e]], [pA])
                cx.op("dve", lambda h, ua=ua: h.tensor_tensor(out=ua[:C, :, :C], in0=pA[:C, :, :C], in1=rmask[:C, 0, :, :C], op=ALU.mult), [pA, rmask], [ua])
                for e in range(2):
                    cx.op("pe", lambda h, e=e, cs=cs: h.matmul(pA[:C, e, :C], lhsT=bt[:, cs], rhs=rt2[e][:, cs], start=True, stop=True), [bt, rt2[e]], [pA])
                    cx.op("pe", lambda h, e=e, cs=cs: h.matmul(pA[:C, 2 + e, :C], lhsT=kt[:, cs], rhs=rt2[e][:, cs], start=True, stop=True), [kt, rt2[e]], [pA])
                cx.op("dve", lambda h, ub=ub: h.tensor_tensor(out=ub[:C, :, :C], in0=pA[:C, :, :C], in1=rmask[:C, 1, :, :C], op=ALU.mult), [pA, rmask], [ub])
                for e in range(2):
                    cx.op("pe", lambda h, e=e, cs=cs: h.matmul(pm1[:C, e * 128:e * 128 + C], lhsT=at2[e][:, cs], rhs=bt[:, cs], start=True, stop=True), [at2[e], bt], [pC])
                cx.op("dve", lambda h, nts=nts: h.tensor_tensor(out=nts[:C, :, :C], in0=pm1[:C, 0:256].rearrange("p (e c) -> p e c", e=2)[:, :, :C], in1=rmask[:C, 2, 0:2, :C], op=ALU.mult), [pC, rmask], [nts])

            def s3(idx, st, res):
                ua, nts = UA[idx], NTs[idx]
                pPP, pRR, PP, RR = pPPs[0], pRRs[0], PPs[st], RRs[st]
                rr = RR[0]
                cx.op("pool", lambda h: h.tensor_tensor(out=rr[:C, 0:2, :C], in0=ua[:C, 0:2, :C], in1=ident2[:C, 0:2, :C], op=ALU.add), [ua, ident2], [rr])
                cx.op("pool", lambda h: h.tensor_tensor(out=rr[:C, 2:4, :C], in0=nts[:C, :, :C], in1=ident2[:C, 0:2, :C], op=ALU.add), [nts, ident2], [rr])
                ppc = PP[0]
                cx.op("act", lambda h: h.activation(out=ppc[:C, 0:2, :C], in_=ua[:C, 0:2, :C], func=AF.Copy), [ua], [ppc])
                cx.op("act", lambda h: h.activation(out=ppc[:C, 2:4, :C], in_=nts[:C, :, :C], func=AF.Copy), [nts], [ppc])
                yield
                for i in range(1, nst + 1):
                    lastst = (i == nst)
                    ppn = PP[i % 3]
                    rrn = RR[i % 3]
                    for e in range(2):
                        cx.op("pe", lambda h, e=e: h.matmul(pPP[:C, e, :C], lhsT=ppc[:C, 2 + e, :C], rhs=ppc[:C, e, :C], start=True, stop=True), [ppc], [pPP])
                        if not lastst:
                            cx.op("pe", lambda h, e=e: h.matmul(pPP[:C, 2 + e, :C], lhsT=ppc[:C, e, :C], rhs=ppc[:C, 2 + e, :C], start=True, stop=True), [ppc], [pPP])
                    nq = 2 if lastst else 4
                    cx.op("act", lambda h: h.activation(out=ppn[:C, 0:nq, :C], in_=pPP[:C, 0:nq, :C], func=AF.Copy), [pPP], [ppn])
                    yield
                    for e in range(2):
                        cx.op("pe", lambda h, e=e: h.matmul(pRR[:C, e, :C], lhsT=rr[:C, 2 + e, :C], rhs=ppn[:C, e, :C], start=True, stop=True), [rr, ppn], [pRR])
                        if not lastst:
                            cx.op("pe", lambda h, e=e: h.matmul(pRR[:C, 2 + e, :C], lhsT=ppn[:C, e, :C], rhs=rr[:C, 2 + e, :C], start=True, stop=True), [rr, ppn], [pRR])
                    cx.op("dve", lambda h: h.tensor_tensor(out=rrn[:C, 0:nq, :C], in0=pRR[:C, 0:nq, :C], in1=rr[:C, 0:nq, :C], op=ALU.add), [pRR, rr], [rrn])
                    ppc, rr = ppn, rrn
                    yield
                res.append(rr)

            def s4(ch, idx, rr):
                cs = slice(ch * C, (ch + 1) * C)
                tk = tokm[idx]
                ua, ub = UA[idx], UB[idx]
                cx.op("act", lambda h, hp=hp: h.activation(out=Sbf[:], in_=Sst[:, hp, :], func=AF.Copy), [Sst], [Sbf])
                for e in range(2):
                    po = slice(64 * e, 64 * e + 64)
                    zc = slice(256 + e * 64, 256 + e * 64 + 64)
                    cx.op("pe", lambda h, e=e, po=po, zc=zc, cs=cs: h.matmul(pm1[:C, zc], lhsT=at2[e][:, cs], rhs=Sbf[:, :], start=True, stop=False), [at2[e], Sbf], [pZ])
                    cx.op("pe", lambda h, e=e, po=po, zc=zc, ua=ua, tk=tk: h.matmul(pm1[:C, zc], lhsT=ua[:C, 2 + e, :C], rhs=tk[:C, 0, po], start=False, stop=True), [ua, tk], [pZ])
                cx.op("act", lambda h: h.activation(out=Zb[:C].rearrange("p e v -> p (e v)"), in_=pm1[:C, 256:384], func=AF.Copy), [pZ], [Zb])
                yield
                for e in range(2):
                    uc = slice(384 + e * 64, 384 + e * 64 + 64)
                    cx.op("pe", lambda h, e=e, uc=uc, rr=rr: h.matmul(pm1[:C, uc], lhsT=rr[:C, e, :C], rhs=Zb[:C, e, :], start=True, stop=True), [rr, Zb], [pU])
                cx.op("dve", lambda h: h.tensor_copy(out=Ub[:C].rearrange("p e v -> p (e v)"), in_=pm1[:C, 384:512]), [pU], [Ub])
                yield
                for e in range(2):
                    po = slice(64 * e, 64 * e + 64)
                    cx.op("pe", lambda h, e=e, po=po, cs=cs: h.matmul(pm2[po, 0:C], lhsT=Sbf[:, :], rhs=rt2[e][:, cs], start=True, stop=False), [Sbf, rt2[e]], [pY])
                    cx.op("pe", lambda h, e=e, po=po, ub=ub: h.matmul(pm2[po, 0:C], lhsT=Ub[:C, e, :], rhs=ub[:C, e, :C], start=False, stop=False), [Ub, ub], [pY])
                    cx.op("pe", lambda h, e=e, po=po, ub=ub, tk=tk: h.matmul(pm2[po, 0:C], lhsT=tk[:C, 0, po], rhs=ub[:C, 2 + e, :C], start=False, stop=True), [tk, ub], [pY])
                cx.op("act", lambda h, cs=cs: h.activation(out=ysb[:, cs], in_=pm2[:, 0:C], func=AF.Copy), [pY], [ysb])
                for e in range(2):
                    po = slice(64 * e, 64 * e + 64)
                    cx.op("pe", lambda h, e=e, po=po, tk=tk: h.matmul(pm2[po, 128:192], lhsT=tk[:C, 1, po], rhs=Ub[:C, e, :], start=True, stop=False), [tk, Ub], [pS])
                    cx.op("pe", lambda h, e=e, po=po, tk=tk: h.matmul(pm2[po, 128:192], lhsT=tk[:C, 2, po], rhs=tk[:C, 0, po], start=False, stop=True), [tk], [pS])
                cx.op("dve", lambda h, hp=hp: h.tensor_tensor(out=tmpS[:], in0=pm2[:, 128:192], in1=Sst[:, hp, :], op=ALU.add), [pS, Sst], [tmpS])
                gcol = (ch + 1) * C - 1
                cx.op("dve", lambda h, hp=hp, gcol=gcol: h.tensor_scalar(out=Sst[:, hp, :], in0=tmpS[:], scalar1=eg[:, gcol:gcol + 1], scalar2=None, op0=ALU.mult), [tmpS, eg], [Sst])


        def chunks():
            for c0 in range(0, nch, 2):
                chs = [c for c in (c0, c0 + 1) if c < nch]
                for idx, ch in enumerate(chs):
                    s2(ch, idx)
                    yield
                res = [[] for _ in chs]
                for idx in range(len(chs)):
                    yield from s3(idx, idx, res[idx])
                for idx, ch in enumerate(chs):
                    yield from s4(ch, idx, res[idx][0])
                    yield

        def S5():
            g_ap = r_in[:, 3, :]
            if True:
                cx.op("act", lambda h: h.activation(out=ysq[:], in_=ysb[:], func=AF.Square), [ysb], [ysq])
                cx.op("act", lambda h: h.activation(out=ysbb[:], in_=ysb[:], func=AF.Copy), [ysb], [ysbb])
                cx.op("pe", lambda h: h.matmul(pXv[0][:, :TT], lhsT=bones[:], rhs=ysbb[:], start=True, stop=True), [bones, ysbb], [pX0])
                yield
                cx.op("pe", lambda h: h.matmul(pXv[1][:, :TT], lhsT=bones[:], rhs=ysq[:], start=True, stop=True), [bones, ysq], [pX1])
                cx.op("dve", lambda h: h.tensor_scalar(out=mean[:], in0=pXv[0][:, :TT], scalar1=1.0 / 64, scalar2=None, op0=ALU.mult), [pX0], [mean])
                cx.op("dve", lambda h: h.tensor_tensor(out=m2[:], in0=mean[:], in1=mean[:], op=ALU.mult), [mean], [m2])
                yield
                cx.op("dve", lambda h: h.scalar_tensor_tensor(out=var[:], in0=pXv[1][:, :TT], scalar=1.0 / 64, in1=m2[:], op0=ALU.mult, op1=ALU.subtract), [pX1, m2], [var])
                cx.op("dve", lambda h: h.tensor_scalar(out=var[:], in0=var[:], scalar1=64e-5, scalar2=None, op0=ALU.add), [var], [var])
                cx.op("act", lambda h: h.activation(out=var[:], in_=var[:], func=AF.Ln), [var], [var])
                yield
                cx.op("act", lambda h: h.activation(out=var[:], in_=var[:], func=AF.Exp, scale=-0.5), [var], [var])
                cx.op("pool", lambda h: h.tensor_tensor(out=yc[:], in0=ysb[:], in1=mean[:], op=ALU.subtract), [ysb, mean], [yc])
                cx.op("dve", lambda h: h.tensor_tensor(out=yc[:], in0=yc[:], in1=var[:], op=ALU.mult), [yc, var], [yc])
                yield
                cx.op("act", lambda h, hp=hp: h.activation(out=yc[:], in_=yc[:], func=AF.Identity, scale=V(V_LG, hp), bias=V(V_LB, hp)), [yc, vecs], [yc])
                cx.op("pool", lambda h: h.tensor_tensor(out=yc[:], in0=yc[:], in1=bon[:], op=ALU.add), [yc, bon], [yc])
                cx.op("dve", lambda h, hp=hp, g_ap=g_ap: h.tensor_tensor(out=oT[:, hp, :], in0=yc[:], in1=g_ap, op=ALU.mult), [yc, r_in], [oT])
                yield
            yield
        return S1, chunks, S5

    gens = [make(hp) for hp in range(NKC)]
    _rr(gens[0][0]())
    for hp in range(NKC):
        def tail(hp=hp):
            if hp >= 1:
                yield from gens[hp - 1][2]()
            if hp + 1 < NKC:
                yield from gens[hp + 1][0]()
        _rr(gens[hp][1](), tail())
    _rr(gens[NKC - 1][2]())


def layer_b(cx, nc, job, A, O, vec, K, hrstd, phases):
    S, TT = job.S, job.TT
    ntile = S // TT
    nsub = (TT + 127) // 128
    P = min(TT, 128)
    identb, onesb, onesf = K["identb"], K["onesb"], K["onesf"]
    amask, gqs, vecs = K["amask"], K["gqs"], K["vecs"]
    nkt_cache = job.ncache
    nkt_own = (S + 127) // 128
    nkt = nkt_cache + nkt_own
    ktscr = K["ktscr"]
    with Scope() as eb:
        with Scope() as e0:
            kst = [cx.sb(e0, [128, 1024], BF16, "kst") for _ in range(2)]
            kts = [cx.sb(e0, [128, 8, 128], BF16, "kts") for _ in range(2)]
            ptk = [cx.ps(e0, [128, 8, 128], BF16, "ptk") for _ in range(2)]
            for kt in (range(nkt) if "6" in phases else ()):
                if kt < nkt_cache:
                    ksrc, rows, pr = A["ck"], slice(kt * 128, kt * 128 + 128), 128
                else:
                    o = kt - nkt_cache
                    pr = min(128, S - o * 128)
                    ksrc, rows = job.k_out.h, slice(o * 128, o * 128 + pr)
                ks_ = kst[kt % 2]
                pk = ptk[kt % 2]
                kb = kts[kt % 2]
                cx.dma("pool", ks_[:pr, :], ksrc[rows, :], ks_, writes=[ks_])
                for hd in range(8):
                    cx.op("pe", lambda h, hd=hd, ks_=ks_, pk=pk, pr=pr: h.transpose(out=pk[:, hd, :pr], in_=ks_[:pr, hd * 128:(hd + 1) * 128], identity=identb[:pr, :pr]), [ks_, identb], [pk])
                if kt % 2 == 0:
                    cx.op("dve", lambda h, pk=pk, kb=kb, pr=pr: h.tensor_copy(out=kb[:, :, :pr], in_=pk[:, :, :pr]), [pk], [kb])
                else:
                    cx.op("act", lambda h, pk=pk, kb=kb, pr=pr: h.activation(out=kb[:, :, :pr], in_=pk[:, :, :pr], func=AF.Copy), [pk], [kb])
                cx.dma("sp", ktscr[:, :, kt * 128:kt * 128 + pr].rearrange("h c t -> c h t"), kb[:, :, :pr], kb, reads=[kb], writes=[ktscr])
            cx.end_scope(e0)
        for t in range(ntile):
            tok0 = t * TT
            with Scope() as et:
                hnT = cx.sb(et, [128, NKC, TT], BF16, "hnT")
                oT = cx.sb(et, [128, NKC, TT], BF16, "oTb")
                with Scope() as e1:
                    if "7" in phases:
                        norm_transpose(cx, e1, job, lambda s, P: job.h_scr[tok0 + s * 128:tok0 + s * 128 + P, :], None, V_BG, hnT, 0,
                                       vecs, identb, compute_rstd=False, hrstd=hrstd, t=t)
                    cx.end_scope(e1)
                with Scope() as e2:
                    ring = [cx.sb(e2, [128, 8, 512], BF16, "wrb") for _ in range(3)]
                    KTh = [cx.sb(e2, [128, nkt * 128], BF16, "KTh") for _ in range(2)]
                    Vh = [cx.sb(e2, [128, nkt, 128], BF16, "Vh") for _ in range(2)]
                    qT = cx.sb(e2, [128, 12, TT], BF16, "qT")
                    sgT = cx.sb(e2, [128, 4, TT], BF16, "sgT")
                    qf = [cx.sb(e2, [128, TT], F32, "qf") for _ in range(2)]
                    sqf = [cx.sb(e2, [128, TT], BF16, "sqf") for _ in range(2)]
                    rq = [cx.sb(e2, [128, TT], F32, "rq") for _ in range(2)]
                    pe_ = [cx.sb(e2, [128, 4, P], BF16, "pexp") for _ in range(4)]
                    rden = cx.sb(e2, [128, 4, P], F32, "rden")
                    of = cx.sb(e2, [128, 4, P], F32, "of")
                    ppq = cx.ps(e2, [128, 4, 512], F32, "ppq")
                    pw = [cx.ps(e2, [128, 512], F32, "pw") for _ in range(2)]
                    pnum = cx.ps(e2, [128, 512], F32, "pnum")
                    pden = cx.ps(e2, [128, 512], F32, "pden")
                    ctr = [0]
                    nw = 0
                    nq = 0
                    npe = 0
                    nkeys = nkt_cache * 128 + S
                    qT2 = [qT, cx.sb(e2, [128, 12, TT], BF16, "qTb")]
                    sgT2 = [sgT, cx.sb(e2, [128, 4, TT], BF16, "sgTb")]
                    nwc, nqc, npc = [0], [0], [0]

                    def proj(kvh):
                        kth, vh = KTh[kvh % 2], Vh[kvh % 2]
                        qT, sgT = qT2[kvh % 2], sgT2[kvh % 2]
                        cx.dma("sp", kth[:, :nkeys], ktscr[kvh, :, :nkeys], kth, reads=[ktscr], writes=[kth])
                        if nkt_cache:
                            cx.dma("pool", vh[:, :nkt_cache, :], A["cv"].rearrange("(kt p) (h c) -> p kt h c", p=128, c=128)[:, :, kvh, :], vh, writes=[vh])
                        if S >= 128:
                            cx.dma("pool", vh[:, nkt_cache:, :], job.v_out.h.rearrange("(kt p) (h c) -> p kt h c", p=128, c=128)[:, :, kvh, :], vh, writes=[vh])
                        else:
                            cx.dma("pool", vh[:S, nkt_cache, :], job.v_out.h[:, kvh * 128:(kvh + 1) * 128], vh, writes=[vh])
                        for g in range(4):
                            col0 = (g * D + kvh * 512) if g < 3 else (3 * D + kvh * 512)
                            for k4 in range(4):
                                slot = stream_weights(cx, e2, A["b_w_in"], k4 * 1024, 1024, col0, 512, ring, ctr)
                                for oc in range(4):
                                    for kc in range(8):
                                        kk = k4 * 8 + kc
                                        cx.op("pe", lambda h, slot=slot, oc=oc, kc=kc, kk=kk: h.matmul(
                                            ppq[:, oc, :TT], lhsT=slot[:, kc, oc * 128:(oc + 1) * 128], rhs=hnT[:, kk, :],
                                            start=(kk == 0), stop=(kk == NKC - 1)), [slot, hnT], [ppq])
                                    yield
                            if g == 3:
                                cx.op("act", lambda h: h.activation(out=sgT[:, :, :], in_=ppq[:, :, :TT], func=AF.Silu), [ppq], [sgT])
                                continue
                            for oc in range(4):
                                q_, s_, r_ = qf[nqc[0] % 2], sqf[nqc[0] % 2], rq[nqc[0] % 2]
                                nqc[0] += 1
                                pwt = pw[nwc[0] % 2]
                                nwc[0] += 1
                                cx.op("act", lambda h, oc=oc, q_=q_: h.activation(out=q_[:, :], in_=ppq[:, oc, :TT], func=AF.Copy), [ppq], [q_])
                                cx.op("act", lambda h, oc=oc, s_=s_: h.activation(out=s_[:, :], in_=ppq[:, oc, :TT], func=AF.Square), [ppq], [s_])
                                cx.op("pe", lambda h, s_=s_, pwt=pwt: h.matmul(pwt[:, :TT], lhsT=onesb[:], rhs=s_[:, :], start=True, stop=True), [onesb, s_], [pwt])
                                cx.op("dve", lambda h, r_=r_, pwt=pwt: h.tensor_scalar(out=r_[:, :], in0=pwt[:, :TT], scalar1=1.0 / 128, scalar2=1e-6, op0=ALU.mult, op1=ALU.add), [pwt], [r_])
                                cx.op("act", lambda h, r_=r_: h.activation(out=r_[:, :], in_=r_[:, :], func=AF.Ln), [r_], [r_])
                                cx.op("act", lambda h, r_=r_: h.activation(out=r_[:, :], in_=r_[:, :], func=AF.Exp, scale=-0.5), [r_], [r_])
                                cx.op("dve", lambda h, q_=q_, r_=r_, g=g, oc=oc: h.scalar_tensor_tensor(out=qT[:, g * 4 + oc, :], in0=q_[:, :], scalar=gqs[:, g:g + 1], in1=r_[:, :], op0=ALU.mult, op1=ALU.mult), [q_, r_, gqs], [qT])
                                yield

                    def attn(kvh):
                        kth, vh = KTh[kvh % 2], Vh[kvh % 2]
                        qT, sgT = qT2[kvh % 2], sgT2[kvh % 2]
                        for s in range(nsub):
                            qa = nkt_cache + t * max(TT // 128, 1) + s
                            units = []
                            for g in range(3):
                                for dl in range(NDELTA[g]):
                                    ktile = qa - dl
                                    if ktile >= 0:
                                        units.append((g, dl, ktile))
                            qs = slice(s * 128, s * 128 + P)
                            for ui, (g, dl, ktile) in enumerate(units):
                                kw = 128
                                if ktile >= nkt_cache:
                                    kw = min(128, S - (ktile - nkt_cache) * 128)
                                pwt = pw[nwc[0] % 2]
                                nwc[0] += 1
                                pb = pe_[npc[0] % 4]
                                npc[0] += 1
                                mi = MOFF[g] + dl
                                cx.op("pe", lambda h, pwt=pwt, ktile=ktile, kw=kw, g=g, qs=qs, kth=kth: h.matmul(
                                    pwt[:kw, 0:4 * P].rearrange("p (a b) -> p a b", a=4), lhsT=kth[:, ktile * 128:ktile * 128 + kw],
                                    rhs=qT[:, g * 4:g * 4 + 4, qs], start=True, stop=True), [kth, qT], [pwt])
                                cx.op("act", lambda h, pwt=pwt, pb=pb, kw=kw: h.activation(out=pb[:kw].rearrange("p a b -> p (a b)"), in_=pwt[:kw, 0:4 * P], func=AF.Exp), [pwt], [pb])
                                meng = "pool" if ui % 2 == 0 else "dve"
                                cx.op(meng, lambda h, pb=pb, kw=kw, mi=mi: h.tensor_tensor(out=pb[:kw], in0=pb[:kw], in1=amask[:kw, mi, :P].unsqueeze(1).to_broadcast([kw, 4, P]), op=ALU.mult), [pb, amask], [pb])
                                first, lastu = (ui == 0), (ui == len(units) - 1)
                                cx.op("pe", lambda h, pb=pb, kw=kw, ktile=ktile, vh=vh, first=first, lastu=lastu: h.matmul(
                                    pnum[:, 0:4 * P], lhsT=vh[:kw, ktile, :], rhs=pb[:kw].rearrange("p a b -> p (a b)"),
                                    start=first, stop=lastu), [vh, pb], [pnum])
                                cx.op("pe", lambda h, pb=pb, kw=kw, first=first, lastu=lastu: h.matmul(
                                    pden[:, 0:4 * P], lhsT=onesb[:kw, :], rhs=pb[:kw].rearrange("p a b -> p (a b)"),
                                    start=first, stop=lastu), [onesb, pb], [pden])
                                yield
                            cx.op("act", lambda h: h.activation(out=rden[:].rearrange("p a b -> p (a b)"), in_=pden[:, 0:4 * P], func=AF.Ln), [pden], [rden])
                            cx.op("act", lambda h: h.activation(out=rden[:].rearrange("p a b -> p (a b)"), in_=rden[:].rearrange("p a b -> p (a b)"), func=AF.Exp, scale=-1.0), [rden], [rden])
                            cx.op("dve", lambda h: h.tensor_tensor(out=of[:].rearrange("p a b -> p (a b)"), in0=pnum[:, 0:4 * P], in1=rden[:].rearrange("p a b -> p (a b)"), op=ALU.mult), [pnum, rden], [of])
                            cx.op("pool", lambda h, kvh=kvh, qs=qs: h.tensor_tensor(out=oT[:, kvh * 4:kvh * 4 + 4, qs], in0=of[:], in1=sgT[:, :, qs], op=ALU.mult), [of, sgT], [oT])

                    if "8" in phases:
                        _rr(proj(0))
                        for kvh in range(8):
                            _rr(attn(kvh), proj(kvh + 1) if kvh + 1 < 8 else iter(()))
                    cx.end_scope(e2)
                with Scope() as e3:
                    ring = [cx.sb(e3, [128, 8, 512], BF16, "wr7") for _ in range(4)]
                    xres = [cx.sb(e3, [128, 512], F32, "hres") for _ in range(4)]
                    hsb = [cx.sb(e3, [128, 512], F32, "ysb") for _ in range(4)]
                    pp = [cx.ps(e3, [128, 4, 512], F32, "pp7") for _ in range(2)]
                    ctr = [0]
                    n = 0
                    for cg in (range(8) if "9" in phases else ()):
                        ppt = pp[cg % 2]
                        for k4 in range(4):
                            slot = stream_weights(cx, e3, A["b_w_out"], k4 * 1024, 1024, cg * 512, 512, ring, ctr)
                            for s in range(nsub):
                                for kc in range(8):
                                    kk = k4 * 8 + kc
                                    cx.op("pe", lambda h, ppt=ppt, slot=slot, s=s, kc=kc, kk=kk: h.matmul(
                                        ppt[:P, s, :], lhsT=oT[:, kk, s * 128:s * 128 + P], rhs=slot[:, kc, :],
                                        start=(kk == 0), stop=(kk == NKC - 1)), [slot, oT], [ppt])
                        for s in range(nsub):
                            xr, hb = xres[n % 4], hsb[n % 4]
                            n += 1
                            rows = slice(tok0 + s * 128, tok0 + s * 128 + P)
                            cx.dma("sp", xr[:P, :], job.h_scr[rows, cg * 512:(cg + 1) * 512], xr, reads=[job.h_scr], writes=[xr])
                            cx.op("dve", lambda h, ppt=ppt, s=s, xr=xr, hb=hb: h.tensor_tensor(out=hb[:P, :], in0=ppt[:P, s, :], in1=xr[:P, :], op=ALU.add), [ppt, xr], [hb])
                            cx.dma("sp", job.y[rows, cg * 512:(cg + 1) * 512], hb[:P, :], hb, reads=[hb], writes=[job.y])
                    cx.end_scope(e3)
                cx.end_scope(et)
        cx.end_scope(eb)


def _consts():
    bf = ml_dtypes.bfloat16
    identf = np.eye(128, dtype=np.float32)
    bones = np.zeros((128, 128), np.float32)
    bones[:64, :64] = 1
    bones[64:, 64:] = 1
    i = np.arange(128)
    su = (i[:, None] < i[None, :]).astype(np.float32)
    iu = (i[:, None] <= i[None, :]).astype(np.float32)
    sl = (i[None, :] < i[:, None]).astype(np.float32)
    rmask = np.stack([np.stack([m] * 4, 1) for m in (su, iu, sl)], 1)
    am = np.zeros((128, 24, 128), np.float32)
    for g in range(3):
        for dl in range(NDELTA[g]):
            d = 128 * dl + i[None, :] - i[:, None]
            ok = (d >= 0) & (d % DILS[g] == 0) & (d // DILS[g] <= 128)
            am[:, MOFF[g] + dl, :] = ok
    return dict(identf=identf, bones=bones, rmask=np.ascontiguousarray(rmask), amask=am)


def _fm(v):
    return np.ascontiguousarray(np.asarray(v, np.float32).reshape(NKC, 128).T)


_CACHE = {}


def kernel(x_prompt, x_sample, state_wkv, state_shift, cache_k, cache_v,
           a_norm_g, a_mu, a_w_in, a_w0, a_w1, a_w2, a_a0, a_a1, a_a2,
           a_k_k, a_k_a, a_r_k, a_lnx_g, a_lnx_b, a_w_out,
           kv_norm_g, w_kv, k_norm_g, b_norm_g, b_w_in, q_norm_g, b_w_out, _SP=None, _NC=8, _PH="AB0123456789", _DP=True, _DS=True):
    f = lambda a: np.ascontiguousarray(np.asarray(a, np.float32))
    B, S, _ = x_prompt.shape
    SP = _SP or S
    key = (SP, _PH, _DP, _DS)
    if key not in _CACHE:
        _CACHE[key] = build_program(SP, phases=_PH, do_prompt=_DP, do_sample=_DS)
    nc = _CACHE[key]
    cs = _consts()
    vlist = [a_norm_g[0]] + [a_mu[0, j] for j in range(6)] + [a_w0[0], a_a0[0], a_k_k[0], a_k_a[0],
             np.asarray(a_r_k[0]).reshape(-1), a_lnx_g[0], a_lnx_b[0], kv_norm_g, b_norm_g[0]]
    vecs = np.ascontiguousarray(np.stack([_fm(v) for v in vlist], 1))
    shared = dict(
        a_w_in=f(a_w_in[0]), a_w1=f(a_w1[0]), a_w2=f(a_w2[0]), a_a1=f(a_a1[0]), a_a2=f(a_a2[0]),
        a_w_out=f(a_w_out[0]), w_kv=f(w_kv), b_w_in=f(b_w_in[0]), b_w_out=f(b_w_out[0]),
        vecs=vecs, grow=f(a_norm_g[0]), gq=np.ascontiguousarray(f(q_norm_g[0]).T),
        gkb=np.ascontiguousarray(np.broadcast_to(f(k_norm_g)[None, :], (128, 128))),
        identf=cs["identf"], bones=cs["bones"], rmask=cs["rmask"], amask=cs["amask"])
    in_maps = []
    for c in range(_NC):
        m = dict(shared)
        m["xp"] = f(x_prompt[c % B, :SP])
        m["xs"] = f(x_sample[c])
        m["swkv"] = f(state_wkv[0, c]).reshape(D, 64)
        m["sshift"] = _fm(state_shift[0, c])
        m["ck"] = f(cache_k[c]).reshape(-1, 1024)
        m["cv"] = f(cache_v[c]).reshape(-1, 1024)
        in_maps.append(m)
    res = run_bass_kernel_spmd(nc, in_maps, core_ids=list(range(_NC)))
    R = list(res.results)
    while len(R) < 8:
        R.append(R[0])
    y_p = np.stack([R[b]["yp"] for b in range(B)])
    y_s = np.stack([R[c]["ys"] for c in range(8)])
    wkv_p = np.stack([R[b]["wkvp"] for b in range(B)])[None]
    sh_p = np.stack([R[b]["shiftp"] for b in range(B)])[None]
    k_p = np.stack([R[b]["kp"].reshape(SP, 8, 128) for b in range(B)])
    v_p = np.stack([R[b]["vp"].reshape(SP, 8, 128) for b in range(B)])
    wkv_s = np.stack([R[c]["wkvs"] for c in range(8)])[None]
    sh_s = np.stack([R[c]["shifts"] for c in range(8)])[None]
    k_s = np.stack([R[c]["ks"].reshape(8, 8, 128) for c in range(8)])
    v_s = np.stack([R[c]["vs"].reshape(8, 8, 128) for c in range(8)])
    return (y_p, y_s, wkv_p, sh_p, k_p, v_p, wkv_s, sh_s, k_s, v_s)
```

```python
import os
import numpy as np
import ml_dtypes
A2CUT = int(os.environ.get("A2CUT", "9"))
from contextlib import ExitStack
import concourse.bass as bass
import concourse.mybir as mybir
from concourse.bass_utils import run_bass_kernel_spmd

F32 = mybir.dt.float32
BF16 = mybir.dt.bfloat16
AF = mybir.ActivationFunctionType
ALU = mybir.AluOpType
AX = mybir.AxisListType

D = 4096
NKC = 32
C0 = float(np.exp(-0.5))
V_G, V_MU, V_W0, V_A0, V_KK, V_KA, V_RK, V_LG, V_LB, V_KVG, V_BG = 0, 1, 7, 8, 9, 10, 11, 12, 13, 14, 15
NV = 16
DILS = (1, 4, 16)
NDELTA = (2, 5, 17)
MOFF = (0, 2, 7)


class T:
    __slots__ = ("h", "name", "lw", "rd", "dsem", "dcnt")

    def __init__(self, h, name):
        self.h = h
        self.name = name
        self.lw = None
        self.rd = {}
        self.dsem = None
        self.dcnt = 0

    def __getitem__(self, k):
        return self.h[k]


class Eng:
    def __init__(self, name, sem):
        self.name = name
        self.sem = sem
        self.cnt = 0
        self.epoch = 0
        self.waited = {}
        self.prog = []


class _Rec:
    def __init__(self):
        self.call = None

    def __getattr__(self, name):
        def f(*args, **kw):
            self.call = (name, args, kw)
        return f


class Scope(ExitStack):
    def __init__(self):
        super().__init__()
        self.tiles = []


class Ctx:
    def __init__(self, nc, es):
        self.free_dsems = []
        self.nc = nc
        self.es = es
        self.eng = {}
        for name in ("pe", "act", "dve", "pool", "sp"):
            sem = es.enter_context(nc.semaphore("s_" + name))
            self.eng[name] = Eng(name, sem)
        self.nt = 0
        self.dsems = []

    def sb(self, es, shape, dt, name):
        self.nt += 1
        h = es.enter_context(self.nc.sbuf_tensor(f"{name}_{self.nt}", list(shape), dt))
        t = T(h, f"{name}_{self.nt}")
        if isinstance(es, Scope):
            es.tiles.append(t)
        return t

    def ps(self, es, shape, dt, name):
        self.nt += 1
        h = es.enter_context(self.nc.psum_tensor(f"{name}_{self.nt}", list(shape), dt))
        return T(h, f"{name}_{self.nt}")

    def view(self, h, name):
        self.nt += 1
        return T(h, f"{name}_{self.nt}")

    def _need(self, e, reads, writes):
        need = {}

        def add(ev):
            if ev is None:
                return
            kind, key, val = ev
            k = (kind, key if kind == "e" else key.num)
            if k not in need or need[k][2] < val:
                need[k] = ev
        for t in reads:
            add(t.lw)
        for t in writes:
            add(t.lw)
            for ev in t.rd.values():
                add(ev)
        for k, ev in need.items():
            kind, key, val = ev
            if kind == "e":
                if key[1] != self.eng[key[0]].epoch:
                    continue
                if key[0] == "pe" and e.name == "pe":
                    continue
                if e.waited.get(k, 0) >= val:
                    continue
                e.waited[k] = val
                e.prog.append(("w", self.eng[key[0]].sem, val))
                continue
            if e.waited.get(k, 0) >= val:
                continue
            e.waited[k] = val
            e.prog.append(("w", self.eng[key].sem if kind == "e" else key, val))

    def op(self, ename, fn, reads=(), writes=()):
        e = self.eng[ename]
        self._need(e, reads, writes)
        e.cnt += 1
        rec = _Rec()
        fn(rec)
        name, args, kw = rec.call
        e.prog.append(("i", (lambda h, name=name, args=args, kw=kw: getattr(h, name)(*args, **kw)), e.sem, 1))
        ev = ("e", (ename, e.epoch), e.cnt)
        for t in reads:
            t.rd[("e", ename)] = ev
        for t in writes:
            t.lw = ev
            t.rd = {}

    def dma(self, qname, out, in_, anchor, reads=(), writes=()):
        e = self.eng[qname]
        self._need(e, reads, writes)
        if anchor.dsem is None:
            if self.free_dsems:
                anchor.dsem, anchor.dcnt = self.free_dsems.pop()
            else:
                anchor.dsem = self.es.enter_context(self.nc.semaphore("d_" + anchor.name))
            self.dsems.append(anchor)
        e.prog.append(("i", (lambda h, out=out, in_=in_: h.dma_start(out=out, in_=in_)), anchor.dsem, 16))
        anchor.dcnt += 16
        assert anchor.dcnt < 60000
        ev = ("d", anchor.dsem, anchor.dcnt)
        for t in reads:
            t.rd[("d", anchor.dsem.num)] = ev
        for t in writes:
            t.lw = ev
            t.rd = {}

    def barrier(self):
        for e in self.eng.values():
            for o in self.eng.values():
                if o is e or o.cnt == 0:
                    continue
                k = ("e", (o.name, o.epoch))
                if e.waited.get(k, 0) >= o.cnt:
                    continue
                e.waited[k] = o.cnt
                e.prog.append(("w", o.sem, o.cnt))
            for a in self.dsems:
                k = ("d", a.dsem.num)
                if e.waited.get(k, 0) >= a.dcnt:
                    continue
                e.waited[k] = a.dcnt
                e.prog.append(("w", a.dsem, a.dcnt))

    def new_epoch(self):
        for e in self.eng.values():
            e.sem = self.es.enter_context(self.nc.semaphore(f"s_{e.name}_{e.epoch + 1}"))
            e.cnt = 0
            e.epoch += 1
            e.waited = {k: v for k, v in e.waited.items() if k[0] != "e"}

    def end_scope(self, sc):
        self.barrier()
        if max(e.cnt for e in self.eng.values()) > 30000:
            self.new_epoch()
        for t in sc.tiles:
            if t.dsem is not None:
                self.free_dsems.append((t.dsem, t.dcnt))
                self.dsems.remove(t)
                t.dsem = None

    def emit(self):
        nc = self.nc
        handles = {"pe": "tensor", "act": "scalar", "dve": "vector", "pool": "gpsimd", "sp": "sync"}
        with nc.Block() as block:
            def mk(e):
                def body(h):
                    pend = []
                    for a in e.prog:
                        if a[0] == "w":
                            pend.append(a)
                            continue
                        for w in pend[:-1]:
                            h.wait_ge(w[1], w[2])
                        ins = a[1](h)
                        if pend:
                            ins._wait_ge(pend[-1][1], pend[-1][2])
                        ins.then_inc(a[2], a[3])
                        pend = []
                    for w in pend:
                        h.wait_ge(w[1], w[2])
                return body
            for n, attr in handles.items():
                getattr(block, attr)(mk(self.eng[n]))


class Job:
    pass


def build_program(SP, do_sample=True, SC=2048, phases="AB0123456789", do_prompt=True):
    nc = bass.Bass("TRN2", target_bir_lowering=False)

    def din(name, shape, dt=F32):
        return nc.dram_tensor(name, list(shape), dt, kind="ExternalInput").ap()

    def dout(name, shape, dt=F32):
        return nc.dram_tensor(name, list(shape), dt, kind="ExternalOutput").ap()

    def dscr(name, shape, dt=F32):
        return nc.dram_tensor(name, list(shape), dt).ap()

    A = {}
    A["xp"] = din("xp", [SP, D])
    A["xs"] = din("xs", [8, D])
    A["swkv"] = din("swkv", [D, 64])
    A["sshift"] = din("sshift", [128, NKC])
    A["ck"] = din("ck", [SC, 1024])
    A["cv"] = din("cv", [SC, 1024])
    A["a_w_in"] = din("a_w_in", [D, 4 * D])
    A["a_w1"] = din("a_w1", [D, 128])
    A["a_w2"] = din("a_w2", [128, D])
    A["a_a1"] = din("a_a1", [D, 128])
    A["a_a2"] = din("a_a2", [128, D])
    A["a_w_out"] = din("a_w_out", [D, D])
    A["w_kv"] = din("w_kv", [D, 2048])
    A["b_w_in"] = din("b_w_in", [D, 4 * D])
    A["b_w_out"] = din("b_w_out", [D, D])
    A["vecs"] = din("vecs", [128, NV, NKC])
    A["grow"] = din("grow", [D])
    A["gq"] = din("gq", [128, 3])
    A["gkb"] = din("gkb", [128, 128])
    A["identf"] = din("identf", [128, 128])
    A["bones"] = din("bones", [128, 128])
    A["rmask"] = din("rmask", [128, 3, 4, 128])
    A["amask"] = din("amask", [128, 24, 128])

    O = {}
    O["yp"] = dout("yp", [SP, D])
    O["ys"] = dout("ys", [8, D])
    O["wkvp"] = dout("wkvp", [64, 64, 64])
    O["shiftp"] = dout("shiftp", [D])
    O["kp"] = dout("kp", [SP, 1024])
    O["vp"] = dout("vp", [SP, 1024])
    O["wkvs"] = dout("wkvs", [64, 64, 64])
    O["shifts"] = dout("shifts", [D])
    O["ks"] = dout("ks", [8, 1024])
    O["vs"] = dout("vs", [8, 1024])

    WSC.clear()
    WDONE.clear()
    for wn in ("a_w_in", "a_w_out", "w_kv", "b_w_in", "b_w_out"):
        WSC[wn] = dscr(wn + "_bf16", [int(A[wn].shape[0]) * int(A[wn].shape[1]) // (128 * 4096), 128, 4096], BF16)
    with ExitStack() as es:
        cx = Ctx(nc, es)
        vecs = cx.sb(es, [128, NV, NKC], F32, "vecs")
        identf = cx.sb(es, [128, 128], F32, "identf")
        identb = cx.sb(es, [128, 128], BF16, "identb")
        bones = cx.sb(es, [128, 128], BF16, "bones")
        onesb = cx.sb(es, [128, 128], BF16, "onesb")
        onesf = cx.sb(es, [128, 128], F32, "onesf")
        rmask = cx.sb(es, [128, 3, 4, 128], BF16, "rmask")
        amask = cx.sb(es, [128, 24, 128], BF16, "amask")
        gqs = cx.sb(es, [128, 3], F32, "gqs")
        gkb = cx.sb(es, [128, 128], F32, "gkb")
        ident2 = cx.sb(es, [128, 4, 128], BF16, "ident2")
        cx.dma("sp", vecs[:], A["vecs"], vecs, writes=[vecs])
        cx.dma("sp", identf[:], A["identf"], identf, writes=[identf])
        cx.dma("pool", bones[:], A["bones"], bones, writes=[bones])
        cx.dma("pool", rmask[:], A["rmask"], rmask, writes=[rmask])
        cx.dma("pool", amask[:], A["amask"], amask, writes=[amask])
        cx.dma("sp", gqs[:], A["gq"], gqs, writes=[gqs])
        cx.dma("sp", gkb[:], A["gkb"], gkb, writes=[gkb])
        cx.op("dve", lambda h: h.tensor_copy(out=identb[:], in_=identf[:]), [identf], [identb])
        cx.op("dve", lambda h: h.memset(onesb[:], 1.0), [], [onesb])
        cx.op("dve", lambda h: h.memset(onesf[:], 1.0), [], [onesf])
        cx.op("dve", lambda h: h.tensor_scalar(out=gqs[:], in0=gqs[:], scalar1=float(128 ** -0.5), scalar2=None, op0=ALU.mult), [gqs], [gqs])
        for i in range(4):
            cx.op("dve", lambda h, i=i: h.tensor_copy(out=ident2[:, i, :], in_=identf[:]), [identf], [ident2])

        def vec(idx, c):
            return vecs[:, idx, c:c + 1]

        scr_h_p = cx.view(dscr("scr_h_p", [SP, D]), "scr_h_p")
        scr_h_s = cx.view(dscr("scr_h_s", [8, D]), "scr_h_s")
        scr_p = cx.view(dscr("scr_p", [4, D, 512]), "scr_p")
        ktscr = cx.view(dscr("ktscr", [8, 128, max(SP, SC + 128)], BF16), "ktscr")
        yp_t = cx.view(O["yp"], "yp")
        ys_t = cx.view(O["ys"], "ys")
        kp_t = cx.view(O["kp"], "kp")
        vp_t = cx.view(O["vp"], "vp")
        ks_t = cx.view(O["ks"], "ks")
        vs_t = cx.view(O["vs"], "vs")
        outs_misc = cx.view(O["wkvp"], "misc")

        jobs = []
        jp = Job()
        jp.name, jp.S, jp.TT, jp.C = "p", SP, 512, 128
        jp.x, jp.h_scr, jp.y = A["xp"], scr_h_p, yp_t
        jp.k_out, jp.v_out = kp_t, vp_t
        jp.wkv_out, jp.shift_out = O["wkvp"], O["shiftp"]
        jp.has_state = False
        jp.ncache = 0
        if do_prompt:
            jobs.append(jp)
        if do_sample:
            js = Job()
            js.name, js.S, js.TT, js.C = "s", 8, 8, 8
            js.x, js.h_scr, js.y = A["xs"], scr_h_s, ys_t
            js.k_out, js.v_out = ks_t, vs_t
            js.wkv_out, js.shift_out = O["wkvs"], O["shifts"]
            js.has_state = True
            js.ncache = SC // 128
            jobs.append(js)

        wq = [0]

        for job in jobs:
            run_job(cx, nc, es, job, A, O, vec, dict(
                vecs=vecs, identf=identf, identb=identb, bones=bones, onesb=onesb, onesf=onesf,
                rmask=rmask, amask=amask, gqs=gqs, gkb=gkb, ident2=ident2, scr_p=scr_p, ktscr=ktscr,
                outs_misc=outs_misc), phases)

        cx.barrier()
        cx.emit()
    return nc


def run_job(cx, nc, es_glob, job, A, O, vec, K, phases):
    S, TT, C = job.S, job.TT, job.C
    ntile = S // TT
    nsub = (TT + 127) // 128
    P = min(TT, 128)
    nch = TT // C
    nst = {128: 6, 8: 2}[C]
    identb, identf, bones, onesb, onesf = K["identb"], K["identf"], K["bones"], K["onesb"], K["onesf"]
    rmask, amask, gqs, gkb, ident2, scr_p, vecs = K["rmask"], K["amask"], K["gqs"], K["gkb"], K["ident2"], K["scr_p"], K["vecs"]

    with Scope() as ej:
        Sst = cx.sb(ej, [128, NKC, 64], F32, "Sst")
        hrstd = cx.sb(ej, [128, 16], F32, "hrstd")
        prevcol = cx.sb(ej, [128, NKC, 1], BF16, "prevcol")
        if job.has_state:
            with Scope() as e0:
                zp = cx.sb(e0, [128, NKC, 128], F32, "zp")
                pst = cx.ps(e0, [128, 4, 128], F32, "pst")
                sh = cx.sb(e0, [128, NKC], F32, "sh")
                cx.op("dve", lambda h: h.memset(zp[:], 0.0), [], [zp])
                src = A["swkv"].rearrange("(hp e v) k -> e v hp k", e=2, v=64)
                cx.dma("sp", zp[0:64, :, 0:64], src[0], zp, writes=[zp])
                cx.dma("sp", zp[64:128, :, 64:128], src[1], zp, writes=[zp])
                for hp in range(NKC):
                    j = hp % 4
                    cx.op("pe", lambda h, hp=hp, j=j: h.transpose(out=pst[:, j, :], in_=zp[:, hp, :], identity=identf[:]), [zp, identf], [pst])
                    if j == 3:
                        h0 = hp - 3
                        cx.op("dve", lambda h, h0=h0: h.tensor_copy(out=Sst[0:64, h0:h0 + 4, :], in_=pst[0:64, :, 0:64]), [pst], [Sst])
                        cx.op("act", lambda h, h0=h0: h.activation(out=Sst[64:128, h0:h0 + 4, :], in_=pst[64:128, :, 64:128], func=AF.Copy), [pst], [Sst])
                cx.dma("sp", sh[:], A["sshift"], sh, writes=[sh])
                cx.op("dve", lambda h: h.tensor_copy(out=prevcol[:, :, 0], in_=sh[:]), [sh], [prevcol])
                cx.end_scope(e0)
        else:
            cx.op("dve", lambda h: h.memset(Sst[:], 0.0), [], [Sst])
            cx.op("dve", lambda h: h.memset(prevcol[:], 0.0), [], [prevcol])

        if "A" in phases:
            for t in range(ntile):
                layer_a_tile(cx, nc, job, t, A, O, vec, K, Sst, hrstd, prevcol, phases)
            with Scope() as e0:
                pst = cx.ps(e0, [64, 4, 128], F32, "pso")
                so = cx.sb(e0, [64, NKC, 128], F32, "so")
                for hp in range(NKC):
                    j = hp % 4
                    cx.op("pe", lambda h, hp=hp, j=j: h.transpose(out=pst[:, j, :], in_=Sst[:, hp, :], identity=identf[:]), [Sst, identf], [pst])
                    if j == 3:
                        h0 = hp - 3
                        cx.op("dve", lambda h, h0=h0: h.tensor_copy(out=so[:, h0:h0 + 4, :], in_=pst[:]), [pst], [so])
                dst = job.wkv_out.rearrange("(hp e) v k -> v hp e k", e=2)
                cx.dma("sp", dst, so[:].rearrange("v hp (e k) -> v hp e k", e=2), so, reads=[so], writes=[K["outs_misc"]])
                cx.end_scope(e0)

        if "B" in phases:
            layer_b(cx, nc, job, A, O, vec, K, hrstd, phases)
        cx.end_scope(ej)


def rms_rstd(cx, ss_in, out, P, scale, eps, reads_extra=()):
    st, sap = ss_in
    ot, oap = out
    cx.op("dve", lambda h: h.tensor_scalar(out=oap, in0=sap, scalar1=scale, scalar2=eps, op0=ALU.mult, op1=ALU.add), [st], [ot])
    cx.op("act", lambda h: h.activation(out=oap, in_=oap, func=AF.Sqrt), [ot], [ot])
    cx.op("dve", lambda h: h.reciprocal(out=oap, in_=oap), [ot], [ot])


def norm_transpose(cx, es, job, src_rows, rstd_ap_fn, gidx, xnT, col0, vecs, identb, last_out=None, grow=None, compute_rstd=True, hrstd=None, t=0):
    TT = job.TT
    nsub = (TT + 127) // 128
    P = min(TT, 128)
    nb = min(nsub, 4)
    xt2 = [cx.sb(es, [128, D], F32, "xt") for _ in range(nb)]
    xs2 = [cx.sb(es, [128, D], BF16, "xs") for _ in range(2)]
    junk = cx.sb(es, [128, D], BF16, "junk")
    ss = cx.sb(es, [128, 4], F32, "ss")
    rs = cx.sb(es, [128, 4], F32, "rs")
    pt2 = [cx.ps(es, [128, 4, 128], BF16, "pt") for _ in range(2)]
    for s in range(nsub):
        xt, xs = xt2[s % nb], xs2[s % 2]
        cx.dma("sp", xt[:P, :], src_rows(s, P), xt, writes=[xt])
        if compute_rstd:
            cx.op("act", lambda h, xt=xt, s=s: h.activation(out=junk[:P, :], in_=xt[:P, :], func=AF.Square, accum_out=ss[:P, s:s + 1]), [xt], [junk, ss])
            rms_rstd(cx, (ss, ss[:P, s:s + 1]), (rs, rs[:P, s:s + 1]), P, 1.0 / D, 1e-6)
            rap, rt = rs[:P, s:s + 1], rs
        else:
            rap, rt = hrstd[:P, t * 4 + s:t * 4 + s + 1], hrstd
        cx.op("act", lambda h, xt=xt, xs=xs, rap=rap: h.activation(out=xs[:P, :], in_=xt[:P, :], func=AF.Identity, scale=rap), [xt, rt], [xs])
        if last_out is not None and s == nsub - 1:
            xnf = cx.sb(es, [128, D], F32, "xnf")
            gbc = cx.sb(es, [128, D], F32, "gbc")
            cx.dma("sp", gbc[:P, :], grow.partition_broadcast(P), gbc, writes=[gbc])
            cx.op("act", lambda h, xt=xt, rap=rap: h.activation(out=xnf[:P, :], in_=xt[:P, :], func=AF.Identity, scale=rap), [xt, rt], [xnf])
            cx.op("dve", lambda h: h.tensor_tensor(out=xnf[:P, :], in0=xnf[:P, :], in1=gbc[:P, :], op=ALU.mult), [xnf, gbc], [xnf])
            cx.dma("sp", last_out.rearrange("(o d) -> o d", o=1), xnf[P - 1:P, :], xnf, reads=[xnf])
        for c4 in range(8):
            p = pt2[c4 % 2]
            for j in range(4):
                c = c4 * 4 + j
                cx.op("pe", lambda h, c=c, j=j, p=p, xs=xs: h.transpose(out=p[:, j, :P], in_=xs[:P, c * 128:(c + 1) * 128], identity=identb[:P, :P]), [xs, identb], [p])
            cx.op("dve", lambda h, c4=c4, p=p, s=s: h.tensor_tensor(
                out=xnT[:, c4 * 4:c4 * 4 + 4, col0 + s * 128:col0 + s * 128 + P], in0=p[:, :, :P],
                in1=vecs[:, gidx, c4 * 4:c4 * 4 + 4].unsqueeze(2).to_broadcast([128, 4, P]), op=ALU.mult), [p, vecs], [xnT])


WSC = {}
WDONE = {}


def stream_weights(cx, es, W, row0, nrows, col0, ncols, ring, ctr):
    slot = ring[ctr[0] % len(ring)]
    ctr[0] += 1
    nk = nrows // 128
    name = W.tensor.name
    key = (name, row0, nrows, col0, ncols)
    w16 = WSC[name]
    assert nk * ncols == 4096 and tuple(slot.h.shape) == (128, nk, ncols)
    sflat = slot.h[:].rearrange("p a b -> p (a b)")
    if key not in WDONE:
        gid = sum(1 for k in WDONE if k[0] == name)
        WDONE[key] = gid
        cx.dma("pool", slot[:, :nk, :ncols], W[row0:row0 + nrows, col0:col0 + ncols].rearrange("(kc p) n -> p kc n", p=128), slot, writes=[slot])
        cx.dma("act", w16[gid], sflat, slot, reads=[slot])
    else:
        cx.dma("pool", sflat, w16[WDONE[key]], slot, writes=[slot])
    return slot


def layer_a_tile(cx, nc, job, t, A, O, vec, K, Sst, hrstd, prevcol, phases):
    S, TT, C = job.S, job.TT, job.C
    nsub = (TT + 127) // 128
    P = min(TT, 128)
    nch = TT // C
    nst = {128: 6, 8: 2}[C]
    tok0 = t * TT
    ntile = S // TT
    identb, identf, bones, onesb, onesf = K["identb"], K["identf"], K["bones"], K["onesb"], K["onesf"]
    rmask, ident2, scr_p, vecs = K["rmask"], K["ident2"], K["scr_p"], K["vecs"]
    last = (t == ntile - 1)

    with Scope() as et:
      tanhT = cx.sb(et, [128, TT], BF16, "tanhT")
      ahT = cx.sb(et, [128, TT], BF16, "ahT")
      with Scope() as ex:
        xnT = cx.sb(ex, [128, NKC, TT + 1], BF16, "xnT")
        with Scope() as e0:
          if "0" in phases:
            cx.op("dve", lambda h: h.tensor_copy(out=xnT[:, :, 0:1], in_=prevcol[:]), [prevcol], [xnT])
            norm_transpose(cx, e0, job, lambda s, P: job.x[tok0 + s * 128:tok0 + s * 128 + P, :], None, V_G, xnT, 1,
                           vecs, identb, last_out=(job.shift_out if last else None), grow=A["grow"])
            cx.op("dve", lambda h: h.tensor_copy(out=prevcol[:], in_=xnT[:, :, TT:TT + 1]), [xnT], [prevcol])
          cx.end_scope(e0)
        with Scope() as e1:
            mixb = [cx.sb(e1, [128, NKC, TT], BF16, "mix") for _ in range(2)]
            tmpd = [cx.sb(e1, [128, TT], F32, "tmpd") for _ in range(3)]
            pp = [cx.ps(e1, [128, 2, 512], F32, "pp") for _ in range(3)]
            pl = cx.ps(e1, [128, 512], F32, "pl")
            ctr = [0]
            ncg = 0
            e1a = Scope()
            e1a.__enter__()
            w1sb = cx.sb(e1a, [128, NKC, 128], BF16, "w1sb")
            a1sb = cx.sb(e1a, [128, NKC, 128], BF16, "a1sb")
            cx.dma("pool", w1sb[:], A["a_w1"].rearrange("(kc p) n -> p kc n", p=128), w1sb, writes=[w1sb])
            cx.dma("pool", a1sb[:], A["a_a1"].rearrange("(kc p) n -> p kc n", p=128), a1sb, writes=[a1sb])
            ring = stg = None
            for j in ((4, 5, 0, 1, 2, 3) if "1" in phases else ()):
                mix = mixb[j % 2]
                if j == 0:
                    cx.end_scope(e1a)
                    e1a.__exit__(None, None, None)
                    ring = [cx.sb(e1, [128, 16, 256], BF16, "wr") for _ in range(4)]
                    stg = [cx.sb(e1, [128, 2, TT], F32, "stg") for _ in range(2)]
                for c in range(NKC):
                    td = tmpd[c % 3]
                    cx.op("pool", lambda h, c=c, td=td: h.tensor_tensor(out=td[:, :TT], in0=xnT[:, c, 0:TT], in1=xnT[:, c, 1:TT + 1], op=ALU.subtract), [xnT], [td])
                    cx.op("dve", lambda h, c=c, td=td, mix=mix, j=j: h.scalar_tensor_tensor(
                        out=mix[:, c, :], in0=td[:, :TT], scalar=vecs[:, V_MU + j, c:c + 1], in1=xnT[:, c, 1:TT + 1],
                        op0=ALU.mult, op1=ALU.add), [td, xnT, vecs], [mix])
                if j >= 4:
                    wsb = w1sb if j == 4 else a1sb
                    for kc in range(NKC):
                        cx.op("pe", lambda h, kc=kc, wsb=wsb, mix=mix: h.matmul(pl[:, :TT], lhsT=wsb[:, kc, :], rhs=mix[:, kc, :], start=(kc == 0), stop=(kc == NKC - 1)), [wsb, mix], [pl])
                    if j == 4:
                        cx.op("act", lambda h: h.activation(out=tanhT[:, :], in_=pl[:, :TT], func=AF.Tanh), [pl], [tanhT])
                    else:
                        cx.op("act", lambda h: h.activation(out=ahT[:, :], in_=pl[:, :TT], func=AF.Copy), [pl], [ahT])
                    continue
                for cg in range(16):
                    ppt = pp[ncg % 3]
                    sg = stg[ncg % 2]
                    ncg += 1
                    for k2 in range(2):
                        slot = stream_weights(cx, e1, A["a_w_in"], k2 * 2048, 2048, j * D + cg * 256, 256, ring, ctr)
                        for oc in range(2):
                            for kc in range(16):
                                kk = k2 * 16 + kc
                                cx.op("pe", lambda h, ppt=ppt, slot=slot, oc=oc, kc=kc, kk=kk, mix=mix: h.matmul(
                                    ppt[:, oc, :TT], lhsT=slot[:, kc, oc * 128:(oc + 1) * 128], rhs=mix[:, kk, :],
                                    start=(kk == 0), stop=(kk == NKC - 1)), [slot, mix], [ppt])
                    if j == 3:
                        cx.op("act", lambda h, ppt=ppt, sg=sg: h.activation(out=sg[:, :, :], in_=ppt[:, :, :TT], func=AF.Silu), [ppt], [sg])
                    elif cg % 2 == 0:
                        cx.op("act", lambda h, ppt=ppt, sg=sg: h.activation(out=sg[:, :, :], in_=ppt[:, :, :TT], func=AF.Copy), [ppt], [sg])
                    else:
                        cx.op("dve", lambda h, ppt=ppt, sg=sg: h.tensor_copy(out=sg[:, :, :], in_=ppt[:, :, :TT]), [ppt], [sg])
                    cx.dma("sp", scr_p[j, cg * 256:(cg + 1) * 256, 0:TT].rearrange("(o p) t -> p o t", p=128), sg[:, :, :], sg, reads=[sg], writes=[scr_p])
            if ring is None:
                cx.end_scope(e1a)
                e1a.__exit__(None, None, None)
            cx.end_scope(e1)
        cx.end_scope(ex)
      with Scope() as ey:
        oT = cx.sb(ey, [128, NKC, TT], BF16, "oT")
        with Scope() as e2:
            if "2" in phases:
                phase_a2(cx, e2, job, A, K, Sst, tanhT, ahT, oT)
            cx.end_scope(e2)
        with Scope() as e3:
            ring = [cx.sb(e3, [128, 8, 512], BF16, "wr3") for _ in range(4)]
            xres = [cx.sb(e3, [128, 512], F32, "xres") for _ in range(4)]
            hsb = [cx.sb(e3, [128, 512], F32, "hsb") for _ in range(4)]
            junk = cx.sb(e3, [128, 512], BF16, "junk3")
            ssq = cx.sb(e3, [128, 4, 8], F32, "ssq")
            sst = cx.sb(e3, [128, 4], F32, "sst")
            pp = [cx.ps(e3, [128, 4, 512], F32, "pp3") for _ in range(2)]
            ctr = [0]
            n = 0
            for cg in (range(8) if "3" in phases else ()):
                ppt = pp[cg % 2]
                for k4 in range(4):
                    slot = stream_weights(cx, e3, A["a_w_out"], k4 * 1024, 1024, cg * 512, 512, ring, ctr)
                    for s in range(nsub):
                        for kc in range(8):
                            kk = k4 * 8 + kc
                            cx.op("pe", lambda h, ppt=ppt, slot=slot, s=s, kc=kc, kk=kk: h.matmul(
                                ppt[:P, s, :], lhsT=oT[:, kk, s * 128:s * 128 + P], rhs=slot[:, kc, :],
                                start=(kk == 0), stop=(kk == NKC - 1)), [slot, oT], [ppt])
                for s in range(nsub):
                    xr, hb = xres[n % 4], hsb[n % 4]
                    n += 1
                    cx.dma("sp", xr[:P, :], job.x[tok0 + s * 128:tok0 + s * 128 + P, cg * 512:(cg + 1) * 512], xr, writes=[xr])
                    cx.op("dve", lambda h, ppt=ppt, s=s, xr=xr, hb=hb: h.tensor_tensor(out=hb[:P, :], in0=ppt[:P, s, :], in1=xr[:P, :], op=ALU.add), [ppt, xr], [hb])
                    cx.op("act", lambda h, hb=hb, s=s, cg=cg: h.activation(out=junk[:P, :], in_=hb[:P, :], func=AF.Square, accum_out=ssq[:P, s, cg:cg + 1]), [hb], [junk, ssq])
                    cx.dma("sp", job.h_scr[tok0 + s * 128:tok0 + s * 128 + P, cg * 512:(cg + 1) * 512], hb[:P, :], hb, reads=[hb], writes=[job.h_scr])
            cx.op("dve", lambda h: h.tensor_reduce(out=sst[:P, :nsub], in_=ssq[:P, :nsub, :], axis=AX.X, op=ALU.add), [ssq], [sst])
            rms_rstd(cx, (sst, sst[:P, :nsub]), (hrstd, hrstd[:P, t * 4:t * 4 + nsub]), P, 1.0 / D, 1e-6)
            cx.end_scope(e3)
        cx.end_scope(ey)
      with Scope() as ez:
        xnT = cx.sb(ez, [128, NKC, TT + 1], BF16, "hnTa")
        with Scope() as e4:
          if "4" in phases:
            norm_transpose(cx, e4, job, lambda s, P: job.h_scr[tok0 + s * 128:tok0 + s * 128 + P, :], None, V_KVG, xnT, 0,
                           vecs, identb, compute_rstd=False, hrstd=hrstd, t=t)
          cx.end_scope(e4)
        with Scope() as e5:
            ring = [cx.sb(e5, [128, 8, 512], BF16, "wr5") for _ in range(4)]
            pp = [cx.ps(e5, [128, 4, 512], F32, "pp5") for _ in range(2)]
            ksb = [cx.sb(e5, [128, 4, 128], F32, "ksb") for _ in range(3)]
            ksq = cx.sb(e5, [128, 4, 128], F32, "ksq")
            kss = [cx.sb(e5, [128, 4], F32, "kss") for _ in range(2)]
            gkb = K["gkb"]
            ctr = [0]
            n = 0
            for cg in (range(4) if "5" in phases else ()):
                ppt = pp[cg % 2]
                for k4 in range(4):
                    slot = stream_weights(cx, e5, A["w_kv"], k4 * 1024, 1024, cg * 512, 512, ring, ctr)
                    for s in range(nsub):
                        for kc in range(8):
                            kk = k4 * 8 + kc
                            cx.op("pe", lambda h, ppt=ppt, slot=slot, s=s, kc=kc, kk=kk: h.matmul(
                                ppt[:P, s, :], lhsT=xnT[:, kk, s * 128:s * 128 + P], rhs=slot[:, kc, :],
                                start=(kk == 0), stop=(kk == NKC - 1)), [slot, xnT], [ppt])
                for s in range(nsub):
                    kb = ksb[n % 3]
                    ks_ = kss[n % 2]
                    n += 1
                    rows = slice(tok0 + s * 128, tok0 + s * 128 + P)
                    if cg < 2:
                        cx.op("act", lambda h, ppt=ppt, s=s, kb=kb: h.activation(out=kb[:P].rearrange("p a b -> p (a b)"), in_=ppt[:P, s, :], func=AF.Copy), [ppt], [kb])
                        cx.op("dve", lambda h, kb=kb: h.tensor_tensor(out=ksq[:P], in0=kb[:P], in1=kb[:P], op=ALU.mult), [kb], [ksq])
                        cx.op("dve", lambda h, ks_=ks_: h.tensor_reduce(out=ks_[:P, :], in_=ksq[:P], axis=AX.X, op=ALU.add), [ksq], [ks_])
                        rms_rstd(cx, (ks_, ks_[:P, :]), (ks_, ks_[:P, :]), P, 1.0 / 128, 1e-6)
                        cx.op("dve", lambda h, kb=kb, ks_=ks_: h.tensor_tensor(out=kb[:P], in0=kb[:P], in1=ks_[:P, :].unsqueeze(2).to_broadcast([P, 4, 128]), op=ALU.mult), [kb, ks_], [kb])
                        cx.op("dve", lambda h, kb=kb: h.tensor_tensor(out=kb[:P], in0=kb[:P], in1=gkb[:P, :].unsqueeze(1).to_broadcast([P, 4, 128]), op=ALU.mult), [kb, gkb], [kb])
                        cx.dma("sp", job.k_out[rows, cg * 512:(cg + 1) * 512], kb[:P].rearrange("p a b -> p (a b)"), kb, reads=[kb], writes=[job.k_out])
                    else:
                        cx.op("act", lambda h, ppt=ppt, s=s, kb=kb: h.activation(out=kb[:P].rearrange("p a b -> p (a b)"), in_=ppt[:P, s, :], func=AF.Copy), [ppt], [kb])
                        cx.dma("sp", job.v_out[rows, (cg - 2) * 512:(cg - 1) * 512], kb[:P].rearrange("p a b -> p (a b)"), kb, reads=[kb], writes=[job.v_out])
            cx.end_scope(e5)
        cx.end_scope(ez)
      cx.end_scope(et)


def _rr(*gens):
    gens = list(gens)
    while gens:
        for g in list(gens):
            try:
                next(g)
            except StopIteration:
                gens.remove(g)


def phase_a2(cx, es, job, A, K, Sst, tanhT, ahT, oT):
    TT, C = job.TT, job.C
    nch = TT // C
    nst = {128: 6, 8: 2}[C]
    identb, identf, bones = K["identb"], K["identf"], K["bones"]
    rmask, ident2, scr_p, vecs = K["rmask"], K["ident2"], K["scr_p"], K["vecs"]
    w2sb = cx.sb(es, [128, D], BF16, "w2sb")
    a2sb = cx.sb(es, [128, D], BF16, "a2sb")
    cx.dma("pool", w2sb[:], A["a_w2"], w2sb, writes=[w2sb])
    cx.dma("pool", a2sb[:], A["a_a2"], a2sb, writes=[a2sb])
    rin = [cx.sb(es, [128, 4, TT], F32, "rin") for _ in range(2)]

    def f32t(name):
        return cx.sb(es, [128, TT], F32, name)

    def bft(name):
        return cx.sb(es, [128, TT], BF16, name)
    lwp, al, clp, egi, egm, kkr, rn, kk, t1, kp, tmp, mean, m2, var, yc = [f32t(n) for n in (
        "lwp", "al", "clp", "egi", "egm", "kkr", "rn", "kk", "t1", "kp", "tmp", "mean", "m2", "var", "yc")]
    sq, rkb, ysq, ysbb = [bft(n) for n in ("sq", "rkb", "ysq", "ysbb")]
    eg2 = [f32t("eg") for _ in range(2)]
    bon2 = [f32t("bon") for _ in range(2)]
    ysb2 = [f32t("ysb") for _ in range(2)]
    bt2 = [bft("bt") for _ in range(2)]
    kt2 = [bft("kt") for _ in range(2)]
    vbf2 = [bft("vbf") for _ in range(2)]
    at22 = [[bft("at0"), bft("at1")] for _ in range(2)]
    rt22 = [[bft("rt0"), bft("rt1")] for _ in range(2)]
    for par_ in range(2):
        for z in at22[par_] + rt22[par_]:
            cx.op("dve", lambda h, z=z: h.memset(z[:], 0.0), [], [z])
    onesT = f32t("onesT")
    cx.op("dve", lambda h: h.memset(onesT[:], 1.0), [], [onesT])
    tokm = [cx.sb(es, [128, 3, 128], BF16, "tokm") for _ in range(2)]
    UA = [cx.sb(es, [128, 4, 128], BF16, "UA") for _ in range(2)]
    UB = [cx.sb(es, [128, 4, 128], BF16, "UB") for _ in range(2)]
    NTs = [cx.sb(es, [128, 2, 128], BF16, "NTs") for _ in range(2)]
    PPs = [[cx.sb(es, [128, 4, 128], BF16, "PP") for _ in range(3)] for _ in range(2)]
    RRs = [[cx.sb(es, [128, 4, 128], BF16, "RR") for _ in range(3)] for _ in range(2)]
    Sbf = cx.sb(es, [128, 64], BF16, "Sbf")
    Zb = cx.sb(es, [128, 2, 64], BF16, "Zb")
    Ub = cx.sb(es, [128, 2, 64], BF16, "Ub")
    tmpS = cx.sb(es, [128, 64], F32, "tmpS")
    pA = cx.ps(es, [128, 4, 128], F32, "pA")
    pPPs = [cx.ps(es, [128, 4, 128], F32, "pPP") for _ in range(2)]
    pRRs = [cx.ps(es, [128, 4, 128], F32, "pRR") for _ in range(2)]
    pX0, pX1 = pPPs[1], pRRs[1]
    pXv = [pX0.h[:].rearrange("p a b -> p (a b)"), pX1.h[:].rearrange("p a b -> p (a b)")]
    pm1 = cx.ps(es, [128, 512], F32, "pm1")
    pC = pZ = pU = pm1
    pm2 = cx.ps(es, [128, 512], F32, "pm2")
    pY = pS = pm2
    ptr = cx.ps(es, [128, 3, 128], BF16, "ptr")

    def V(idx, hp):
        return vecs[:, idx, hp:hp + 1]

    def make(hp):
        par = hp % 2
        r_in = rin[par]
        eg, bon, ysb, bt, kt, vbf, at2, rt2 = eg2[par], bon2[par], ysb2[par], bt2[par], kt2[par], vbf2[par], at22[par], rt22[par]

        def S1():
            if True:
                pass
                cx.dma("sp", r_in[:, :, :], scr_p[:, hp * 128:(hp + 1) * 128, 0:TT].rearrange("j p t -> p j t"), r_in, reads=[scr_p], writes=[r_in])
                r_ap, k_ap, v_ap, g_ap = r_in[:, 0, :], r_in[:, 1, :], r_in[:, 2, :], r_in[:, 3, :]
                fs = slice(hp * 128, (hp + 1) * 128)
                cx.op("pe", lambda h, fs=fs: h.matmul(pXv[0][:, :TT], lhsT=w2sb[:, fs], rhs=tanhT[:, :], start=True, stop=True), [w2sb, tanhT], [pX0])
                cx.op("pe", lambda h, fs=fs: h.matmul(pXv[1][:, :TT], lhsT=a2sb[:, fs], rhs=ahT[:, :], start=True, stop=True), [a2sb, ahT], [pX1])
                cx.op("act", lambda h, hp=hp: h.activation(out=lwp[:], in_=pXv[0][:, :TT], func=AF.Sigmoid, bias=V(V_W0, hp)), [pX0, vecs], [lwp])
                yield
                cx.op("act", lambda h, hp=hp: h.activation(out=al[:], in_=pXv[1][:, :TT], func=AF.Sigmoid, bias=V(V_A0, hp)), [pX1, vecs], [al])
                for ch in range(nch):
                    cs = slice(ch * C, (ch + 1) * C)
                    cx.op("dve", lambda h, cs=cs: h.tensor_tensor_scan(out=clp[:, cs], data0=onesT[:, cs], data1=lwp[:, cs], initial=0.0, op0=ALU.mult, op1=ALU.add), [onesT, lwp], [clp])
                cx.op("act", lambda h: h.activation(out=eg[:], in_=clp[:], func=AF.Exp, scale=-C0), [clp], [eg])
                yield
                cx.op("act", lambda h: h.activation(out=egi[:], in_=clp[:], func=AF.Exp, scale=C0), [clp], [egi])
                cx.op("dve", lambda h: h.tensor_tensor(out=tmp[:], in0=clp[:], in1=lwp[:], op=ALU.subtract), [clp, lwp], [tmp])
                cx.op("act", lambda h: h.activation(out=egm[:], in_=tmp[:], func=AF.Exp, scale=-C0), [tmp], [egm])
                yield
                cx.op("dve", lambda h, hp=hp, k_ap=k_ap: h.tensor_scalar(out=kkr[:], in0=k_ap, scalar1=V(V_KK, hp), scalar2=None, op0=ALU.mult), [r_in, vecs], [kkr])
                cx.op("act", lambda h: h.activation(out=sq[:], in_=kkr[:], func=AF.Square), [kkr], [sq])
                cx.op("pe", lambda h: h.matmul(pXv[0][:, :TT], lhsT=bones[:], rhs=sq[:], start=True, stop=True), [bones, sq], [pX0])
                yield
                cx.op("dve", lambda h: h.tensor_scalar(out=rn[:], in0=pXv[0][:, :TT], scalar1=1e-24, scalar2=None, op0=ALU.max), [pX0], [rn])
                cx.op("act", lambda h: h.activation(out=rn[:], in_=rn[:], func=AF.Ln), [rn], [rn])
                cx.op("act", lambda h: h.activation(out=rn[:], in_=rn[:], func=AF.Exp, scale=-0.5), [rn], [rn])
                yield
                cx.op("dve", lambda h: h.tensor_tensor(out=kk[:], in0=kkr[:], in1=rn[:], op=ALU.mult), [kkr, rn], [kk])
                cx.op("dve", lambda h, hp=hp: h.tensor_scalar(out=t1[:], in0=al[:], scalar1=-1.0, scalar2=V(V_KA, hp), op0=ALU.add, op1=ALU.mult), [al, vecs], [t1])
                cx.op("dve", lambda h, k_ap=k_ap: h.scalar_tensor_tensor(out=kp[:], in0=t1[:], scalar=1.0, in1=k_ap, op0=ALU.add, op1=ALU.mult), [t1, r_in], [kp])
                yield
                for e in range(2):
                    po = slice(64 * e, 64 * e + 64)
                    cx.op("dve", lambda h, e=e, po=po: h.scalar_tensor_tensor(out=at2[e][po, :], in0=kk[po, :], scalar=-1.0, in1=egm[po, :], op0=ALU.mult, op1=ALU.mult), [kk, egm], [at2[e]])
                cx.op("pool", lambda h: h.tensor_tensor(out=tmp[:], in0=kk[:], in1=al[:], op=ALU.mult), [kk, al], [tmp])
                cx.op("pool", lambda h: h.tensor_tensor(out=bt[:], in0=tmp[:], in1=egi[:], op=ALU.mult), [tmp, egi], [bt])
                yield
                cx.op("pool", lambda h: h.tensor_tensor(out=kt[:], in0=kp[:], in1=egi[:], op=ALU.mult), [kp, egi], [kt])
                for e in range(2):
                    po = slice(64 * e, 64 * e + 64)
                    cx.op("pool", lambda h, e=e, po=po, r_in=r_in: h.tensor_tensor(out=rt2[e][po, :], in0=r_in[po, 0, :], in1=eg[po, :], op=ALU.mult), [r_in, eg], [rt2[e]])
                cx.op("dve", lambda h, hp=hp, r_ap=r_ap: h.scalar_tensor_tensor(out=rkb[:], in0=r_ap, scalar=V(V_RK, hp), in1=kp[:], op0=ALU.mult, op1=ALU.mult), [r_in, vecs, kp], [rkb])
                yield
                cx.op("pe", lambda h: h.matmul(pXv[1][:, :TT], lhsT=bones[:], rhs=rkb[:], start=True, stop=True), [bones, rkb], [pX1])
                cx.op("dve", lambda h, v_ap=v_ap: h.tensor_tensor(out=bon[:], in0=pXv[1][:, :TT], in1=v_ap, op=ALU.mult), [pX1, r_in], [bon])
                cx.op("act", lambda h, v_ap=v_ap: h.activation(out=vbf[:], in_=v_ap, func=AF.Copy), [r_in], [vbf])
                yield
            yield

        if True:
            def s2(ch, idx):
                cs = slice(ch * C, (ch + 1) * C)
                tk = tokm[idx]
                ua, ub, nts = UA[idx], UB[idx], NTs[idx]
                for i, srcT in enumerate((vbf, bt, kt)):
                    cx.op("pe", lambda h, i=i, srcT=srcT, cs=cs: h.transpose(out=ptr[:C, i, :], in_=srcT[:, cs], identity=identb[:]), [srcT, identb], [ptr])
                cx.op("act", lambda h, tk=tk: h.activation(out=tk[:C], in_=ptr[:C], func=AF.Copy), [ptr], [tk])
                for e in range(2):
                    cx.op("pe", lambda h, e=e, cs=cs: h.matmul(pA[:C, e, :C], lhsT=bt[:, cs], rhs=at2[e][:, cs], start=True, stop=True), [bt, at2[e]], [pA])
                    cx.op("pe", lambda h, e=e, cs=cs: h.matmul(pA[:C, 2 + e, :C], lhsT=kt[:, cs], rhs=at2[e][:, cs], start=True, stop=True), [kt, at2[e]], [pA])
                cx.op("dve", lambda h, ua=ua: h.tensor_tensor(out=ua[:C, :, :C], in0=pA[:C, :, :C], in1=rmask[:C, 0, :, :C], op=ALU.mult), [pA, rmask], [ua])
                for e in range(2):
                    cx.op("pe", lambda h, e=e, cs=cs: h.matmul(pA[:C, e, :C], lhsT=bt[:, cs], rhs=rt2[e][:, cs], start=True, stop=True), [bt, rt2[e]], [pA])
                    cx.op("pe", lambda h, e=e, cs=cs: h.matmul(pA[:C, 2 + e, :C], lhsT=kt[:, cs], rhs=rt2[e][:, cs], start=True, stop=True), [kt, rt2[e]], [pA])
                cx.op("dve", lambda h, ub=ub: h.tensor_tensor(out=ub[:C, :, :C], in0=pA[:C, :, :C], in1=rmask[:C, 1, :, :C], op=ALU.mult), [pA, rmask], [ub])
                for e in range(2):
                    cx.op("pe", lambda h, e=e, cs=cs: h.matmul(pm1[:C, e * 128:e * 128 + C], lhsT=at2[e][:, cs], rhs=bt[:, cs], start=True, stop=True), [at2[e], bt], [pC])
                cx.op("dve", lambda h, nts=nts: h.tensor_tensor(out=nts[:C, :, :C], in0=pm1[:C, 0:256].rearrange("p (e c) -> p e c", e=2)[:, :, :C], in1=rmask[:C, 2, 0:2, :C], op=ALU.mult), [pC, rmask], [nts])

            def s3(idx, st, res):
                ua, nts = UA[idx], NTs[idx]
                pPP, pRR, PP, RR = pPPs[0], pRRs[0], PPs[st], RRs[st]
                rr = RR[0]
                cx.op("pool", lambda h: h.tensor_tensor(out=rr[:C, 0:2, :C], in0=ua[:C, 0:2, :C], in1=ident2[:C, 0:2, :C], op=ALU.add), [ua, ident2], [rr])
                cx.op("pool", lambda h: h.tensor_tensor(out=rr[:C, 2:4, :C], in0=nts[:C, :, :C], in1=ident2[:C, 0:2, :C], op=ALU.add), [nts, ident2], [rr])
                ppc = PP[0]
                cx.op("act", lambda h: h.activation(out=ppc[:C, 0:2, :C], in_=ua[:C, 0:2, :C], func=AF.Copy), [ua], [ppc])
                cx.op("act", lambda h: h.activation(out=ppc[:C, 2:4, :C], in_=nts[:C, :, :C], func=AF.Copy), [nts], [ppc])
                yield
                for i in range(1, nst + 1):
                    lastst = (i == nst)
                    ppn = PP[i % 3]
                    rrn = RR[i % 3]
                    for e in range(2):
                        cx.op("pe", lambda h, e=e: h.matmul(pPP[:C, e, :C], lhsT=ppc[:C, 2 + e, :C], rhs=ppc[:C, e, :C], start=True, stop=True), [ppc], [pPP])
                        if not lastst:
                            cx.op("pe", lambda h, e=e: h.matmul(pPP[:C, 2 + e, :C], lhsT=ppc[:C, e, :C], rhs=ppc[:C, 2 + e, :C], start=True, stop=True), [ppc], [pPP])
                    nq = 2 if lastst else 4
                    cx.op("act", lambda h: h.activation(out=ppn[:C, 0:nq, :C], in_=pPP[:C, 0:nq, :C], func=AF.Copy), [pPP], [ppn])
                    yield
                    for e in range(2):
                        cx.op("pe", lambda h, e=e: h.matmul(pRR[:C, e, :C], lhsT=rr[:C, 2 + e, :C], rhs=ppn[:C, e, :C], start=True, stop=True), [rr, ppn], [pRR])
                        if not lastst:
                            cx.op("pe", lambda h, e=e: h.matmul(pRR[:C, 2 + e, :C], lhsT=ppn[:C, e, :C], rhs=rr[:C, 2 + e, :C], start=True, stop=True), [rr, ppn], [pRR])
                    cx.op("dve", lambda h: h.tensor_tensor(out=rrn[:C, 0:nq, :C], in0=pRR[:C, 0:nq, :C], in1=rr[:C, 0:nq, :C], op=ALU.add), [pRR, rr], [rrn])
                    ppc, rr = ppn, rrn
                    yield
                res.append(rr)

            def s4(ch, idx, rr):
                cs = slice(ch * C, (ch + 1) * C)
                tk = tokm[idx]
                ua, ub = UA[idx], UB[idx]
                cx.op("act", lambda h, hp=hp: h.activation(out=Sbf[:], in_=Sst[:, hp, :], func=AF.Copy), [Sst], [Sbf])
                for e in range(2):
                    po = slice(64 * e, 64 * e + 64)
                    zc = slice(256 + e * 64, 256 + e * 64 + 64)
                    cx.op("pe", lambda h, e=e, po=po, zc=zc, cs=cs: h.matmul(pm1[:C, zc], lhsT=at2[e][:, cs], rhs=Sbf[:, :], start=True, stop=False), [at2[e], Sbf], [pZ])
                    cx.op("pe", lambda h, e=e, po=po, zc=zc, ua=ua, tk=tk: h.matmul(pm1[:C, zc], lhsT=ua[:C, 2 + e, :C], rhs=tk[:C, 0, po], start=False, stop=True), [ua, tk], [pZ])
                cx.op("act", lambda h: h.activation(out=Zb[:C].rearrange("p e v -> p (e v)"), in_=pm1[:C, 256:384], func=AF.Copy), [pZ], [Zb])
                yield
                for e in range(2):
                    uc = slice(384 + e * 64, 384 + e * 64 + 64)
                    cx.op("pe", lambda h, e=e, uc=uc, rr=rr: h.matmul(pm1[:C, uc], lhsT=rr[:C, e, :C], rhs=Zb[:C, e, :], start=True, stop=True), [rr, Zb], [pU])
                cx.op("dve", lambda h: h.tensor_copy(out=Ub[:C].rearrange("p e v -> p (e v)"), in_=pm1[:C, 384:512]), [pU], [Ub])
                yield
                for e in range(2):
                    po = slice(64 * e, 64 * e + 64)
                    cx.op("pe", lambda h, e=e, po=po, cs=cs: h.matmul(pm2[po, 0:C], lhsT=Sbf[:, :], rhs=rt2[e][:, cs], start=True, stop=False), [Sbf, rt2[e]], [pY])
                    cx.op("pe", lambda h, e=e, po=po, ub=ub: h.matmul(pm2[po, 0:C], lhsT=Ub[:C, e, :], rhs=ub[:C, e, :C], start=False, stop=False), [Ub, ub], [pY])
                    cx.op("pe", lambda h, e=e, po=po, ub=ub, tk=tk: h.matmul(pm2[po, 0:C], lhsT=tk[:C, 0, po], rhs=ub[:C, 2 + e, :C], start=False, stop=True), [tk, ub], [pY])
                cx.op("act", lambda h, cs=cs: h.activation(out=ysb[:, cs], in_=pm2[:, 0:C], func=AF.Copy), [pY], [ysb])
                for e in range(2):
                    po = slice(64 * e, 64 * e + 64)
                    cx.op("pe", lambda h, e=e, po=po, tk=tk: h.matmul(pm2[po, 128:192], lhsT=tk[:C, 1, po], rhs=Ub[:C, e, :], start=True, stop=False), [tk, Ub], [pS])
                    cx.op("pe", lambda h, e=e, po=po, tk=tk: h.matmul(pm2[po, 128:192], lhsT=tk[:C, 2, po], rhs=tk[:C, 0, po], start=False, stop=True), [tk], [pS])
                cx.op("dve", lambda h, hp=hp: h.tensor_tensor(out=tmpS[:], in0=pm2[:, 128:192], in1=Sst[:, hp, :], op=ALU.add), [pS, Sst], [tmpS])
                gcol = (ch + 1) * C - 1
                cx.op("dve", lambda h, hp=hp, gcol=gcol: h.tensor_scalar(out=Sst[:, hp, :], in0=tmpS[:], scalar1=eg[:, gcol:gcol + 1], scalar2=None, op0=ALU.mult), [tmpS, eg], [Sst])


        def chunks():
            for c0 in range(0, nch, 2):
                chs = [c for c in (c0, c0 + 1) if c < nch]
                for idx, ch in enumerate(chs):
                    s2(ch, idx)
                    yield
                res = [[] for _ in chs]
                for idx in range(len(chs)):
                    yield from s3(idx, idx, res[idx])
                for idx, ch in enumerate(chs):
                    yield from s4(ch, idx, res[idx][0])
                    yield

        def S5():
            g_ap = r_in[:, 3, :]
            if True:
                cx.op("act", lambda h: h.activation(out=ysq[:], in_=ysb[:], func=AF.Square), [ysb], [ysq])
                cx.op("act", lambda h: h.activation(out=ysbb[:], in_=ysb[:], func=AF.Copy), [ysb], [ysbb])
                cx.op("pe", lambda h: h.matmul(pXv[0][:, :TT], lhsT=bones[:], rhs=ysbb[:], start=True, stop=True), [bones, ysbb], [pX0])
                yield
                cx.op("pe", lambda h: h.matmul(pXv[1][:, :TT], lhsT=bones[:], rhs=ysq[:], start=True, stop=True), [bones, ysq], [pX1])
                cx.op("dve", lambda h: h.tensor_scalar(out=mean[:], in0=pXv[0][:, :TT], scalar1=1.0 / 64, scalar2=None, op0=ALU.mult), [pX0], [mean])
                cx.op("dve", lambda h: h.tensor_tensor(out=m2[:], in0=mean[:], in1=mean[:], op=ALU.mult), [mean], [m2])
                yield
                cx.op("dve", lambda h: h.scalar_tensor_tensor(out=var[:], in0=pXv[1][:, :TT], scalar=1.0 / 64, in1=m2[:], op0=ALU.mult, op1=ALU.subtract), [pX1, m2], [var])
                cx.op("dve", lambda h: h.tensor_scalar(out=var[:], in0=var[:], scalar1=64e-5, scalar2=None, op0=ALU.add), [var], [var])
                cx.op("act", lambda h: h.activation(out=var[:], in_=var[:], func=AF.Ln), [var], [var])
                yield
                cx.op("act", lambda h: h.activation(out=var[:], in_=var[:], func=AF.Exp, scale=-0.5), [var], [var])
                cx.op("pool", lambda h: h.tensor_tensor(out=yc[:], in0=ysb[:], in1=mean[:], op=ALU.subtract), [ysb, mean], [yc])
                cx.op("dve", lambda h: h.tensor_tensor(out=yc[:], in0=yc[:], in1=var[:], op=ALU.mult), [yc, var], [yc])
                yield
                cx.op("act", lambda h, hp=hp: h.activation(out=yc[:], in_=yc[:], func=AF.Identity, scale=V(V_LG, hp), bias=V(V_LB, hp)), [yc, vecs], [yc])
                cx.op("pool", lambda h: h.tensor_tensor(out=yc[:], in0=yc[:], in1=bon[:], op=ALU.add), [yc, bon], [yc])
                cx.op("dve", lambda h, hp=hp, g_ap=g_ap: h.tensor_tensor(out=oT[:, hp, :], in0=yc[:], in1=g_ap, op=ALU.mult), [yc, r_in], [oT])
                yield
            yield
        return S1, chunks, S5

    gens = [make(hp) for hp in range(NKC)]
    _rr(gens[0][0]())
    for hp in range(NKC):
        def tail(hp=hp):
            if hp >= 1:
                yield from gens[hp - 1][2]()
            if hp + 1 < NKC:
                yield from gens[hp + 1][0]()
        _rr(gens[hp][1](), tail())
    _rr(gens[NKC - 1][2]())


def layer_b(cx, nc, job, A, O, vec, K, hrstd, phases):
    S, TT = job.S, job.TT
    ntile = S // TT
    nsub = (TT + 127) // 128
    P = min(TT, 128)
    identb, onesb, onesf = K["identb"], K["onesb"], K["onesf"]
    amask, gqs, vecs = K["amask"], K["gqs"], K["vecs"]
    nkt_cache = job.ncache
    nkt_own = (S + 127) // 128
    nkt = nkt_cache + nkt_own
    ktscr = K["ktscr"]
    with Scope() as eb:
        with Scope() as e0:
            kst = [cx.sb(e0, [128, 1024], BF16, "kst") for _ in range(2)]
            kts = [cx.sb(e0, [128, 8, 128], BF16, "kts") for _ in range(2)]
            ptk = [cx.ps(e0, [128, 8, 128], BF16, "ptk") for _ in range(2)]
            for kt in (range(nkt) if "6" in phases else ()):
                if kt < nkt_cache:
                    ksrc, rows, pr = A["ck"], slice(kt * 128, kt * 128 + 128), 128
                else:
                    o = kt - nkt_cache
                    pr = min(128, S - o * 128)
                    ksrc, rows = job.k_out.h, slice(o * 128, o * 128 + pr)
                ks_ = kst[kt % 2]
                pk = ptk[kt % 2]
                kb = kts[kt % 2]
                cx.dma("pool", ks_[:pr, :], ksrc[rows, :], ks_, writes=[ks_])
                for hd in range(8):
                    cx.op("pe", lambda h, hd=hd, ks_=ks_, pk=pk, pr=pr: h.transpose(out=pk[:, hd, :pr], in_=ks_[:pr, hd * 128:(hd + 1) * 128], identity=identb[:pr, :pr]), [ks_, identb], [pk])
                if kt % 2 == 0:
                    cx.op("dve", lambda h, pk=pk, kb=kb, pr=pr: h.tensor_copy(out=kb[:, :, :pr], in_=pk[:, :, :pr]), [pk], [kb])
                else:
                    cx.op("act", lambda h, pk=pk, kb=kb, pr=pr: h.activation(out=kb[:, :, :pr], in_=pk[:, :, :pr], func=AF.Copy), [pk], [kb])
                cx.dma("sp", ktscr[:, :, kt * 128:kt * 128 + pr].rearrange("h c t -> c h t"), kb[:, :, :pr], kb, reads=[kb], writes=[ktscr])
            cx.end_scope(e0)
        for t in range(ntile):
            tok0 = t * TT
            with Scope() as et:
                hnT = cx.sb(et, [128, NKC, TT], BF16, "hnT")
                oT = cx.sb(et, [128, NKC, TT], BF16, "oTb")
                with Scope() as e1:
                    if "7" in phases:
                        norm_transpose(cx, e1, job, lambda s, P: job.h_scr[tok0 + s * 128:tok0 + s * 128 + P, :], None, V_BG, hnT, 0,
                                       vecs, identb, compute_rstd=False, hrstd=hrstd, t=t)
                    cx.end_scope(e1)
                with Scope() as e2:
                    ring = [cx.sb(e2, [128, 8, 512], BF16, "wrb") for _ in range(3)]
                    KTh = [cx.sb(e2, [128, nkt * 128], BF16, "KTh") for _ in range(2)]
                    Vh = [cx.sb(e2, [128, nkt, 128], BF16, "Vh") for _ in range(2)]
                    qT = cx.sb(e2, [128, 12, TT], BF16, "qT")
                    sgT = cx.sb(e2, [128, 4, TT], BF16, "sgT")
                    qf = [cx.sb(e2, [128, TT], F32, "qf") for _ in range(4)]
                    sqf = [cx.sb(e2, [128, TT], BF16, "sqf") for _ in range(4)]
                    rq = [cx.sb(e2, [128, TT], F32, "rq") for _ in range(2)]
                    pe_ = [cx.sb(e2, [128, 4, P], BF16, "pexp") for _ in range(4)]
                    rden = cx.sb(e2, [128, 4, P], F32, "rden")
                    of = cx.sb(e2, [128, 4, P], F32, "of")
                    ppq = cx.ps(e2, [128, 4, 512], F32, "ppq")
                    pw = [cx.ps(e2, [128, 512], F32, "pw") for _ in range(2)]
                    pnum = cx.ps(e2, [128, 512], F32, "pnum")
                    pden = cx.ps(e2, [128, 512], F32, "pden")
                    ctr = [0]
                    nw = 0
                    nq = 0
                    npe = 0
                    nkeys = nkt_cache * 128 + S
                    for kvh in (range(8) if "8" in phases else ()):
                        kth, vh = KTh[kvh % 2], Vh[kvh % 2]
                        cx.dma("sp", kth[:, :nkeys], ktscr[kvh, :, :nkeys], kth, reads=[ktscr], writes=[kth])
                        if nkt_cache:
                            cx.dma("pool", vh[:, :nkt_cache, :], A["cv"].rearrange("(kt p) (h c) -> p kt h c", p=128, c=128)[:, :, kvh, :], vh, writes=[vh])
                        if S >= 128:
                            cx.dma("pool", vh[:, nkt_cache:, :], job.v_out.h.rearrange("(kt p) (h c) -> p kt h c", p=128, c=128)[:, :, kvh, :], vh, writes=[vh])
                        else:
                            cx.dma("pool", vh[:S, nkt_cache, :], job.v_out.h[:, kvh * 128:(kvh + 1) * 128], vh, writes=[vh])
                        for g in range(4):
                            col0 = (g * D + kvh * 512) if g < 3 else (3 * D + kvh * 512)
                            for k4 in range(4):
                                slot = stream_weights(cx, e2, A["b_w_in"], k4 * 1024, 1024, col0, 512, ring, ctr)
                                for oc in range(4):
                                    for kc in range(8):
                                        kk = k4 * 8 + kc
                                        cx.op("pe", lambda h, slot=slot, oc=oc, kc=kc, kk=kk: h.matmul(
                                            ppq[:, oc, :TT], lhsT=slot[:, kc, oc * 128:(oc + 1) * 128], rhs=hnT[:, kk, :],
                                            start=(kk == 0), stop=(kk == NKC - 1)), [slot, hnT], [ppq])
                            if g == 3:
                                cx.op("act", lambda h: h.activation(out=sgT[:, :, :], in_=ppq[:, :, :TT], func=AF.Silu), [ppq], [sgT])
                                continue
                            for oc in range(4):
                                q_, s_ = qf[oc], sqf[oc]
                                cx.op("act", lambda h, oc=oc, q_=q_: h.activation(out=q_[:, :], in_=ppq[:, oc, :TT], func=AF.Copy), [ppq], [q_])
                                cx.op("act", lambda h, oc=oc, s_=s_: h.activation(out=s_[:, :], in_=ppq[:, oc, :TT], func=AF.Square), [ppq], [s_])
                            for oc in range(4):
                                q_, s_, r_ = qf[oc], sqf[oc], rq[nq % 2]
                                nq += 1
                                pwt = pw[nw % 2]
                                nw += 1
                                cx.op("pe", lambda h, s_=s_, pwt=pwt: h.matmul(pwt[:, :TT], lhsT=onesb[:], rhs=s_[:, :], start=True, stop=True), [onesb, s_], [pwt])
                                cx.op("dve", lambda h, r_=r_, pwt=pwt: h.tensor_scalar(out=r_[:, :], in0=pwt[:, :TT], scalar1=1.0 / 128, scalar2=1e-6, op0=ALU.mult, op1=ALU.add), [pwt], [r_])
                                cx.op("act", lambda h, r_=r_: h.activation(out=r_[:, :], in_=r_[:, :], func=AF.Ln), [r_], [r_])
                                cx.op("act", lambda h, r_=r_: h.activation(out=r_[:, :], in_=r_[:, :], func=AF.Exp, scale=-0.5), [r_], [r_])
                                cx.op("dve", lambda h, q_=q_, r_=r_, g=g, oc=oc: h.scalar_tensor_tensor(out=qT[:, g * 4 + oc, :], in0=q_[:, :], scalar=gqs[:, g:g + 1], in1=r_[:, :], op0=ALU.mult, op1=ALU.mult), [q_, r_, gqs], [qT])
                        for s in range(nsub):
                            qa = nkt_cache + t * max(TT // 128, 1) + s
                            units = []
                            for g in range(3):
                                for dl in range(NDELTA[g]):
                                    ktile = qa - dl
                                    if ktile >= 0:
                                        units.append((g, dl, ktile))
                            qs = slice(s * 128, s * 128 + P)
                            for ui, (g, dl, ktile) in enumerate(units):
                                kw = 128
                                if ktile >= nkt_cache:
                                    kw = min(128, S - (ktile - nkt_cache) * 128)
                                pwt = pw[nw % 2]
                                nw += 1
                                pb = pe_[npe % 4]
                                npe += 1
                                mi = MOFF[g] + dl
                                cx.op("pe", lambda h, pwt=pwt, ktile=ktile, kw=kw, g=g, qs=qs, kth=kth: h.matmul(
                                    pwt[:kw, 0:4 * P].rearrange("p (a b) -> p a b", a=4), lhsT=kth[:, ktile * 128:ktile * 128 + kw],
                                    rhs=qT[:, g * 4:g * 4 + 4, qs], start=True, stop=True), [kth, qT], [pwt])
                                cx.op("act", lambda h, pwt=pwt, pb=pb, kw=kw: h.activation(out=pb[:kw].rearrange("p a b -> p (a b)"), in_=pwt[:kw, 0:4 * P], func=AF.Exp), [pwt], [pb])
                                meng = "dve"
                                cx.op(meng, lambda h, pb=pb, kw=kw, mi=mi: h.tensor_tensor(out=pb[:kw], in0=pb[:kw], in1=amask[:kw, mi, :P].unsqueeze(1).to_broadcast([kw, 4, P]), op=ALU.mult), [pb, amask], [pb])
                                first, lastu = (ui == 0), (ui == len(units) - 1)
                                cx.op("pe", lambda h, pb=pb, kw=kw, ktile=ktile, vh=vh, first=first, lastu=lastu: h.matmul(
                                    pnum[:, 0:4 * P], lhsT=vh[:kw, ktile, :], rhs=pb[:kw].rearrange("p a b -> p (a b)"),
                                    start=first, stop=lastu), [vh, pb], [pnum])
                                cx.op("pe", lambda h, pb=pb, kw=kw, first=first, lastu=lastu: h.matmul(
                                    pden[:, 0:4 * P], lhsT=onesb[:kw, :], rhs=pb[:kw].rearrange("p a b -> p (a b)"),
                                    start=first, stop=lastu), [onesb, pb], [pden])
                            cx.op("act", lambda h: h.activation(out=rden[:].rearrange("p a b -> p (a b)"), in_=pden[:, 0:4 * P], func=AF.Ln), [pden], [rden])
                            cx.op("act", lambda h: h.activation(out=rden[:].rearrange("p a b -> p (a b)"), in_=rden[:].rearrange("p a b -> p (a b)"), func=AF.Exp, scale=-1.0), [rden], [rden])
                            cx.op("dve", lambda h: h.tensor_tensor(out=of[:].rearrange("p a b -> p (a b)"), in0=pnum[:, 0:4 * P], in1=rden[:].rearrange("p a b -> p (a b)"), op=ALU.mult), [pnum, rden], [of])
                            cx.op("pool", lambda h, kvh=kvh, qs=qs: h.tensor_tensor(out=oT[:, kvh * 4:kvh * 4 + 4, qs], in0=of[:], in1=sgT[:, :, qs], op=ALU.mult), [of, sgT], [oT])
                    cx.end_scope(e2)
                with Scope() as e3:
                    ring = [cx.sb(e3, [128, 8, 512], BF16, "wr7") for _ in range(4)]
                    xres = [cx.sb(e3, [128, 512], F32, "hres") for _ in range(4)]
                    hsb = [cx.sb(e3, [128, 512], F32, "ysb") for _ in range(4)]
                    pp = [cx.ps(e3, [128, 4, 512], F32, "pp7") for _ in range(2)]
                    ctr = [0]
                    n = 0
                    for cg in (range(8) if "9" in phases else ()):
                        ppt = pp[cg % 2]
                        for k4 in range(4):
                            slot = stream_weights(cx, e3, A["b_w_out"], k4 * 1024, 1024, cg * 512, 512, ring, ctr)
                            for s in range(nsub):
                                for kc in range(8):
                                    kk = k4 * 8 + kc
                                    cx.op("pe", lambda h, ppt=ppt, slot=slot, s=s, kc=kc, kk=kk: h.matmul(
                                        ppt[:P, s, :], lhsT=oT[:, kk, s * 128:s * 128 + P], rhs=slot[:, kc, :],
                                        start=(kk == 0), stop=(kk == NKC - 1)), [slot, oT], [ppt])
                        for s in range(nsub):
                            xr, hb = xres[n % 4], hsb[n % 4]
                            n += 1
                            rows = slice(tok0 + s * 128, tok0 + s * 128 + P)
                            cx.dma("sp", xr[:P, :], job.h_scr[rows, cg * 512:(cg + 1) * 512], xr, reads=[job.h_scr], writes=[xr])
                            cx.op("dve", lambda h, ppt=ppt, s=s, xr=xr, hb=hb: h.tensor_tensor(out=hb[:P, :], in0=ppt[:P, s, :], in1=xr[:P, :], op=ALU.add), [ppt, xr], [hb])
                            cx.dma("sp", job.y[rows, cg * 512:(cg + 1) * 512], hb[:P, :], hb, reads=[hb], writes=[job.y])
                    cx.end_scope(e3)
                cx.end_scope(et)
        cx.end_scope(eb)


def _consts():
    bf = ml_dtypes.bfloat16
    identf = np.eye(128, dtype=np.float32)
    bones = np.zeros((128, 128), np.float32)
    bones[:64, :64] = 1
    bones[64:, 64:] = 1
    i = np.arange(128)
    su = (i[:, None] < i[None, :]).astype(np.float32)
    iu = (i[:, None] <= i[None, :]).astype(np.float32)
    sl = (i[None, :] < i[:, None]).astype(np.float32)
    rmask = np.stack([np.stack([m] * 4, 1) for m in (su, iu, sl)], 1)
    am = np.zeros((128, 24, 128), np.float32)
    for g in range(3):
        for dl in range(NDELTA[g]):
            d = 128 * dl + i[None, :] - i[:, None]
            ok = (d >= 0) & (d % DILS[g] == 0) & (d // DILS[g] <= 128)
            am[:, MOFF[g] + dl, :] = ok
    return dict(identf=identf, bones=bones, rmask=np.ascontiguousarray(rmask), amask=am)


def _fm(v):
    return np.ascontiguousarray(np.asarray(v, np.float32).reshape(NKC, 128).T)


_CACHE = {}


def kernel(x_prompt, x_sample, state_wkv, state_shift, cache_k, cache_v,
           a_norm_g, a_mu, a_w_in, a_w0, a_w1, a_w2, a_a0, a_a1, a_a2,
           a_k_k, a_k_a, a_r_k, a_lnx_g, a_lnx_b, a_w_out,
           kv_norm_g, w_kv, k_norm_g, b_norm_g, b_w_in, q_norm_g, b_w_out, _SP=None, _NC=8, _PH="AB0123456789", _DP=True, _DS=True):
    f = lambda a: np.ascontiguousarray(np.asarray(a, np.float32))
    B, S, _ = x_prompt.shape
    SP = _SP or S
    key = (SP, _PH, _DP, _DS)
    if key not in _CACHE:
        _CACHE[key] = build_program(SP, phases=_PH, do_prompt=_DP, do_sample=_DS)
    nc = _CACHE[key]
    cs = _consts()
    vlist = [a_norm_g[0]] + [a_mu[0, j] for j in range(6)] + [a_w0[0], a_a0[0], a_k_k[0], a_k_a[0],
             np.asarray(a_r_k[0]).reshape(-1), a_lnx_g[0], a_lnx_b[0], kv_norm_g, b_norm_g[0]]
    vecs = np.ascontiguousarray(np.stack([_fm(v) for v in vlist], 1))
    shared = dict(
        a_w_in=f(a_w_in[0]), a_w1=f(a_w1[0]), a_w2=f(a_w2[0]), a_a1=f(a_a1[0]), a_a2=f(a_a2[0]),
        a_w_out=f(a_w_out[0]), w_kv=f(w_kv), b_w_in=f(b_w_in[0]), b_w_out=f(b_w_out[0]),
        vecs=vecs, grow=f(a_norm_g[0]), gq=np.ascontiguousarray(f(q_norm_g[0]).T),
        gkb=np.ascontiguousarray(np.broadcast_to(f(k_norm_g)[None, :], (128, 128))),
        identf=cs["identf"], bones=cs["bones"], rmask=cs["rmask"], amask=cs["amask"])
    in_maps = []
    for c in range(_NC):
        m = dict(shared)
        m["xp"] = f(x_prompt[c % B, :SP])
        m["xs"] = f(x_sample[c])
        m["swkv"] = f(state_wkv[0, c]).reshape(D, 64)
        m["sshift"] = _fm(state_shift[0, c])
        m["ck"] = f(cache_k[c]).reshape(-1, 1024)
        m["cv"] = f(cache_v[c]).reshape(-1, 1024)
        in_maps.append(m)
    res = run_bass_kernel_spmd(nc, in_maps, core_ids=list(range(_NC)))
    R = list(res.results)
    while len(R) < 8:
        R.append(R[0])
    y_p = np.stack([R[b]["yp"] for b in range(B)])
    y_s = np.stack([R[c]["ys"] for c in range(8)])
    wkv_p = np.stack([R[b]["wkvp"] for b in range(B)])[None]
    sh_p = np.stack([R[b]["shiftp"] for b in range(B)])[None]
    k_p = np.stack([R[b]["kp"].reshape(SP, 8, 128) for b in range(B)])
    v_p = np.stack([R[b]["vp"].reshape(SP, 8, 128) for b in range(B)])
    wkv_s = np.stack([R[c]["wkvs"] for c in range(8)])[None]
    sh_s = np.stack([R[c]["shifts"] for c in range(8)])[None]
    k_s = np.stack([R[c]["ks"].reshape(8, 8, 128) for c in range(8)])
    v_s = np.stack([R[c]["vs"].reshape(8, 8, 128) for c in range(8)])
    return (y_p, y_s, wkv_p, sh_p, k_p, v_p, wkv_s, sh_s, k_s, v_s)
```

```python
import os
import numpy as np
import ml_dtypes
A2CUT = int(os.environ.get("A2CUT", "9"))
from contextlib import ExitStack
import concourse.bass as bass
import concourse.mybir as mybir
from concourse.bass_utils import run_bass_kernel_spmd

F32 = mybir.dt.float32
BF16 = mybir.dt.bfloat16
AF = mybir.ActivationFunctionType
ALU = mybir.AluOpType
AX = mybir.AxisListType

D = 4096
NKC = 32
C0 = float(np.exp(-0.5))
V_G, V_MU, V_W0, V_A0, V_KK, V_KA, V_RK, V_LG, V_LB, V_KVG, V_BG = 0, 1, 7, 8, 9, 10, 11, 12, 13, 14, 15
NV = 16
DILS = (1, 4, 16)
NDELTA = (2, 5, 17)
MOFF = (0, 2, 7)


class T:
    __slots__ = ("h", "name", "lw", "rd", "dsem", "dcnt")

    def __init__(self, h, name):
        self.h = h
        self.name = name
        self.lw = None
        self.rd = {}
        self.dsem = None
        self.dcnt = 0

    def __getitem__(self, k):
        return self.h[k]


class Eng:
    def __init__(self, name, sem):
        self.name = name
        self.sem = sem
        self.cnt = 0
        self.epoch = 0
        self.waited = {}
        self.prog = []


class _Rec:
    def __init__(self):
        self.call = None

    def __getattr__(self, name):
        def f(*args, **kw):
            self.call = (name, args, kw)
        return f


class Scope(ExitStack):
    def __init__(self):
        super().__init__()
        self.tiles = []


class Ctx:
    def __init__(self, nc, es):
        self.free_dsems = []
        self.nc = nc
        self.es = es
        self.eng = {}
        for name in ("pe", "act", "dve", "pool", "sp"):
            sem = es.enter_context(nc.semaphore("s_" + name))
            self.eng[name] = Eng(name, sem)
        self.nt = 0
        self.dsems = []

    def sb(self, es, shape, dt, name):
        self.nt += 1
        h = es.enter_context(self.nc.sbuf_tensor(f"{name}_{self.nt}", list(shape), dt))
        t = T(h, f"{name}_{self.nt}")
        if isinstance(es, Scope):
            es.tiles.append(t)
        return t

    def ps(self, es, shape, dt, name):
        self.nt += 1
        h = es.enter_context(self.nc.psum_tensor(f"{name}_{self.nt}", list(shape), dt))
        return T(h, f"{name}_{self.nt}")

    def view(self, h, name):
        self.nt += 1
        return T(h, f"{name}_{self.nt}")

    def _need(self, e, reads, writes):
        need = {}

        def add(ev):
            if ev is None:
                return
            kind, key, val = ev
            k = (kind, key if kind == "e" else key.num)
            if k not in need or need[k][2] < val:
                need[k] = ev
        for t in reads:
            add(t.lw)
        for t in writes:
            add(t.lw)
            for ev in t.rd.values():
                add(ev)
        for k, ev in need.items():
            kind, key, val = ev
            if kind == "e":
                if key[1] != self.eng[key[0]].epoch:
                    continue
                if key[0] == "pe" and e.name == "pe":
                    continue
                if e.waited.get(k, 0) >= val:
                    continue
                e.waited[k] = val
                e.prog.append(("w", self.eng[key[0]].sem, val))
                continue
            if e.waited.get(k, 0) >= val:
                continue
            e.waited[k] = val
            e.prog.append(("w", self.eng[key].sem if kind == "e" else key, val))

    def op(self, ename, fn, reads=(), writes=()):
        e = self.eng[ename]
        self._need(e, reads, writes)
        e.cnt += 1
        rec = _Rec()
        fn(rec)
        name, args, kw = rec.call
        e.prog.append(("i", (lambda h, name=name, args=args, kw=kw: getattr(h, name)(*args, **kw)), e.sem, 1))
        ev = ("e", (ename, e.epoch), e.cnt)
        for t in reads:
            t.rd[("e", ename)] = ev
        for t in writes:
            t.lw = ev
            t.rd = {}

    def dma(self, qname, out, in_, anchor, reads=(), writes=()):
        e = self.eng[qname]
        self._need(e, reads, writes)
        if anchor.dsem is None:
            if self.free_dsems:
                anchor.dsem, anchor.dcnt = self.free_dsems.pop()
            else:
                anchor.dsem = self.es.enter_context(self.nc.semaphore("d_" + anchor.name))
            self.dsems.append(anchor)
        e.prog.append(("i", (lambda h, out=out, in_=in_: h.dma_start(out=out, in_=in_)), anchor.dsem, 16))
        anchor.dcnt += 16
        assert anchor.dcnt < 60000
        ev = ("d", anchor.dsem, anchor.dcnt)
        for t in reads:
            t.rd[("d", anchor.dsem.num)] = ev
        for t in writes:
            t.lw = ev
            t.rd = {}

    def barrier(self):
        for e in self.eng.values():
            for o in self.eng.values():
                if o is e or o.cnt == 0:
                    continue
                k = ("e", (o.name, o.epoch))
                if e.waited.get(k, 0) >= o.cnt:
                    continue
                e.waited[k] = o.cnt
                e.prog.append(("w", o.sem, o.cnt))
            for a in self.dsems:
                k = ("d", a.dsem.num)
                if e.waited.get(k, 0) >= a.dcnt:
                    continue
                e.waited[k] = a.dcnt
                e.prog.append(("w", a.dsem, a.dcnt))

    def new_epoch(self):
        for e in self.eng.values():
            e.sem = self.es.enter_context(self.nc.semaphore(f"s_{e.name}_{e.epoch + 1}"))
            e.cnt = 0
            e.epoch += 1
            e.waited = {k: v for k, v in e.waited.items() if k[0] != "e"}

    def end_scope(self, sc):
        self.barrier()
        if max(e.cnt for e in self.eng.values()) > 30000:
            self.new_epoch()
        for t in sc.tiles:
            if t.dsem is not None:
                self.free_dsems.append((t.dsem, t.dcnt))
                self.dsems.remove(t)
                t.dsem = None

    def emit(self):
        nc = self.nc
        handles = {"pe": "tensor", "act": "scalar", "dve": "vector", "pool": "gpsimd", "sp": "sync"}
        with nc.Block() as block:
            def mk(e):
                def body(h):
                    pend = []
                    for a in e.prog:
                        if a[0] == "w":
                            pend.append(a)
                            continue
                        for w in pend[:-1]:
                            h.wait_ge(w[1], w[2])
                        ins = a[1](h)
                        if pend:
                            ins._wait_ge(pend[-1][1], pend[-1][2])
                        ins.then_inc(a[2], a[3])
                        pend = []
                    for w in pend:
                        h.wait_ge(w[1], w[2])
                return body
            for n, attr in handles.items():
                getattr(block, attr)(mk(self.eng[n]))


class Job:
    pass


def build_program(SP, do_sample=True, SC=2048, phases="AB0123456789", do_prompt=True):
    nc = bass.Bass("TRN2", target_bir_lowering=False)

    def din(name, shape, dt=F32):
        return nc.dram_tensor(name, list(shape), dt, kind="ExternalInput").ap()

    def dout(name, shape, dt=F32):
        return nc.dram_tensor(name, list(shape), dt, kind="ExternalOutput").ap()

    def dscr(name, shape, dt=F32):
        return nc.dram_tensor(name, list(shape), dt).ap()

    A = {}
    A["xp"] = din("xp", [SP, D])
    A["xs"] = din("xs", [8, D])
    A["swkv"] = din("swkv", [D, 64])
    A["sshift"] = din("sshift", [128, NKC])
    A["ck"] = din("ck", [SC, 1024])
    A["cv"] = din("cv", [SC, 1024])
    A["a_w_in"] = din("a_w_in", [D, 4 * D])
    A["a_w1"] = din("a_w1", [D, 128])
    A["a_w2"] = din("a_w2", [128, D])
    A["a_a1"] = din("a_a1", [D, 128])
    A["a_a2"] = din("a_a2", [128, D])
    A["a_w_out"] = din("a_w_out", [D, D])
    A["w_kv"] = din("w_kv", [D, 2048])
    A["b_w_in"] = din("b_w_in", [D, 4 * D])
    A["b_w_out"] = din("b_w_out", [D, D])
    A["vecs"] = din("vecs", [128, NV, NKC])
    A["grow"] = din("grow", [D])
    A["gq"] = din("gq", [128, 3])
    A["gkb"] = din("gkb", [128, 128])
    A["identf"] = din("identf", [128, 128])
    A["bones"] = din("bones", [128, 128])
    A["rmask"] = din("rmask", [128, 3, 4, 128])
    A["amask"] = din("amask", [128, 24, 128])

    O = {}
    O["yp"] = dout("yp", [SP, D])
    O["ys"] = dout("ys", [8, D])
    O["wkvp"] = dout("wkvp", [64, 64, 64])
    O["shiftp"] = dout("shiftp", [D])
    O["kp"] = dout("kp", [SP, 1024])
    O["vp"] = dout("vp", [SP, 1024])
    O["wkvs"] = dout("wkvs", [64, 64, 64])
    O["shifts"] = dout("shifts", [D])
    O["ks"] = dout("ks", [8, 1024])
    O["vs"] = dout("vs", [8, 1024])

    WSC.clear()
    WDONE.clear()
    for wn in ("a_w_in", "a_w_out", "w_kv", "b_w_in", "b_w_out"):
        WSC[wn] = dscr(wn + "_bf16", [int(A[wn].shape[0]) * int(A[wn].shape[1]) // (128 * 4096), 128, 4096], BF16)
    with ExitStack() as es:
        cx = Ctx(nc, es)
        vecs = cx.sb(es, [128, NV, NKC], F32, "vecs")
        identf = cx.sb(es, [128, 128], F32, "identf")
        identb = cx.sb(es, [128, 128], BF16, "identb")
        bones = cx.sb(es, [128, 128], BF16, "bones")
        onesb = cx.sb(es, [128, 128], BF16, "onesb")
        onesf = cx.sb(es, [128, 128], F32, "onesf")
        rmask = cx.sb(es, [128, 3, 4, 128], BF16, "rmask")
        amask = cx.sb(es, [128, 24, 128], BF16, "amask")
        gqs = cx.sb(es, [128, 3], F32, "gqs")
        gkb = cx.sb(es, [128, 128], F32, "gkb")
        ident2 = cx.sb(es, [128, 4, 128], BF16, "ident2")
        cx.dma("sp", vecs[:], A["vecs"], vecs, writes=[vecs])
        cx.dma("sp", identf[:], A["identf"], identf, writes=[identf])
        cx.dma("pool", bones[:], A["bones"], bones, writes=[bones])
        cx.dma("pool", rmask[:], A["rmask"], rmask, writes=[rmask])
        cx.dma("pool", amask[:], A["amask"], amask, writes=[amask])
        cx.dma("sp", gqs[:], A["gq"], gqs, writes=[gqs])
        cx.dma("sp", gkb[:], A["gkb"], gkb, writes=[gkb])
        cx.op("dve", lambda h: h.tensor_copy(out=identb[:], in_=identf[:]), [identf], [identb])
        cx.op("dve", lambda h: h.memset(onesb[:], 1.0), [], [onesb])
        cx.op("dve", lambda h: h.memset(onesf[:], 1.0), [], [onesf])
        cx.op("dve", lambda h: h.tensor_scalar(out=gqs[:], in0=gqs[:], scalar1=float(128 ** -0.5), scalar2=None, op0=ALU.mult), [gqs], [gqs])
        for i in range(4):
            cx.op("dve", lambda h, i=i: h.tensor_copy(out=ident2[:, i, :], in_=identf[:]), [identf], [ident2])

        def vec(idx, c):
            return vecs[:, idx, c:c + 1]

        scr_h_p = cx.view(dscr("scr_h_p", [SP, D]), "scr_h_p")
        scr_h_s = cx.view(dscr("scr_h_s", [8, D]), "scr_h_s")
        scr_p = cx.view(dscr("scr_p", [4, D, 512]), "scr_p")
        ktscr = cx.view(dscr("ktscr", [8, 128, max(SP, SC + 128)], BF16), "ktscr")
        yp_t = cx.view(O["yp"], "yp")
        ys_t = cx.view(O["ys"], "ys")
        kp_t = cx.view(O["kp"], "kp")
        vp_t = cx.view(O["vp"], "vp")
        ks_t = cx.view(O["ks"], "ks")
        vs_t = cx.view(O["vs"], "vs")
        outs_misc = cx.view(O["wkvp"], "misc")

        jobs = []
        jp = Job()
        jp.name, jp.S, jp.TT, jp.C = "p", SP, 512, 128
        jp.x, jp.h_scr, jp.y = A["xp"], scr_h_p, yp_t
        jp.k_out, jp.v_out = kp_t, vp_t
        jp.wkv_out, jp.shift_out = O["wkvp"], O["shiftp"]
        jp.has_state = False
        jp.ncache = 0
        if do_prompt:
            jobs.append(jp)
        if do_sample:
            js = Job()
            js.name, js.S, js.TT, js.C = "s", 8, 8, 8
            js.x, js.h_scr, js.y = A["xs"], scr_h_s, ys_t
            js.k_out, js.v_out = ks_t, vs_t
            js.wkv_out, js.shift_out = O["wkvs"], O["shifts"]
            js.has_state = True
            js.ncache = SC // 128
            jobs.append(js)

        wq = [0]

        for job in jobs:
            run_job(cx, nc, es, job, A, O, vec, dict(
                vecs=vecs, identf=identf, identb=identb, bones=bones, onesb=onesb, onesf=onesf,
                rmask=rmask, amask=amask, gqs=gqs, gkb=gkb, ident2=ident2, scr_p=scr_p, ktscr=ktscr,
                outs_misc=outs_misc), phases)

        cx.barrier()
        cx.emit()
    return nc


def run_job(cx, nc, es_glob, job, A, O, vec, K, phases):
    S, TT, C = job.S, job.TT, job.C
    ntile = S // TT
    nsub = (TT + 127) // 128
    P = min(TT, 128)
    nch = TT // C
    nst = {128: 6, 8: 2}[C]
    identb, identf, bones, onesb, onesf = K["identb"], K["identf"], K["bones"], K["onesb"], K["onesf"]
    rmask, amask, gqs, gkb, ident2, scr_p, vecs = K["rmask"], K["amask"], K["gqs"], K["gkb"], K["ident2"], K["scr_p"], K["vecs"]

    with Scope() as ej:
        Sst = cx.sb(ej, [128, NKC, 64], F32, "Sst")
        hrstd = cx.sb(ej, [128, 16], F32, "hrstd")
        prevcol = cx.sb(ej, [128, NKC, 1], BF16, "prevcol")
        if job.has_state:
            with Scope() as e0:
                zp = cx.sb(e0, [128, NKC, 128], F32, "zp")
                pst = cx.ps(e0, [128, 4, 128], F32, "pst")
                sh = cx.sb(e0, [128, NKC], F32, "sh")
                cx.op("dve", lambda h: h.memset(zp[:], 0.0), [], [zp])
                src = A["swkv"].rearrange("(hp e v) k -> e v hp k", e=2, v=64)
                cx.dma("sp", zp[0:64, :, 0:64], src[0], zp, writes=[zp])
                cx.dma("sp", zp[64:128, :, 64:128], src[1], zp, writes=[zp])
                for hp in range(NKC):
                    j = hp % 4
                    cx.op("pe", lambda h, hp=hp, j=j: h.transpose(out=pst[:, j, :], in_=zp[:, hp, :], identity=identf[:]), [zp, identf], [pst])
                    if j == 3:
                        h0 = hp - 3
                        cx.op("dve", lambda h, h0=h0: h.tensor_copy(out=Sst[0:64, h0:h0 + 4, :], in_=pst[0:64, :, 0:64]), [pst], [Sst])
                        cx.op("act", lambda h, h0=h0: h.activation(out=Sst[64:128, h0:h0 + 4, :], in_=pst[64:128, :, 64:128], func=AF.Copy), [pst], [Sst])
                cx.dma("sp", sh[:], A["sshift"], sh, writes=[sh])
                cx.op("dve", lambda h: h.tensor_copy(out=prevcol[:, :, 0], in_=sh[:]), [sh], [prevcol])
                cx.end_scope(e0)
        else:
            cx.op("dve", lambda h: h.memset(Sst[:], 0.0), [], [Sst])
            cx.op("dve", lambda h: h.memset(prevcol[:], 0.0), [], [prevcol])

        if "A" in phases:
            for t in range(ntile):
                layer_a_tile(cx, nc, job, t, A, O, vec, K, Sst, hrstd, prevcol, phases)
            with Scope() as e0:
                pst = cx.ps(e0, [64, 4, 128], F32, "pso")
                so = cx.sb(e0, [64, NKC, 128], F32, "so")
                for hp in range(NKC):
                    j = hp % 4
                    cx.op("pe", lambda h, hp=hp, j=j: h.transpose(out=pst[:, j, :], in_=Sst[:, hp, :], identity=identf[:]), [Sst, identf], [pst])
                    if j == 3:
                        h0 = hp - 3
                        cx.op("dve", lambda h, h0=h0: h.tensor_copy(out=so[:, h0:h0 + 4, :], in_=pst[:]), [pst], [so])
                dst = job.wkv_out.rearrange("(hp e) v k -> v hp e k", e=2)
                cx.dma("sp", dst, so[:].rearrange("v hp (e k) -> v hp e k", e=2), so, reads=[so], writes=[K["outs_misc"]])
                cx.end_scope(e0)

        if "B" in phases:
            layer_b(cx, nc, job, A, O, vec, K, hrstd, phases)
        cx.end_scope(ej)


def rms_rstd(cx, ss_in, out, P, scale, eps, reads_extra=()):
    st, sap = ss_in
    ot, oap = out
    cx.op("dve", lambda h: h.tensor_scalar(out=oap, in0=sap, scalar1=scale, scalar2=eps, op0=ALU.mult, op1=ALU.add), [st], [ot])
    cx.op("act", lambda h: h.activation(out=oap, in_=oap, func=AF.Sqrt), [ot], [ot])
    cx.op("dve", lambda h: h.reciprocal(out=oap, in_=oap), [ot], [ot])


def norm_transpose(cx, es, job, src_rows, rstd_ap_fn, gidx, xnT, col0, vecs, identb, last_out=None, grow=None, compute_rstd=True, hrstd=None, t=0):
    TT = job.TT
    nsub = (TT + 127) // 128
    P = min(TT, 128)
    nb = min(nsub, 4)
    xt2 = [cx.sb(es, [128, D], F32, "xt") for _ in range(nb)]
    xs2 = [cx.sb(es, [128, D], BF16, "xs") for _ in range(2)]
    junk = cx.sb(es, [128, D], BF16, "junk")
    ss = cx.sb(es, [128, 4], F32, "ss")
    rs = cx.sb(es, [128, 4], F32, "rs")
    pt2 = [cx.ps(es, [128, 4, 128], BF16, "pt") for _ in range(2)]
    for s in range(nsub):
        xt, xs = xt2[s % nb], xs2[s % 2]
        cx.dma("sp", xt[:P, :], src_rows(s, P), xt, writes=[xt])
        if compute_rstd:
            cx.op("act", lambda h, xt=xt, s=s: h.activation(out=junk[:P, :], in_=xt[:P, :], func=AF.Square, accum_out=ss[:P, s:s + 1]), [xt], [junk, ss])
            rms_rstd(cx, (ss, ss[:P, s:s + 1]), (rs, rs[:P, s:s + 1]), P, 1.0 / D, 1e-6)
            rap, rt = rs[:P, s:s + 1], rs
        else:
            rap, rt = hrstd[:P, t * 4 + s:t * 4 + s + 1], hrstd
        cx.op("act", lambda h, xt=xt, xs=xs, rap=rap: h.activation(out=xs[:P, :], in_=xt[:P, :], func=AF.Identity, scale=rap), [xt, rt], [xs])
        if last_out is not None and s == nsub - 1:
            xnf = cx.sb(es, [128, D], F32, "xnf")
            gbc = cx.sb(es, [128, D], F32, "gbc")
            cx.dma("sp", gbc[:P, :], grow.partition_broadcast(P), gbc, writes=[gbc])
            cx.op("act", lambda h, xt=xt, rap=rap: h.activation(out=xnf[:P, :], in_=xt[:P, :], func=AF.Identity, scale=rap), [xt, rt], [xnf])
            cx.op("dve", lambda h: h.tensor_tensor(out=xnf[:P, :], in0=xnf[:P, :], in1=gbc[:P, :], op=ALU.mult), [xnf, gbc], [xnf])
            cx.dma("sp", last_out.rearrange("(o d) -> o d", o=1), xnf[P - 1:P, :], xnf, reads=[xnf])
        for c4 in range(8):
            p = pt2[c4 % 2]
            for j in range(4):
                c = c4 * 4 + j
                cx.op("pe", lambda h, c=c, j=j, p=p, xs=xs: h.transpose(out=p[:, j, :P], in_=xs[:P, c * 128:(c + 1) * 128], identity=identb[:P, :P]), [xs, identb], [p])
            cx.op("dve", lambda h, c4=c4, p=p, s=s: h.tensor_tensor(
                out=xnT[:, c4 * 4:c4 * 4 + 4, col0 + s * 128:col0 + s * 128 + P], in0=p[:, :, :P],
                in1=vecs[:, gidx, c4 * 4:c4 * 4 + 4].unsqueeze(2).to_broadcast([128, 4, P]), op=ALU.mult), [p, vecs], [xnT])


WSC = {}
WDONE = {}


def stream_weights(cx, es, W, row0, nrows, col0, ncols, ring, ctr):
    slot = ring[ctr[0] % len(ring)]
    ctr[0] += 1
    nk = nrows // 128
    name = W.tensor.name
    key = (name, row0, nrows, col0, ncols)
    w16 = WSC[name]
    assert nk * ncols == 4096 and tuple(slot.h.shape) == (128, nk, ncols)
    sflat = slot.h[:].rearrange("p a b -> p (a b)")
    if key not in WDONE:
        gid = sum(1 for k in WDONE if k[0] == name)
        WDONE[key] = gid
        cx.dma("pool", slot[:, :nk, :ncols], W[row0:row0 + nrows, col0:col0 + ncols].rearrange("(kc p) n -> p kc n", p=128), slot, writes=[slot])
        cx.dma("act", w16[gid], sflat, slot, reads=[slot])
    else:
        cx.dma("pool", sflat, w16[WDONE[key]], slot, writes=[slot])
    return slot


def layer_a_tile(cx, nc, job, t, A, O, vec, K, Sst, hrstd, prevcol, phases):
    S, TT, C = job.S, job.TT, job.C
    nsub = (TT + 127) // 128
    P = min(TT, 128)
    nch = TT // C
    nst = {128: 6, 8: 2}[C]
    tok0 = t * TT
    ntile = S // TT
    identb, identf, bones, onesb, onesf = K["identb"], K["identf"], K["bones"], K["onesb"], K["onesf"]
    rmask, ident2, scr_p, vecs = K["rmask"], K["ident2"], K["scr_p"], K["vecs"]
    last = (t == ntile - 1)

    with Scope() as et:
      tanhT = cx.sb(et, [128, TT], BF16, "tanhT")
      ahT = cx.sb(et, [128, TT], BF16, "ahT")
      with Scope() as ex:
        xnT = cx.sb(ex, [128, NKC, TT + 1], BF16, "xnT")
        with Scope() as e0:
          if "0" in phases:
            cx.op("dve", lambda h: h.tensor_copy(out=xnT[:, :, 0:1], in_=prevcol[:]), [prevcol], [xnT])
            norm_transpose(cx, e0, job, lambda s, P: job.x[tok0 + s * 128:tok0 + s * 128 + P, :], None, V_G, xnT, 1,
                           vecs, identb, last_out=(job.shift_out if last else None), grow=A["grow"])
            cx.op("dve", lambda h: h.tensor_copy(out=prevcol[:], in_=xnT[:, :, TT:TT + 1]), [xnT], [prevcol])
          cx.end_scope(e0)
        with Scope() as e1:
            mixb = [cx.sb(e1, [128, NKC, TT], BF16, "mix") for _ in range(2)]
            tmpd = [cx.sb(e1, [128, TT], F32, "tmpd") for _ in range(3)]
            pp = [cx.ps(e1, [128, 2, 512], F32, "pp") for _ in range(3)]
            pl = cx.ps(e1, [128, 512], F32, "pl")
            ctr = [0]
            ncg = 0
            e1a = Scope()
            e1a.__enter__()
            w1sb = cx.sb(e1a, [128, NKC, 128], BF16, "w1sb")
            a1sb = cx.sb(e1a, [128, NKC, 128], BF16, "a1sb")
            cx.dma("pool", w1sb[:], A["a_w1"].rearrange("(kc p) n -> p kc n", p=128), w1sb, writes=[w1sb])
            cx.dma("pool", a1sb[:], A["a_a1"].rearrange("(kc p) n -> p kc n", p=128), a1sb, writes=[a1sb])
            ring = stg = None
            for j in ((4, 5, 0, 1, 2, 3) if "1" in phases else ()):
                mix = mixb[j % 2]
                if j == 0:
                    cx.end_scope(e1a)
                    e1a.__exit__(None, None, None)
                    ring = [cx.sb(e1, [128, 16, 256], BF16, "wr") for _ in range(4)]
                    stg = [cx.sb(e1, [128, 2, TT], F32, "stg") for _ in range(2)]
                for c in range(NKC):
                    td = tmpd[c % 3]
                    cx.op("pool", lambda h, c=c, td=td: h.tensor_tensor(out=td[:, :TT], in0=xnT[:, c, 0:TT], in1=xnT[:, c, 1:TT + 1], op=ALU.subtract), [xnT], [td])
                    cx.op("dve", lambda h, c=c, td=td, mix=mix, j=j: h.scalar_tensor_tensor(
                        out=mix[:, c, :], in0=td[:, :TT], scalar=vecs[:, V_MU + j, c:c + 1], in1=xnT[:, c, 1:TT + 1],
                        op0=ALU.mult, op1=ALU.add), [td, xnT, vecs], [mix])
                if j >= 4:
                    wsb = w1sb if j == 4 else a1sb
                    for kc in range(NKC):
                        cx.op("pe", lambda h, kc=kc, wsb=wsb, mix=mix: h.matmul(pl[:, :TT], lhsT=wsb[:, kc, :], rhs=mix[:, kc, :], start=(kc == 0), stop=(kc == NKC - 1)), [wsb, mix], [pl])
                    if j == 4:
                        cx.op("act", lambda h: h.activation(out=tanhT[:, :], in_=pl[:, :TT], func=AF.Tanh), [pl], [tanhT])
                    else:
                        cx.op("act", lambda h: h.activation(out=ahT[:, :], in_=pl[:, :TT], func=AF.Copy), [pl], [ahT])
                    continue
                for cg in range(16):
                    ppt = pp[ncg % 3]
                    sg = stg[ncg % 2]
                    ncg += 1
                    for k2 in range(2):
                        slot = stream_weights(cx, e1, A["a_w_in"], k2 * 2048, 2048, j * D + cg * 256, 256, ring, ctr)
                        for oc in range(2):
                            for kc in range(16):
                                kk = k2 * 16 + kc
                                cx.op("pe", lambda h, ppt=ppt, slot=slot, oc=oc, kc=kc, kk=kk, mix=mix: h.matmul(
                                    ppt[:, oc, :TT], lhsT=slot[:, kc, oc * 128:(oc + 1) * 128], rhs=mix[:, kk, :],
                                    start=(kk == 0), stop=(kk == NKC - 1)), [slot, mix], [ppt])
                    if j == 3:
                        cx.op("act", lambda h, ppt=ppt, sg=sg: h.activation(out=sg[:, :, :], in_=ppt[:, :, :TT], func=AF.Silu), [ppt], [sg])
                    elif cg % 2 == 0:
                        cx.op("act", lambda h, ppt=ppt, sg=sg: h.activation(out=sg[:, :, :], in_=ppt[:, :, :TT], func=AF.Copy), [ppt], [sg])
                    else:
                        cx.op("act", lambda h, ppt=ppt, sg=sg: h.activation(out=sg[:, :, :], in_=ppt[:, :, :TT], func=AF.Copy), [ppt], [sg])
                    cx.dma("sp", scr_p[j, cg * 256:(cg + 1) * 256, 0:TT].rearrange("(o p) t -> p o t", p=128), sg[:, :, :], sg, reads=[sg], writes=[scr_p])
            if ring is None:
                cx.end_scope(e1a)
                e1a.__exit__(None, None, None)
            cx.end_scope(e1)
        cx.end_scope(ex)
      with Scope() as ey:
        oT = cx.sb(ey, [128, NKC, TT], BF16, "oT")
        with Scope() as e2:
            if "2" in phases:
                phase_a2(cx, e2, job, A, K, Sst, tanhT, ahT, oT)
            cx.end_scope(e2)
        with Scope() as e3:
            ring = [cx.sb(e3, [128, 8, 512], BF16, "wr3") for _ in range(4)]
            xres = [cx.sb(e3, [128, 512], F32, "xres") for _ in range(4)]
            hsb = [cx.sb(e3, [128, 512], F32, "hsb") for _ in range(4)]
            junk = cx.sb(e3, [128, 512], BF16, "junk3")
            ssq = cx.sb(e3, [128, 4, 8], F32, "ssq")
            sst = cx.sb(e3, [128, 4], F32, "sst")
            pp = [cx.ps(e3, [128, 4, 512], F32, "pp3") for _ in range(2)]
            ctr = [0]
            n = 0
            for cg in (range(8) if "3" in phases else ()):
                ppt = pp[cg % 2]
                for k4 in range(4):
                    slot = stream_weights(cx, e3, A["a_w_out"], k4 * 1024, 1024, cg * 512, 512, ring, ctr)
                    for s in range(nsub):
                        for kc in range(8):
                            kk = k4 * 8 + kc
                            cx.op("pe", lambda h, ppt=ppt, slot=slot, s=s, kc=kc, kk=kk: h.matmul(
                                ppt[:P, s, :], lhsT=oT[:, kk, s * 128:s * 128 + P], rhs=slot[:, kc, :],
                                start=(kk == 0), stop=(kk == NKC - 1)), [slot, oT], [ppt])
                for s in range(nsub):
                    xr, hb = xres[n % 4], hsb[n % 4]
                    n += 1
                    cx.dma("sp", xr[:P, :], job.x[tok0 + s * 128:tok0 + s * 128 + P, cg * 512:(cg + 1) * 512], xr, writes=[xr])
                    cx.op("dve", lambda h, ppt=ppt, s=s, xr=xr, hb=hb: h.tensor_tensor(out=hb[:P, :], in0=ppt[:P, s, :], in1=xr[:P, :], op=ALU.add), [ppt, xr], [hb])
                    cx.op("act", lambda h, hb=hb, s=s, cg=cg: h.activation(out=junk[:P, :], in_=hb[:P, :], func=AF.Square, accum_out=ssq[:P, s, cg:cg + 1]), [hb], [junk, ssq])
                    cx.dma("sp", job.h_scr[tok0 + s * 128:tok0 + s * 128 + P, cg * 512:(cg + 1) * 512], hb[:P, :], hb, reads=[hb], writes=[job.h_scr])
            cx.op("dve", lambda h: h.tensor_reduce(out=sst[:P, :nsub], in_=ssq[:P, :nsub, :], axis=AX.X, op=ALU.add), [ssq], [sst])
            rms_rstd(cx, (sst, sst[:P, :nsub]), (hrstd, hrstd[:P, t * 4:t * 4 + nsub]), P, 1.0 / D, 1e-6)
            cx.end_scope(e3)
        cx.end_scope(ey)
      with Scope() as ez:
        xnT = cx.sb(ez, [128, NKC, TT + 1], BF16, "hnTa")
        with Scope() as e4:
          if "4" in phases:
            norm_transpose(cx, e4, job, lambda s, P: job.h_scr[tok0 + s * 128:tok0 + s * 128 + P, :], None, V_KVG, xnT, 0,
                           vecs, identb, compute_rstd=False, hrstd=hrstd, t=t)
          cx.end_scope(e4)
        with Scope() as e5:
            ring = [cx.sb(e5, [128, 8, 512], BF16, "wr5") for _ in range(4)]
            pp = [cx.ps(e5, [128, 4, 512], F32, "pp5") for _ in range(2)]
            ksb = [cx.sb(e5, [128, 4, 128], F32, "ksb") for _ in range(3)]
            ksq = cx.sb(e5, [128, 4, 128], F32, "ksq")
            kss = [cx.sb(e5, [128, 4], F32, "kss") for _ in range(2)]
            gkb = K["gkb"]
            ctr = [0]
            n = 0
            for cg in (range(4) if "5" in phases else ()):
                ppt = pp[cg % 2]
                for k4 in range(4):
                    slot = stream_weights(cx, e5, A["w_kv"], k4 * 1024, 1024, cg * 512, 512, ring, ctr)
                    for s in range(nsub):
                        for kc in range(8):
                            kk = k4 * 8 + kc
                            cx.op("pe", lambda h, ppt=ppt, slot=slot, s=s, kc=kc, kk=kk: h.matmul(
                                ppt[:P, s, :], lhsT=xnT[:, kk, s * 128:s * 128 + P], rhs=slot[:, kc, :],
                                start=(kk == 0), stop=(kk == NKC - 1)), [slot, xnT], [ppt])
                for s in range(nsub):
                    kb = ksb[n % 3]
                    ks_ = kss[n % 2]
                    n += 1
                    rows = slice(tok0 + s * 128, tok0 + s * 128 + P)
                    if cg < 2:
                        cx.op("act", lambda h, ppt=ppt, s=s, kb=kb: h.activation(out=kb[:P].rearrange("p a b -> p (a b)"), in_=ppt[:P, s, :], func=AF.Copy), [ppt], [kb])
                        cx.op("dve", lambda h, kb=kb: h.tensor_tensor(out=ksq[:P], in0=kb[:P], in1=kb[:P], op=ALU.mult), [kb], [ksq])
                        cx.op("dve", lambda h, ks_=ks_: h.tensor_reduce(out=ks_[:P, :], in_=ksq[:P], axis=AX.X, op=ALU.add), [ksq], [ks_])
                        rms_rstd(cx, (ks_, ks_[:P, :]), (ks_, ks_[:P, :]), P, 1.0 / 128, 1e-6)
                        cx.op("dve", lambda h, kb=kb, ks_=ks_: h.tensor_tensor(out=kb[:P], in0=kb[:P], in1=ks_[:P, :].unsqueeze(2).to_broadcast([P, 4, 128]), op=ALU.mult), [kb, ks_], [kb])
                        cx.op("dve", lambda h, kb=kb: h.tensor_tensor(out=kb[:P], in0=kb[:P], in1=gkb[:P, :].unsqueeze(1).to_broadcast([P, 4, 128]), op=ALU.mult), [kb, gkb], [kb])
                        cx.dma("sp", job.k_out[rows, cg * 512:(cg + 1) * 512], kb[:P].rearrange("p a b -> p (a b)"), kb, reads=[kb], writes=[job.k_out])
                    else:
                        cx.op("act", lambda h, ppt=ppt, s=s, kb=kb: h.activation(out=kb[:P].rearrange("p a b -> p (a b)"), in_=ppt[:P, s, :], func=AF.Copy), [ppt], [kb])
                        cx.dma("sp", job.v_out[rows, (cg - 2) * 512:(cg - 1) * 512], kb[:P].rearrange("p a b -> p (a b)"), kb, reads=[kb], writes=[job.v_out])
            cx.end_scope(e5)
        cx.end_scope(ez)
      cx.end_scope(et)


def _rr(*gens):
    gens = list(gens)
    while gens:
        for g in list(gens):
            try:
                next(g)
            except StopIteration:
                gens.remove(g)


def phase_a2(cx, es, job, A, K, Sst, tanhT, ahT, oT):
    TT, C = job.TT, job.C
    nch = TT // C
    nst = {128: 6, 8: 2}[C]
    identb, identf, bones = K["identb"], K["identf"], K["bones"]
    rmask, ident2, scr_p, vecs = K["rmask"], K["ident2"], K["scr_p"], K["vecs"]
    w2sb = cx.sb(es, [128, D], BF16, "w2sb")
    a2sb = cx.sb(es, [128, D], BF16, "a2sb")
    cx.dma("pool", w2sb[:], A["a_w2"], w2sb, writes=[w2sb])
    cx.dma("pool", a2sb[:], A["a_a2"], a2sb, writes=[a2sb])
    rin = [cx.sb(es, [128, 4, TT], F32, "rin") for _ in range(2)]

    def f32t(name):
        return cx.sb(es, [128, TT], F32, name)

    def bft(name):
        return cx.sb(es, [128, TT], BF16, name)
    lwp, al, clp, egi, egm, kkr, rn, kk, t1, kp, tmp, mean, m2, var, yc = [f32t(n) for n in (
        "lwp", "al", "clp", "egi", "egm", "kkr", "rn", "kk", "t1", "kp", "tmp", "mean", "m2", "var", "yc")]
    sq, rkb, ysq, ysbb = [bft(n) for n in ("sq", "rkb", "ysq", "ysbb")]
    eg2 = [f32t("eg") for _ in range(2)]
    bon2 = [f32t("bon") for _ in range(2)]
    ysb2 = [f32t("ysb") for _ in range(2)]
    bt2 = [bft("bt") for _ in range(2)]
    kt2 = [bft("kt") for _ in range(2)]
    vbf2 = [bft("vbf") for _ in range(2)]
    at22 = [[bft("at0"), bft("at1")] for _ in range(2)]
    rt22 = [[bft("rt0"), bft("rt1")] for _ in range(2)]
    for par_ in range(2):
        for z in at22[par_] + rt22[par_]:
            cx.op("dve", lambda h, z=z: h.memset(z[:], 0.0), [], [z])
    onesT = f32t("onesT")
    cx.op("dve", lambda h: h.memset(onesT[:], 1.0), [], [onesT])
    tokm = [cx.sb(es, [128, 3, 128], BF16, "tokm") for _ in range(2)]
    UA = [cx.sb(es, [128, 4, 128], BF16, "UA") for _ in range(2)]
    UB = [cx.sb(es, [128, 4, 128], BF16, "UB") for _ in range(2)]
    NTs = [cx.sb(es, [128, 2, 128], BF16, "NTs") for _ in range(2)]
    PPs = [[cx.sb(es, [128, 4, 128], BF16, "PP") for _ in range(3)] for _ in range(2)]
    RRs = [[cx.sb(es, [128, 4, 128], BF16, "RR") for _ in range(3)] for _ in range(2)]
    Sbf = cx.sb(es, [128, 64], BF16, "Sbf")
    Zb = cx.sb(es, [128, 2, 64], BF16, "Zb")
    Ub = cx.sb(es, [128, 2, 64], BF16, "Ub")
    tmpS = cx.sb(es, [128, 64], F32, "tmpS")
    pA = cx.ps(es, [128, 4, 128], F32, "pA")
    pPPs = [cx.ps(es, [128, 4, 128], F32, "pPP") for _ in range(2)]
    pRRs = [cx.ps(es, [128, 4, 128], F32, "pRR") for _ in range(2)]
    pX0, pX1 = pPPs[1], pRRs[1]
    pXv = [pX0.h[:].rearrange("p a b -> p (a b)"), pX1.h[:].rearrange("p a b -> p (a b)")]
    pm1 = cx.ps(es, [128, 512], F32, "pm1")
    pC = pZ = pU = pm1
    pm2 = cx.ps(es, [128, 512], F32, "pm2")
    pY = pS = pm2
    ptr = cx.ps(es, [128, 3, 128], BF16, "ptr")

    def V(idx, hp):
        return vecs[:, idx, hp:hp + 1]

    def make(hp):
        par = hp % 2
        r_in = rin[par]
        eg, bon, ysb, bt, kt, vbf, at2, rt2 = eg2[par], bon2[par], ysb2[par], bt2[par], kt2[par], vbf2[par], at22[par], rt22[par]

        def S1():
            if True:
                pass
                cx.dma("sp", r_in[:, :, :], scr_p[:, hp * 128:(hp + 1) * 128, 0:TT].rearrange("j p t -> p j t"), r_in, reads=[scr_p], writes=[r_in])
                r_ap, k_ap, v_ap, g_ap = r_in[:, 0, :], r_in[:, 1, :], r_in[:, 2, :], r_in[:, 3, :]
                fs = slice(hp * 128, (hp + 1) * 128)
                cx.op("pe", lambda h, fs=fs: h.matmul(pXv[0][:, :TT], lhsT=w2sb[:, fs], rhs=tanhT[:, :], start=True, stop=True), [w2sb, tanhT], [pX0])
                cx.op("pe", lambda h, fs=fs: h.matmul(pXv[1][:, :TT], lhsT=a2sb[:, fs], rhs=ahT[:, :], start=True, stop=True), [a2sb, ahT], [pX1])
                cx.op("act", lambda h, hp=hp: h.activation(out=lwp[:], in_=pXv[0][:, :TT], func=AF.Sigmoid, bias=V(V_W0, hp)), [pX0, vecs], [lwp])
                yield
                cx.op("act", lambda h, hp=hp: h.activation(out=al[:], in_=pXv[1][:, :TT], func=AF.Sigmoid, bias=V(V_A0, hp)), [pX1, vecs], [al])
                for ch in range(nch):
                    cs = slice(ch * C, (ch + 1) * C)
                    cx.op("dve", lambda h, cs=cs: h.tensor_tensor_scan(out=clp[:, cs], data0=onesT[:, cs], data1=lwp[:, cs], initial=0.0, op0=ALU.mult, op1=ALU.add), [onesT, lwp], [clp])
                cx.op("act", lambda h: h.activation(out=eg[:], in_=clp[:], func=AF.Exp, scale=-C0), [clp], [eg])
                yield
                cx.op("act", lambda h: h.activation(out=egi[:], in_=clp[:], func=AF.Exp, scale=C0), [clp], [egi])
                cx.op("dve", lambda h: h.tensor_tensor(out=tmp[:], in0=clp[:], in1=lwp[:], op=ALU.subtract), [clp, lwp], [tmp])
                cx.op("act", lambda h: h.activation(out=egm[:], in_=tmp[:], func=AF.Exp, scale=-C0), [tmp], [egm])
                yield
                cx.op("dve", lambda h, hp=hp, k_ap=k_ap: h.tensor_scalar(out=kkr[:], in0=k_ap, scalar1=V(V_KK, hp), scalar2=None, op0=ALU.mult), [r_in, vecs], [kkr])
                cx.op("act", lambda h: h.activation(out=sq[:], in_=kkr[:], func=AF.Square), [kkr], [sq])
                cx.op("pe", lambda h: h.matmul(pXv[0][:, :TT], lhsT=bones[:], rhs=sq[:], start=True, stop=True), [bones, sq], [pX0])
                yield
                cx.op("dve", lambda h: h.tensor_scalar(out=rn[:], in0=pXv[0][:, :TT], scalar1=1e-24, scalar2=None, op0=ALU.max), [pX0], [rn])
                cx.op("act", lambda h: h.activation(out=rn[:], in_=rn[:], func=AF.Ln), [rn], [rn])
                cx.op("act", lambda h: h.activation(out=rn[:], in_=rn[:], func=AF.Exp, scale=-0.5), [rn], [rn])
                yield
                cx.op("dve", lambda h: h.tensor_tensor(out=kk[:], in0=kkr[:], in1=rn[:], op=ALU.mult), [kkr, rn], [kk])
                cx.op("dve", lambda h, hp=hp: h.tensor_scalar(out=t1[:], in0=al[:], scalar1=-1.0, scalar2=V(V_KA, hp), op0=ALU.add, op1=ALU.mult), [al, vecs], [t1])
                cx.op("dve", lambda h, k_ap=k_ap: h.scalar_tensor_tensor(out=kp[:], in0=t1[:], scalar=1.0, in1=k_ap, op0=ALU.add, op1=ALU.mult), [t1, r_in], [kp])
                yield
                for e in range(2):
                    po = slice(64 * e, 64 * e + 64)
                    cx.op("dve", lambda h, e=e, po=po: h.scalar_tensor_tensor(out=at2[e][po, :], in0=kk[po, :], scalar=-1.0, in1=egm[po, :], op0=ALU.mult, op1=ALU.mult), [kk, egm], [at2[e]])
                cx.op("pool", lambda h: h.tensor_tensor(out=tmp[:], in0=kk[:], in1=al[:], op=ALU.mult), [kk, al], [tmp])
                cx.op("pool", lambda h: h.tensor_tensor(out=bt[:], in0=tmp[:], in1=egi[:], op=ALU.mult), [tmp, egi], [bt])
                yield
                cx.op("pool", lambda h: h.tensor_tensor(out=kt[:], in0=kp[:], in1=egi[:], op=ALU.mult), [kp, egi], [kt])
                for e in range(2):
                    po = slice(64 * e, 64 * e + 64)
                    cx.op("pool", lambda h, e=e, po=po, r_in=r_in: h.tensor_tensor(out=rt2[e][po, :], in0=r_in[po, 0, :], in1=eg[po, :], op=ALU.mult), [r_in, eg], [rt2[e]])
                cx.op("dve", lambda h, hp=hp, r_ap=r_ap: h.scalar_tensor_tensor(out=rkb[:], in0=r_ap, scalar=V(V_RK, hp), in1=kp[:], op0=ALU.mult, op1=ALU.mult), [r_in, vecs, kp], [rkb])
                yield
                cx.op("pe", lambda h: h.matmul(pXv[1][:, :TT], lhsT=bones[:], rhs=rkb[:], start=True, stop=True), [bones, rkb], [pX1])
                cx.op("dve", lambda h, v_ap=v_ap: h.tensor_tensor(out=bon[:], in0=pXv[1][:, :TT], in1=v_ap, op=ALU.mult), [pX1, r_in], [bon])
                cx.op("act", lambda h, v_ap=v_ap: h.activation(out=vbf[:], in_=v_ap, func=AF.Copy), [r_in], [vbf])
                yield
            yield

        if True:
            def s2(ch, idx):
                cs = slice(ch * C, (ch + 1) * C)
                tk = tokm[idx]
                ua, ub, nts = UA[idx], UB[idx], NTs[idx]
                for i, srcT in enumerate((vbf, bt, kt)):
                    cx.op("pe", lambda h, i=i, srcT=srcT, cs=cs: h.transpose(out=ptr[:C, i, :], in_=srcT[:, cs], identity=identb[:]), [srcT, identb], [ptr])
                cx.op("act", lambda h, tk=tk: h.activation(out=tk[:C], in_=ptr[:C], func=AF.Copy), [ptr], [tk])
                for e in range(2):
                    cx.op("pe", lambda h, e=e, cs=cs: h.matmul(pA[:C, e, :C], lhsT=bt[:, cs], rhs=at2[e][:, cs], start=True, stop=True), [bt, at2[e]], [pA])
                    cx.op("pe", lambda h, e=e, cs=cs: h.matmul(pA[:C, 2 + e, :C], lhsT=kt[:, cs], rhs=at2[e][:, cs], start=True, stop=True), [kt, at2[e]], [pA])
                cx.op("dve", lambda h, ua=ua: h.tensor_tensor(out=ua[:C, :, :C], in0=pA[:C, :, :C], in1=rmask[:C, 0, :, :C], op=ALU.mult), [pA, rmask], [ua])
                for e in range(2):
                    cx.op("pe", lambda h, e=e, cs=cs: h.matmul(pA[:C, e, :C], lhsT=bt[:, cs], rhs=rt2[e][:, cs], start=True, stop=True), [bt, rt2[e]], [pA])
                    cx.op("pe", lambda h, e=e, cs=cs: h.matmul(pA[:C, 2 + e, :C], lhsT=kt[:, cs], rhs=rt2[e][:, cs], start=True, stop=True), [kt, rt2[e]], [pA])
                cx.op("dve", lambda h, ub=ub: h.tensor_tensor(out=ub[:C, :, :C], in0=pA[:C, :, :C], in1=rmask[:C, 1, :, :C], op=ALU.mult), [pA, rmask], [ub])
                for e in range(2):
                    cx.op("pe", lambda h, e=e, cs=cs: h.matmul(pm1[:C, e * 128:e * 128 + C], lhsT=at2[e][:, cs], rhs=bt[:, cs], start=True, stop=True), [at2[e], bt], [pC])
                cx.op("dve", lambda h, nts=nts: h.tensor_tensor(out=nts[:C, :, :C], in0=pm1[:C, 0:256].rearrange("p (e c) -> p e c", e=2)[:, :, :C], in1=rmask[:C, 2, 0:2, :C], op=ALU.mult), [pC, rmask], [nts])

            def s3(idx, st, res):
                ua, nts = UA[idx], NTs[idx]
                pPP, pRR, PP, RR = pPPs[0], pRRs[0], PPs[st], RRs[st]
                rr = RR[0]
                cx.op("dve", lambda h: h.tensor_tensor(out=rr[:C, 0:2, :C], in0=ua[:C, 0:2, :C], in1=ident2[:C, 0:2, :C], op=ALU.add), [ua, ident2], [rr])
                cx.op("dve", lambda h: h.tensor_tensor(out=rr[:C, 2:4, :C], in0=nts[:C, :, :C], in1=ident2[:C, 0:2, :C], op=ALU.add), [nts, ident2], [rr])
                ppc = PP[0]
                cx.op("act", lambda h: h.activation(out=ppc[:C, 0:2, :C], in_=ua[:C, 0:2, :C], func=AF.Copy), [ua], [ppc])
                cx.op("act", lambda h: h.activation(out=ppc[:C, 2:4, :C], in_=nts[:C, :, :C], func=AF.Copy), [nts], [ppc])
                yield
                for i in range(1, nst + 1):
                    lastst = (i == nst)
                    ppn = PP[i % 3]
                    rrn = RR[i % 3]
                    for e in range(2):
                        cx.op("pe", lambda h, e=e: h.matmul(pPP[:C, e, :C], lhsT=ppc[:C, 2 + e, :C], rhs=ppc[:C, e, :C], start=True, stop=True), [ppc], [pPP])
                        if not lastst:
                            cx.op("pe", lambda h, e=e: h.matmul(pPP[:C, 2 + e, :C], lhsT=ppc[:C, e, :C], rhs=ppc[:C, 2 + e, :C], start=True, stop=True), [ppc], [pPP])
                    nq = 2 if lastst else 4
                    cx.op("act", lambda h: h.activation(out=ppn[:C, 0:nq, :C], in_=pPP[:C, 0:nq, :C], func=AF.Copy), [pPP], [ppn])
                    yield
                    for e in range(2):
                        cx.op("pe", lambda h, e=e: h.matmul(pRR[:C, e, :C], lhsT=rr[:C, 2 + e, :C], rhs=ppn[:C, e, :C], start=True, stop=True), [rr, ppn], [pRR])
                        if not lastst:
                            cx.op("pe", lambda h, e=e: h.matmul(pRR[:C, 2 + e, :C], lhsT=ppn[:C, e, :C], rhs=rr[:C, 2 + e, :C], start=True, stop=True), [rr, ppn], [pRR])
                    cx.op("dve", lambda h: h.tensor_tensor(out=rrn[:C, 0:nq, :C], in0=pRR[:C, 0:nq, :C], in1=rr[:C, 0:nq, :C], op=ALU.add), [pRR, rr], [rrn])
                    ppc, rr = ppn, rrn
                    yield
                res.append(rr)

            def s4(ch, idx, rr):
                cs = slice(ch * C, (ch + 1) * C)
                tk = tokm[idx]
                ua, ub = UA[idx], UB[idx]
                cx.op("act", lambda h, hp=hp: h.activation(out=Sbf[:], in_=Sst[:, hp, :], func=AF.Copy), [Sst], [Sbf])
                for e in range(2):
                    po = slice(64 * e, 64 * e + 64)
                    zc = slice(256 + e * 64, 256 + e * 64 + 64)
                    cx.op("pe", lambda h, e=e, po=po, zc=zc, cs=cs: h.matmul(pm1[:C, zc], lhsT=at2[e][:, cs], rhs=Sbf[:, :], start=True, stop=False), [at2[e], Sbf], [pZ])
                    cx.op("pe", lambda h, e=e, po=po, zc=zc, ua=ua, tk=tk: h.matmul(pm1[:C, zc], lhsT=ua[:C, 2 + e, :C], rhs=tk[:C, 0, po], start=False, stop=True), [ua, tk], [pZ])
                cx.op("act", lambda h: h.activation(out=Zb[:C].rearrange("p e v -> p (e v)"), in_=pm1[:C, 256:384], func=AF.Copy), [pZ], [Zb])
                yield
                for e in range(2):
                    uc = slice(384 + e * 64, 384 + e * 64 + 64)
                    cx.op("pe", lambda h, e=e, uc=uc, rr=rr: h.matmul(pm1[:C, uc], lhsT=rr[:C, e, :C], rhs=Zb[:C, e, :], start=True, stop=True), [rr, Zb], [pU])
                cx.op("dve", lambda h: h.tensor_copy(out=Ub[:C].rearrange("p e v -> p (e v)"), in_=pm1[:C, 384:512]), [pU], [Ub])
                yield
                for e in range(2):
                    po = slice(64 * e, 64 * e + 64)
                    cx.op("pe", lambda h, e=e, po=po, cs=cs: h.matmul(pm2[po, 0:C], lhsT=Sbf[:, :], rhs=rt2[e][:, cs], start=True, stop=False), [Sbf, rt2[e]], [pY])
                    cx.op("pe", lambda h, e=e, po=po, ub=ub: h.matmul(pm2[po, 0:C], lhsT=Ub[:C, e, :], rhs=ub[:C, e, :C], start=False, stop=False), [Ub, ub], [pY])
                    cx.op("pe", lambda h, e=e, po=po, ub=ub, tk=tk: h.matmul(pm2[po, 0:C], lhsT=tk[:C, 0, po], rhs=ub[:C, 2 + e, :C], start=False, stop=True), [tk, ub], [pY])
                cx.op("act", lambda h, cs=cs: h.activation(out=ysb[:, cs], in_=pm2[:, 0:C], func=AF.Copy), [pY], [ysb])
                for e in range(2):
                    po = slice(64 * e, 64 * e + 64)
                    cx.op("pe", lambda h, e=e, po=po, tk=tk: h.matmul(pm2[po, 128:192], lhsT=tk[:C, 1, po], rhs=Ub[:C, e, :], start=True, stop=False), [tk, Ub], [pS])
                    cx.op("pe", lambda h, e=e, po=po, tk=tk: h.matmul(pm2[po, 128:192], lhsT=tk[:C, 2, po], rhs=tk[:C, 0, po], start=False, stop=True), [tk], [pS])
                cx.op("dve", lambda h, hp=hp: h.tensor_tensor(out=tmpS[:], in0=pm2[:, 128:192], in1=Sst[:, hp, :], op=ALU.add), [pS, Sst], [tmpS])
                gcol = (ch + 1) * C - 1
                cx.op("dve", lambda h, hp=hp, gcol=gcol: h.tensor_scalar(out=Sst[:, hp, :], in0=tmpS[:], scalar1=eg[:, gcol:gcol + 1], scalar2=None, op0=ALU.mult), [tmpS, eg], [Sst])


        def chunks():
            for c0 in range(0, nch, 2):
                chs = [c for c in (c0, c0 + 1) if c < nch]
                for idx, ch in enumerate(chs):
                    s2(ch, idx)
                    yield
                res = [[] for _ in chs]
                for idx in range(len(chs)):
                    yield from s3(idx, idx, res[idx])
                for idx, ch in enumerate(chs):
                    yield from s4(ch, idx, res[idx][0])
                    yield

        def S5():
            g_ap = r_in[:, 3, :]
            if True:
                cx.op("act", lambda h: h.activation(out=ysq[:], in_=ysb[:], func=AF.Square), [ysb], [ysq])
                cx.op("act", lambda h: h.activation(out=ysbb[:], in_=ysb[:], func=AF.Copy), [ysb], [ysbb])
                cx.op("pe", lambda h: h.matmul(pXv[0][:, :TT], lhsT=bones[:], rhs=ysbb[:], start=True, stop=True), [bones, ysbb], [pX0])
                yield
                cx.op("pe", lambda h: h.matmul(pXv[1][:, :TT], lhsT=bones[:], rhs=ysq[:], start=True, stop=True), [bones, ysq], [pX1])
                cx.op("dve", lambda h: h.tensor_scalar(out=mean[:], in0=pXv[0][:, :TT], scalar1=1.0 / 64, scalar2=None, op0=ALU.mult), [pX0], [mean])
                cx.op("dve", lambda h: h.tensor_tensor(out=m2[:], in0=mean[:], in1=mean[:], op=ALU.mult), [mean], [m2])
                yield
                cx.op("dve", lambda h: h.scalar_tensor_tensor(out=var[:], in0=pXv[1][:, :TT], scalar=1.0 / 64, in1=m2[:], op0=ALU.mult, op1=ALU.subtract), [pX1, m2], [var])
                cx.op("dve", lambda h: h.tensor_scalar(out=var[:], in0=var[:], scalar1=64e-5, scalar2=None, op0=ALU.add), [var], [var])
                cx.op("act", lambda h: h.activation(out=var[:], in_=var[:], func=AF.Ln), [var], [var])
                yield
                cx.op("act", lambda h: h.activation(out=var[:], in_=var[:], func=AF.Exp, scale=-0.5), [var], [var])
                cx.op("pool", lambda h: h.tensor_tensor(out=yc[:], in0=ysb[:], in1=mean[:], op=ALU.subtract), [ysb, mean], [yc])
                cx.op("dve", lambda h: h.tensor_tensor(out=yc[:], in0=yc[:], in1=var[:], op=ALU.mult), [yc, var], [yc])
                yield
                cx.op("act", lambda h, hp=hp: h.activation(out=yc[:], in_=yc[:], func=AF.Identity, scale=V(V_LG, hp), bias=V(V_LB, hp)), [yc, vecs], [yc])
                cx.op("pool", lambda h: h.tensor_tensor(out=yc[:], in0=yc[:], in1=bon[:], op=ALU.add), [yc, bon], [yc])
                cx.op("dve", lambda h, hp=hp, g_ap=g_ap: h.tensor_tensor(out=oT[:, hp, :], in0=yc[:], in1=g_ap, op=ALU.mult), [yc, r_in], [oT])
                yield
            yield
        return S1, chunks, S5

    gens = [make(hp) for hp in range(NKC)]
    _rr(gens[0][0]())
    for hp in range(NKC):
        def tail(hp=hp):
            if hp >= 1:
                yield from gens[hp - 1][2]()
            if hp + 1 < NKC:
                yield from gens[hp + 1][0]()
        _rr(gens[hp][1](), tail())
    _rr(gens[NKC - 1][2]())


def layer_b(cx, nc, job, A, O, vec, K, hrstd, phases):
    S, TT = job.S, job.TT
    ntile = S // TT
    nsub = (TT + 127) // 128
    P = min(TT, 128)
    identb, onesb, onesf = K["identb"], K["onesb"], K["onesf"]
    amask, gqs, vecs = K["amask"], K["gqs"], K["vecs"]
    nkt_cache = job.ncache
    nkt_own = (S + 127) // 128
    nkt = nkt_cache + nkt_own
    ktscr = K["ktscr"]
    with Scope() as eb:
        with Scope() as e0:
            kst = [cx.sb(e0, [128, 1024], BF16, "kst") for _ in range(2)]
            kts = [cx.sb(e0, [128, 8, 128], BF16, "kts") for _ in range(2)]
            ptk = [cx.ps(e0, [128, 8, 128], BF16, "ptk") for _ in range(2)]
            for kt in (range(nkt) if "6" in phases else ()):
                if kt < nkt_cache:
                    ksrc, rows, pr = A["ck"], slice(kt * 128, kt * 128 + 128), 128
                else:
                    o = kt - nkt_cache
                    pr = min(128, S - o * 128)
                    ksrc, rows = job.k_out.h, slice(o * 128, o * 128 + pr)
                ks_ = kst[kt % 2]
                pk = ptk[kt % 2]
                kb = kts[kt % 2]
                cx.dma("pool", ks_[:pr, :], ksrc[rows, :], ks_, writes=[ks_])
                for hd in range(8):
                    cx.op("pe", lambda h, hd=hd, ks_=ks_, pk=pk, pr=pr: h.transpose(out=pk[:, hd, :pr], in_=ks_[:pr, hd * 128:(hd + 1) * 128], identity=identb[:pr, :pr]), [ks_, identb], [pk])
                if kt % 2 == 0:
                    cx.op("dve", lambda h, pk=pk, kb=kb, pr=pr: h.tensor_copy(out=kb[:, :, :pr], in_=pk[:, :, :pr]), [pk], [kb])
                else:
                    cx.op("act", lambda h, pk=pk, kb=kb, pr=pr: h.activation(out=kb[:, :, :pr], in_=pk[:, :, :pr], func=AF.Copy), [pk], [kb])
                cx.dma("sp", ktscr[:, :, kt * 128:kt * 128 + pr].rearrange("h c t -> c h t"), kb[:, :, :pr], kb, reads=[kb], writes=[ktscr])
            cx.end_scope(e0)
        for t in range(ntile):
            tok0 = t * TT
            with Scope() as et:
                hnT = cx.sb(et, [128, NKC, TT], BF16, "hnT")
                oT = cx.sb(et, [128, NKC, TT], BF16, "oTb")
                with Scope() as e1:
                    if "7" in phases:
                        norm_transpose(cx, e1, job, lambda s, P: job.h_scr[tok0 + s * 128:tok0 + s * 128 + P, :], None, V_BG, hnT, 0,
                                       vecs, identb, compute_rstd=False, hrstd=hrstd, t=t)
                    cx.end_scope(e1)
                with Scope() as e2:
                    ring = [cx.sb(e2, [128, 8, 512], BF16, "wrb") for _ in range(3)]
                    KTh = [cx.sb(e2, [128, nkt * 128], BF16, "KTh") for _ in range(2)]
                    Vh = [cx.sb(e2, [128, nkt, 128], BF16, "Vh") for _ in range(2)]
                    qT = cx.sb(e2, [128, 12, TT], BF16, "qT")
                    sgT = cx.sb(e2, [128, 4, TT], BF16, "sgT")
                    qf = [cx.sb(e2, [128, TT], F32, "qf") for _ in range(4)]
                    sqf = [cx.sb(e2, [128, TT], BF16, "sqf") for _ in range(4)]
                    rq = [cx.sb(e2, [128, TT], F32, "rq") for _ in range(2)]
                    pe_ = [cx.sb(e2, [128, 4, P], BF16, "pexp") for _ in range(4)]
                    rden = cx.sb(e2, [128, 4, P], F32, "rden")
                    of = cx.sb(e2, [128, 4, P], F32, "of")
                    ppq = cx.ps(e2, [128, 4, 512], F32, "ppq")
                    pw = [cx.ps(e2, [128, 512], F32, "pw") for _ in range(2)]
                    pnum = cx.ps(e2, [128, 512], F32, "pnum")
                    pden = cx.ps(e2, [128, 512], F32, "pden")
                    ctr = [0]
                    nw = 0
                    nq = 0
                    npe = 0
                    nkeys = nkt_cache * 128 + S
                    for kvh in (range(8) if "8" in phases else ()):
                        kth, vh = KTh[kvh % 2], Vh[kvh % 2]
                        cx.dma("sp", kth[:, :nkeys], ktscr[kvh, :, :nkeys], kth, reads=[ktscr], writes=[kth])
                        if nkt_cache:
                            cx.dma("pool", vh[:, :nkt_cache, :], A["cv"].rearrange("(kt p) (h c) -> p kt h c", p=128, c=128)[:, :, kvh, :], vh, writes=[vh])
                        if S >= 128:
                            cx.dma("pool", vh[:, nkt_cache:, :], job.v_out.h.rearrange("(kt p) (h c) -> p kt h c", p=128, c=128)[:, :, kvh, :], vh, writes=[vh])
                        else:
                            cx.dma("pool", vh[:S, nkt_cache, :], job.v_out.h[:, kvh * 128:(kvh + 1) * 128], vh, writes=[vh])
                        for g in range(4):
                            col0 = (g * D + kvh * 512) if g < 3 else (3 * D + kvh * 512)
                            for k4 in range(4):
                                slot = stream_weights(cx, e2, A["b_w_in"], k4 * 1024, 1024, col0, 512, ring, ctr)
                                for oc in range(4):
                                    for kc in range(8):
                                        kk = k4 * 8 + kc
                                        cx.op("pe", lambda h, slot=slot, oc=oc, kc=kc, kk=kk: h.matmul(
                                            ppq[:, oc, :TT], lhsT=slot[:, kc, oc * 128:(oc + 1) * 128], rhs=hnT[:, kk, :],
                                            start=(kk == 0), stop=(kk == NKC - 1)), [slot, hnT], [ppq])
                            if g == 3:
                                cx.op("act", lambda h: h.activation(out=sgT[:, :, :], in_=ppq[:, :, :TT], func=AF.Silu), [ppq], [sgT])
                                continue
                            for oc in range(4):
                                q_, s_ = qf[oc], sqf[oc]
                                cx.op("act", lambda h, oc=oc, q_=q_: h.activation(out=q_[:, :], in_=ppq[:, oc, :TT], func=AF.Copy), [ppq], [q_])
                                cx.op("act", lambda h, oc=oc, s_=s_: h.activation(out=s_[:, :], in_=ppq[:, oc, :TT], func=AF.Square), [ppq], [s_])
                            for oc in range(4):
                                q_, s_, r_ = qf[oc], sqf[oc], rq[nq % 2]
                                nq += 1
                                pwt = pw[nw % 2]
                                nw += 1
                                cx.op("pe", lambda h, s_=s_, pwt=pwt: h.matmul(pwt[:, :TT], lhsT=onesb[:], rhs=s_[:, :], start=True, stop=True), [onesb, s_], [pwt])
                                cx.op("dve", lambda h, r_=r_, pwt=pwt: h.tensor_scalar(out=r_[:, :], in0=pwt[:, :TT], scalar1=1.0 / 128, scalar2=1e-6, op0=ALU.mult, op1=ALU.add), [pwt], [r_])
                                cx.op("act", lambda h, r_=r_: h.activation(out=r_[:, :], in_=r_[:, :], func=AF.Ln), [r_], [r_])
                                cx.op("act", lambda h, r_=r_: h.activation(out=r_[:, :], in_=r_[:, :], func=AF.Exp, scale=-0.5), [r_], [r_])
                                cx.op("dve", lambda h, q_=q_, r_=r_, g=g, oc=oc: h.scalar_tensor_tensor(out=qT[:, g * 4 + oc, :], in0=q_[:, :], scalar=gqs[:, g:g + 1], in1=r_[:, :], op0=ALU.mult, op1=ALU.mult), [q_, r_, gqs], [qT])
                        for s in range(nsub):
                            qa = nkt_cache + t * max(TT // 128, 1) + s
                            units = []
                            for g in range(3):
                                for dl in range(NDELTA[g]):
                                    ktile = qa - dl
                                    if ktile >= 0:
                                        units.append((g, dl, ktile))
                            qs = slice(s * 128, s * 128 + P)
                            for ui, (g, dl, ktile) in enumerate(units):
                                kw = 128
                                if ktile >= nkt_cache:
                                    kw = min(128, S - (ktile - nkt_cache) * 128)
                                pwt = pw[nw % 2]
                                nw += 1
                                pb = pe_[npe % 4]
                                npe += 1
                                mi = MOFF[g] + dl
                                cx.op("pe", lambda h, pwt=pwt, ktile=ktile, kw=kw, g=g, qs=qs, kth=kth: h.matmul(
                                    pwt[:kw, 0:4 * P].rearrange("p (a b) -> p a b", a=4), lhsT=kth[:, ktile * 128:ktile * 128 + kw],
                                    rhs=qT[:, g * 4:g * 4 + 4, qs], start=True, stop=True), [kth, qT], [pwt])
                                cx.op("act", lambda h, pwt=pwt, pb=pb, kw=kw: h.activation(out=pb[:kw].rearrange("p a b -> p (a b)"), in_=pwt[:kw, 0:4 * P], func=AF.Exp), [pwt], [pb])
                                meng = "dve"
                                cx.op(meng, lambda h, pb=pb, kw=kw, mi=mi: h.tensor_tensor(out=pb[:kw], in0=pb[:kw], in1=amask[:kw, mi, :P].unsqueeze(1).to_broadcast([kw, 4, P]), op=ALU.mult), [pb, amask], [pb])
                                first, lastu = (ui == 0), (ui == len(units) - 1)
                                cx.op("pe", lambda h, pb=pb, kw=kw, ktile=ktile, vh=vh, first=first, lastu=lastu: h.matmul(
                                    pnum[:, 0:4 * P], lhsT=vh[:kw, ktile, :], rhs=pb[:kw].rearrange("p a b -> p (a b)"),
                                    start=first, stop=lastu), [vh, pb], [pnum])
                                cx.op("pe", lambda h, pb=pb, kw=kw, first=first, lastu=lastu: h.matmul(
                                    pden[:, 0:4 * P], lhsT=onesb[:kw, :], rhs=pb[:kw].rearrange("p a b -> p (a b)"),
                                    start=first, stop=lastu), [onesb, pb], [pden])
                            cx.op("act", lambda h: h.activation(out=rden[:].rearrange("p a b -> p (a b)"), in_=pden[:, 0:4 * P], func=AF.Ln), [pden], [rden])
                            cx.op("act", lambda h: h.activation(out=rden[:].rearrange("p a b -> p (a b)"), in_=rden[:].rearrange("p a b -> p (a b)"), func=AF.Exp, scale=-1.0), [rden], [rden])
                            cx.op("dve", lambda h: h.tensor_tensor(out=of[:].rearrange("p a b -> p (a b)"), in0=pnum[:, 0:4 * P], in1=rden[:].rearrange("p a b -> p (a b)"), op=ALU.mult), [pnum, rden], [of])
                            cx.op("dve", lambda h, kvh=kvh, qs=qs: h.tensor_tensor(out=oT[:, kvh * 4:kvh * 4 + 4, qs], in0=of[:], in1=sgT[:, :, qs], op=ALU.mult), [of, sgT], [oT])
                    cx.end_scope(e2)
                with Scope() as e3:
                    ring = [cx.sb(e3, [128, 8, 512], BF16, "wr7") for _ in range(4)]
                    xres = [cx.sb(e3, [128, 512], F32, "hres") for _ in range(4)]
                    hsb = [cx.sb(e3, [128, 512], F32, "ysb") for _ in range(4)]
                    pp = [cx.ps(e3, [128, 4, 512], F32, "pp7") for _ in range(2)]
                    ctr = [0]
                    n = 0
                    for cg in (range(8) if "9" in phases else ()):
                        ppt = pp[cg % 2]
                        for k4 in range(4):
                            slot = stream_weights(cx, e3, A["b_w_out"], k4 * 1024, 1024, cg * 512, 512, ring, ctr)
                            for s in range(nsub):
                                for kc in range(8):
                                    kk = k4 * 8 + kc
                                    cx.op("pe", lambda h, ppt=ppt, slot=slot, s=s, kc=kc, kk=kk: h.matmul(
                                        ppt[:P, s, :], lhsT=oT[:, kk, s * 128:s * 128 + P], rhs=slot[:, kc, :],
                                        start=(kk == 0), stop=(kk == NKC - 1)), [slot, oT], [ppt])
                        for s in range(nsub):
                            xr, hb = xres[n % 4], hsb[n % 4]
                            n += 1
                            rows = slice(tok0 + s * 128, tok0 + s * 128 + P)
                            cx.dma("sp", xr[:P, :], job.h_scr[rows, cg * 512:(cg + 1) * 512], xr, reads=[job.h_scr], writes=[xr])
                            cx.op("dve", lambda h, ppt=ppt, s=s, xr=xr, hb=hb: h.tensor_tensor(out=hb[:P, :], in0=ppt[:P, s, :], in1=xr[:P, :], op=ALU.add), [ppt, xr], [hb])
                            cx.dma("sp", job.y[rows, cg * 512:(cg + 1) * 512], hb[:P, :], hb, reads=[hb], writes=[job.y])
                    cx.end_scope(e3)
                cx.end_scope(et)
        cx.end_scope(eb)


def _consts():
    bf = ml_dtypes.bfloat16
    identf = np.eye(128, dtype=np.float32)
    bones = np.zeros((128, 128), np.float32)
    bones[:64, :64] = 1
    bones[64:, 64:] = 1
    i = np.arange(128)
    su = (i[:, None] < i[None, :]).astype(np.float32)
    iu = (i[:, None] <= i[None, :]).astype(np.float32)
    sl = (i[None, :] < i[:, None]).astype(np.float32)
    rmask = np.stack([np.stack([m] * 4, 1) for m in (su, iu, sl)], 1)
    am = np.zeros((128, 24, 128), np.float32)
    for g in range(3):
        for dl in range(NDELTA[g]):
            d = 128 * dl + i[None, :] - i[:, None]
            ok = (d >= 0) & (d % DILS[g] == 0) & (d // DILS[g] <= 128)
            am[:, MOFF[g] + dl, :] = ok
    return dict(identf=identf, bones=bones, rmask=np.ascontiguousarray(rmask), amask=am)


def _fm(v):
    return np.ascontiguousarray(np.asarray(v, np.float32).reshape(NKC, 128).T)


_CACHE = {}


def kernel(x_prompt, x_sample, state_wkv, state_shift, cache_k, cache_v,
           a_norm_g, a_mu, a_w_in, a_w0, a_w1, a_w2, a_a0, a_a1, a_a2,
           a_k_k, a_k_a, a_r_k, a_lnx_g, a_lnx_b, a_w_out,
           kv_norm_g, w_kv, k_norm_g, b_norm_g, b_w_in, q_norm_g, b_w_out, _SP=None, _NC=8, _PH="AB0123456789", _DP=True, _DS=True):
    f = lambda a: np.ascontiguousarray(np.asarray(a, np.float32))
    B, S, _ = x_prompt.shape
    SP = _SP or S
    key = (SP, _PH, _DP, _DS)
    if key not in _CACHE:
        _CACHE[key] = build_program(SP, phases=_PH, do_prompt=_DP, do_sample=_DS)
    nc = _CACHE[key]
    cs = _consts()
    vlist = [a_norm_g[0]] + [a_mu[0, j] for j in range(6)] + [a_w0[0], a_a0[0], a_k_k[0], a_k_a[0],
             np.asarray(a_r_k[0]).reshape(-1), a_lnx_g[0], a_lnx_b[0], kv_norm_g, b_norm_g[0]]
    vecs = np.ascontiguousarray(np.stack([_fm(v) for v in vlist], 1))
    shared = dict(
        a_w_in=f(a_w_in[0]), a_w1=f(a_w1[0]), a_w2=f(a_w2[0]), a_a1=f(a_a1[0]), a_a2=f(a_a2[0]),
        a_w_out=f(a_w_out[0]), w_kv=f(w_kv), b_w_in=f(b_w_in[0]), b_w_out=f(b_w_out[0]),
        vecs=vecs, grow=f(a_norm_g[0]), gq=np.ascontiguousarray(f(q_norm_g[0]).T),
        gkb=np.ascontiguousarray(np.broadcast_to(f(k_norm_g)[None, :], (128, 128))),
        identf=cs["identf"], bones=cs["bones"], rmask=cs["rmask"], amask=cs["amask"])
    in_maps = []
    for c in range(_NC):
        m = dict(shared)
        m["xp"] = f(x_prompt[c % B, :SP])
        m["xs"] = f(x_sample[c])
        m["swkv"] = f(state_wkv[0, c]).reshape(D, 64)
        m["sshift"] = _fm(state_shift[0, c])
        m["ck"] = f(cache_k[c]).reshape(-1, 1024)
        m["cv"] = f(cache_v[c]).reshape(-1, 1024)
        in_maps.append(m)
    res = run_bass_kernel_spmd(nc, in_maps, core_ids=list(range(_NC)))
    R = list(res.results)
    while len(R) < 8:
        R.append(R[0])
    y_p = np.stack([R[b]["yp"] for b in range(B)])
    y_s = np.stack([R[c]["ys"] for c in range(8)])
    wkv_p = np.stack([R[b]["wkvp"] for b in range(B)])[None]
    sh_p = np.stack([R[b]["shiftp"] for b in range(B)])[None]
    k_p = np.stack([R[b]["kp"].reshape(SP, 8, 128) for b in range(B)])
    v_p = np.stack([R[b]["vp"].reshape(SP, 8, 128) for b in range(B)])
    wkv_s = np.stack([R[c]["wkvs"] for c in range(8)])[None]
    sh_s = np.stack([R[c]["shifts"] for c in range(8)])[None]
    k_s = np.stack([R[c]["ks"].reshape(8, 8, 128) for c in range(8)])
    v_s = np.stack([R[c]["vs"].reshape(8, 8, 128) for c in range(8)])
    return (y_p, y_s, wkv_p, sh_p, k_p, v_p, wkv_s, sh_s, k_s, v_s)
```

```python
import os
import numpy as np
import ml_dtypes
A2CUT = int(os.environ.get("A2CUT", "9"))
from contextlib import ExitStack
import concourse.bass as bass
import concourse.mybir as mybir
from concourse.bass_utils import run_bass_kernel_spmd

F32 = mybir.dt.float32
BF16 = mybir.dt.bfloat16
AF = mybir.ActivationFunctionType
ALU = mybir.AluOpType
AX = mybir.AxisListType

D = 4096
NKC = 32
C0 = float(np.exp(-0.5))
V_G, V_MU, V_W0, V_A0, V_KK, V_KA, V_RK, V_LG, V_LB, V_KVG, V_BG = 0, 1, 7, 8, 9, 10, 11, 12, 13, 14, 15
NV = 16
DILS = (1, 4, 16)
NDELTA = (2, 5, 17)
MOFF = (0, 2, 7)


class T:
    __slots__ = ("h", "name", "lw", "rd", "dsem", "dcnt")

    def __init__(self, h, name):
        self.h = h
        self.name = name
        self.lw = None
        self.rd = {}
        self.dsem = None
        self.dcnt = 0

    def __getitem__(self, k):
        return self.h[k]


class Eng:
    def __init__(self, name, sem):
        self.name = name
        self.sem = sem
        self.cnt = 0
        self.epoch = 0
        self.waited = {}
        self.prog = []


class _Rec:
    def __init__(self):
        self.call = None

    def __getattr__(self, name):
        def f(*args, **kw):
            self.call = (name, args, kw)
        return f


class Scope(ExitStack):
    def __init__(self):
        super().__init__()
        self.tiles = []


class Ctx:
    def __init__(self, nc, es):
        self.free_dsems = []
        self.nc = nc
        self.es = es
        self.eng = {}
        for name in ("pe", "act", "dve", "pool", "sp"):
            sem = es.enter_context(nc.semaphore("s_" + name))
            self.eng[name] = Eng(name, sem)
        self.nt = 0
        self.dsems = []

    def sb(self, es, shape, dt, name):
        self.nt += 1
        h = es.enter_context(self.nc.sbuf_tensor(f"{name}_{self.nt}", list(shape), dt))
        t = T(h, f"{name}_{self.nt}")
        if isinstance(es, Scope):
            es.tiles.append(t)
        return t

    def ps(self, es, shape, dt, name):
        self.nt += 1
        h = es.enter_context(self.nc.psum_tensor(f"{name}_{self.nt}", list(shape), dt))
        return T(h, f"{name}_{self.nt}")

    def view(self, h, name):
        self.nt += 1
        return T(h, f"{name}_{self.nt}")

    def _need(self, e, reads, writes):
        need = {}

        def add(ev):
            if ev is None:
                return
            kind, key, val = ev
            k = (kind, key if kind == "e" else key.num)
            if k not in need or need[k][2] < val:
                need[k] = ev
        for t in reads:
            add(t.lw)
        for t in writes:
            add(t.lw)
            for ev in t.rd.values():
                add(ev)
        for k, ev in need.items():
            kind, key, val = ev
            if kind == "e":
                if key[1] != self.eng[key[0]].epoch:
                    continue
                if key[0] == "pe" and e.name == "pe":
                    continue
                if e.waited.get(k, 0) >= val:
                    continue
                e.waited[k] = val
                e.prog.append(("w", self.eng[key[0]].sem, val))
                continue
            if e.waited.get(k, 0) >= val:
                continue
            e.waited[k] = val
            e.prog.append(("w", self.eng[key].sem if kind == "e" else key, val))

    def op(self, ename, fn, reads=(), writes=()):
        e = self.eng[ename]
        self._need(e, reads, writes)
        e.cnt += 1
        rec = _Rec()
        fn(rec)
        name, args, kw = rec.call
        e.prog.append(("i", (lambda h, name=name, args=args, kw=kw: getattr(h, name)(*args, **kw)), e.sem, 1))
        ev = ("e", (ename, e.epoch), e.cnt)
        for t in reads:
            t.rd[("e", ename)] = ev
        for t in writes:
            t.lw = ev
            t.rd = {}

    def dma(self, qname, out, in_, anchor, reads=(), writes=()):
        e = self.eng[qname]
        self._need(e, reads, writes)
        if anchor.dsem is None:
            if self.free_dsems:
                anchor.dsem, anchor.dcnt = self.free_dsems.pop()
            else:
                anchor.dsem = self.es.enter_context(self.nc.semaphore("d_" + anchor.name))
            self.dsems.append(anchor)
        e.prog.append(("i", (lambda h, out=out, in_=in_: h.dma_start(out=out, in_=in_)), anchor.dsem, 16))
        anchor.dcnt += 16
        assert anchor.dcnt < 60000
        ev = ("d", anchor.dsem, anchor.dcnt)
        for t in reads:
            t.rd[("d", anchor.dsem.num)] = ev
        for t in writes:
            t.lw = ev
            t.rd = {}

    def barrier(self):
        for e in self.eng.values():
            for o in self.eng.values():
                if o is e or o.cnt == 0:
                    continue
                k = ("e", (o.name, o.epoch))
                if e.waited.get(k, 0) >= o.cnt:
                    continue
                e.waited[k] = o.cnt
                e.prog.append(("w", o.sem, o.cnt))
            for a in self.dsems:
                k = ("d", a.dsem.num)
                if e.waited.get(k, 0) >= a.dcnt:
                    continue
                e.waited[k] = a.dcnt
                e.prog.append(("w", a.dsem, a.dcnt))

    def new_epoch(self):
        for e in self.eng.values():
            e.sem = self.es.enter_context(self.nc.semaphore(f"s_{e.name}_{e.epoch + 1}"))
            e.cnt = 0
            e.epoch += 1
            e.waited = {k: v for k, v in e.waited.items() if k[0] != "e"}

    def end_scope(self, sc):
        self.barrier()
        if max(e.cnt for e in self.eng.values()) > 30000:
            self.new_epoch()
        for t in sc.tiles:
            if t.dsem is not None:
                self.free_dsems.append((t.dsem, t.dcnt))
                self.dsems.remove(t)
                t.dsem = None

    def emit(self):
        nc = self.nc
        handles = {"pe": "tensor", "act": "scalar", "dve": "vector", "pool": "gpsimd", "sp": "sync"}
        with nc.Block() as block:
            def mk(e):
                def body(h):
                    pend = []
                    for a in e.prog:
                        if a[0] == "w":
                            pend.append(a)
                            continue
                        for w in pend[:-1]:
                            h.wait_ge(w[1], w[2])
                        ins = a[1](h)
                        if pend:
                            ins._wait_ge(pend[-1][1], pend[-1][2])
                        ins.then_inc(a[2], a[3])
                        pend = []
                    for w in pend:
                        h.wait_ge(w[1], w[2])
                return body
            for n, attr in handles.items():
                getattr(block, attr)(mk(self.eng[n]))


class Job:
    pass


def build_program(SP, do_sample=True, SC=2048, phases="AB0123456789", do_prompt=True):
    nc = bass.Bass("TRN2", target_bir_lowering=False)

    def din(name, shape, dt=F32):
        return nc.dram_tensor(name, list(shape), dt, kind="ExternalInput").ap()

    def dout(name, shape, dt=F32):
        return nc.dram_tensor(name, list(shape), dt, kind="ExternalOutput").ap()

    def dscr(name, shape, dt=F32):
        return nc.dram_tensor(name, list(shape), dt).ap()

    A = {}
    A["xp"] = din("xp", [SP, D])
    A["xs"] = din("xs", [8, D])
    A["swkv"] = din("swkv", [D, 64])
    A["sshift"] = din("sshift", [128, NKC])
    A["ck"] = din("ck", [SC, 1024])
    A["cv"] = din("cv", [SC, 1024])
    A["a_w_in"] = din("a_w_in", [D, 4 * D])
    A["a_w1"] = din("a_w1", [D, 128])
    A["a_w2"] = din("a_w2", [128, D])
    A["a_a1"] = din("a_a1", [D, 128])
    A["a_a2"] = din("a_a2", [128, D])
    A["a_w_out"] = din("a_w_out", [D, D])
    A["w_kv"] = din("w_kv", [D, 2048])
    A["b_w_in"] = din("b_w_in", [D, 4 * D])
    A["b_w_out"] = din("b_w_out", [D, D])
    A["vecs"] = din("vecs", [128, NV, NKC])
    A["grow"] = din("grow", [D])
    A["gq"] = din("gq", [128, 3])
    A["gkb"] = din("gkb", [128, 128])
    A["identf"] = din("identf", [128, 128])
    A["bones"] = din("bones", [128, 128])
    A["rmask"] = din("rmask", [128, 3, 4, 128])
    A["amask"] = din("amask", [128, 24, 128])

    O = {}
    O["yp"] = dout("yp", [SP, D])
    O["ys"] = dout("ys", [8, D])
    O["wkvp"] = dout("wkvp", [64, 64, 64])
    O["shiftp"] = dout("shiftp", [D])
    O["kp"] = dout("kp", [SP, 1024])
    O["vp"] = dout("vp", [SP, 1024])
    O["wkvs"] = dout("wkvs", [64, 64, 64])
    O["shifts"] = dout("shifts", [D])
    O["ks"] = dout("ks", [8, 1024])
    O["vs"] = dout("vs", [8, 1024])

    WSC.clear()
    WDONE.clear()
    for wn in ("a_w_in", "a_w_out", "w_kv", "b_w_in", "b_w_out"):
        WSC[wn] = dscr(wn + "_bf16", [int(A[wn].shape[0]) * int(A[wn].shape[1]) // (128 * 4096), 128, 4096], BF16)
    with ExitStack() as es:
        cx = Ctx(nc, es)
        vecs = cx.sb(es, [128, NV, NKC], F32, "vecs")
        identf = cx.sb(es, [128, 128], F32, "identf")
        identb = cx.sb(es, [128, 128], BF16, "identb")
        bones = cx.sb(es, [128, 128], BF16, "bones")
        onesb = cx.sb(es, [128, 128], BF16, "onesb")
        onesf = cx.sb(es, [128, 128], F32, "onesf")
        rmask = cx.sb(es, [128, 3, 4, 128], BF16, "rmask")
        amask = cx.sb(es, [128, 24, 128], BF16, "amask")
        gqs = cx.sb(es, [128, 3], F32, "gqs")
        gkb = cx.sb(es, [128, 128], F32, "gkb")
        ident2 = cx.sb(es, [128, 4, 128], BF16, "ident2")
        cx.dma("sp", vecs[:], A["vecs"], vecs, writes=[vecs])
        cx.dma("sp", identf[:], A["identf"], identf, writes=[identf])
        cx.dma("pool", bones[:], A["bones"], bones, writes=[bones])
        cx.dma("pool", rmask[:], A["rmask"], rmask, writes=[rmask])
        cx.dma("pool", amask[:], A["amask"], amask, writes=[amask])
        cx.dma("sp", gqs[:], A["gq"], gqs, writes=[gqs])
        cx.dma("sp", gkb[:], A["gkb"], gkb, writes=[gkb])
        cx.op("dve", lambda h: h.tensor_copy(out=identb[:], in_=identf[:]), [identf], [identb])
        cx.op("dve", lambda h: h.memset(onesb[:], 1.0), [], [onesb])
        cx.op("dve", lambda h: h.memset(onesf[:], 1.0), [], [onesf])
        cx.op("dve", lambda h: h.tensor_scalar(out=gqs[:], in0=gqs[:], scalar1=float(128 ** -0.5), scalar2=None, op0=ALU.mult), [gqs], [gqs])
        for i in range(4):
            cx.op("dve", lambda h, i=i: h.tensor_copy(out=ident2[:, i, :], in_=identf[:]), [identf], [ident2])

        def vec(idx, c):
            return vecs[:, idx, c:c + 1]

        scr_h_p = cx.view(dscr("scr_h_p", [SP, D]), "scr_h_p")
        scr_h_s = cx.view(dscr("scr_h_s", [8, D]), "scr_h_s")
        scr_p = cx.view(dscr("scr_p", [4, D, 512]), "scr_p")
        ktscr = cx.view(dscr("ktscr", [8, 128, max(SP, SC + 128)], BF16), "ktscr")
        yp_t = cx.view(O["yp"], "yp")
        ys_t = cx.view(O["ys"], "ys")
        kp_t = cx.view(O["kp"], "kp")
        vp_t = cx.view(O["vp"], "vp")
        ks_t = cx.view(O["ks"], "ks")
        vs_t = cx.view(O["vs"], "vs")
        outs_misc = cx.view(O["wkvp"], "misc")

        jobs = []
        jp = Job()
        jp.name, jp.S, jp.TT, jp.C = "p", SP, 512, 128
        jp.x, jp.h_scr, jp.y = A["xp"], scr_h_p, yp_t
        jp.k_out, jp.v_out = kp_t, vp_t
        jp.wkv_out, jp.shift_out = O["wkvp"], O["shiftp"]
        jp.has_state = False
        jp.ncache = 0
        if do_prompt:
            jobs.append(jp)
        if do_sample:
            js = Job()
            js.name, js.S, js.TT, js.C = "s", 8, 8, 8
            js.x, js.h_scr, js.y = A["xs"], scr_h_s, ys_t
            js.k_out, js.v_out = ks_t, vs_t
            js.wkv_out, js.shift_out = O["wkvs"], O["shifts"]
            js.has_state = True
            js.ncache = SC // 128
            jobs.append(js)

        wq = [0]

        for job in jobs:
            run_job(cx, nc, es, job, A, O, vec, dict(
                vecs=vecs, identf=identf, identb=identb, bones=bones, onesb=onesb, onesf=onesf,
                rmask=rmask, amask=amask, gqs=gqs, gkb=gkb, ident2=ident2, scr_p=scr_p, ktscr=ktscr,
                outs_misc=outs_misc), phases)

        cx.barrier()
        cx.emit()
    return nc


def run_job(cx, nc, es_glob, job, A, O, vec, K, phases):
    S, TT, C = job.S, job.TT, job.C
    ntile = S // TT
    nsub = (TT + 127) // 128
    P = min(TT, 128)
    nch = TT // C
    nst = {128: 6, 8: 2}[C]
    identb, identf, bones, onesb, onesf = K["identb"], K["identf"], K["bones"], K["onesb"], K["onesf"]
    rmask, amask, gqs, gkb, ident2, scr_p, vecs = K["rmask"], K["amask"], K["gqs"], K["gkb"], K["ident2"], K["scr_p"], K["vecs"]

    with Scope() as ej:
        Sst = cx.sb(ej, [128, NKC, 64], F32, "Sst")
        hrstd = cx.sb(ej, [128, 16], F32, "hrstd")
        prevcol = cx.sb(ej, [128, NKC, 1], BF16, "prevcol")
        if job.has_state:
            with Scope() as e0:
                zp = cx.sb(e0, [128, NKC, 128], F32, "zp")
                pst = cx.ps(e0, [128, 4, 128], F32, "pst")
                sh = cx.sb(e0, [128, NKC], F32, "sh")
                cx.op("dve", lambda h: h.memset(zp[:], 0.0), [], [zp])
                src = A["swkv"].rearrange("(hp e v) k -> e v hp k", e=2, v=64)
                cx.dma("sp", zp[0:64, :, 0:64], src[0], zp, writes=[zp])
                cx.dma("sp", zp[64:128, :, 64:128], src[1], zp, writes=[zp])
                for hp in range(NKC):
                    j = hp % 4
                    cx.op("pe", lambda h, hp=hp, j=j: h.transpose(out=pst[:, j, :], in_=zp[:, hp, :], identity=identf[:]), [zp, identf], [pst])
                    if j == 3:
                        h0 = hp - 3
                        cx.op("dve", lambda h, h0=h0: h.tensor_copy(out=Sst[0:64, h0:h0 + 4, :], in_=pst[0:64, :, 0:64]), [pst], [Sst])
                        cx.op("act", lambda h, h0=h0: h.activation(out=Sst[64:128, h0:h0 + 4, :], in_=pst[64:128, :, 64:128], func=AF.Copy), [pst], [Sst])
                cx.dma("sp", sh[:], A["sshift"], sh, writes=[sh])
                cx.op("dve", lambda h: h.tensor_copy(out=prevcol[:, :, 0], in_=sh[:]), [sh], [prevcol])
                cx.end_scope(e0)
        else:
            cx.op("dve", lambda h: h.memset(Sst[:], 0.0), [], [Sst])
            cx.op("dve", lambda h: h.memset(prevcol[:], 0.0), [], [prevcol])

        if "A" in phases:
            for t in range(ntile):
                layer_a_tile(cx, nc, job, t, A, O, vec, K, Sst, hrstd, prevcol, phases)
            with Scope() as e0:
                pst = cx.ps(e0, [64, 4, 128], F32, "pso")
                so = cx.sb(e0, [64, NKC, 128], F32, "so")
                for hp in range(NKC):
                    j = hp % 4
                    cx.op("pe", lambda h, hp=hp, j=j: h.transpose(out=pst[:, j, :], in_=Sst[:, hp, :], identity=identf[:]), [Sst, identf], [pst])
                    if j == 3:
                        h0 = hp - 3
                        cx.op("dve", lambda h, h0=h0: h.tensor_copy(out=so[:, h0:h0 + 4, :], in_=pst[:]), [pst], [so])
                dst = job.wkv_out.rearrange("(hp e) v k -> v hp e k", e=2)
                cx.dma("sp", dst, so[:].rearrange("v hp (e k) -> v hp e k", e=2), so, reads=[so], writes=[K["outs_misc"]])
                cx.end_scope(e0)

        if "B" in phases:
            layer_b(cx, nc, job, A, O, vec, K, hrstd, phases)
        cx.end_scope(ej)


def rms_rstd(cx, ss_in, out, P, scale, eps, reads_extra=()):
    st, sap = ss_in
    ot, oap = out
    cx.op("dve", lambda h: h.tensor_scalar(out=oap, in0=sap, scalar1=scale, scalar2=eps, op0=ALU.mult, op1=ALU.add), [st], [ot])
    cx.op("act", lambda h: h.activation(out=oap, in_=oap, func=AF.Sqrt), [ot], [ot])
    cx.op("dve", lambda h: h.reciprocal(out=oap, in_=oap), [ot], [ot])


def norm_transpose(cx, es, job, src_rows, rstd_ap_fn, gidx, xnT, col0, vecs, identb, last_out=None, grow=None, compute_rstd=True, hrstd=None, t=0):
    TT = job.TT
    nsub = (TT + 127) // 128
    P = min(TT, 128)
    nb = min(nsub, 4)
    xt2 = [cx.sb(es, [128, D], F32, "xt") for _ in range(nb)]
    xs2 = [cx.sb(es, [128, D], BF16, "xs") for _ in range(2)]
    junk = cx.sb(es, [128, D], BF16, "junk")
    ss = cx.sb(es, [128, 4], F32, "ss")
    rs = cx.sb(es, [128, 4], F32, "rs")
    pt2 = [cx.ps(es, [128, 4, 128], BF16, "pt") for _ in range(2)]
    for s in range(nsub):
        xt, xs = xt2[s % nb], xs2[s % 2]
        cx.dma("sp", xt[:P, :], src_rows(s, P), xt, writes=[xt])
        if compute_rstd:
            cx.op("act", lambda h, xt=xt, s=s: h.activation(out=junk[:P, :], in_=xt[:P, :], func=AF.Square, accum_out=ss[:P, s:s + 1]), [xt], [junk, ss])
            rms_rstd(cx, (ss, ss[:P, s:s + 1]), (rs, rs[:P, s:s + 1]), P, 1.0 / D, 1e-6)
            rap, rt = rs[:P, s:s + 1], rs
        else:
            rap, rt = hrstd[:P, t * 4 + s:t * 4 + s + 1], hrstd
        cx.op("act", lambda h, xt=xt, xs=xs, rap=rap: h.activation(out=xs[:P, :], in_=xt[:P, :], func=AF.Identity, scale=rap), [xt, rt], [xs])
        if last_out is not None and s == nsub - 1:
            xnf = cx.sb(es, [128, D], F32, "xnf")
            gbc = cx.sb(es, [128, D], F32, "gbc")
            cx.dma("sp", gbc[:P, :], grow.partition_broadcast(P), gbc, writes=[gbc])
            cx.op("act", lambda h, xt=xt, rap=rap: h.activation(out=xnf[:P, :], in_=xt[:P, :], func=AF.Identity, scale=rap), [xt, rt], [xnf])
            cx.op("dve", lambda h: h.tensor_tensor(out=xnf[:P, :], in0=xnf[:P, :], in1=gbc[:P, :], op=ALU.mult), [xnf, gbc], [xnf])
            cx.dma("sp", last_out.rearrange("(o d) -> o d", o=1), xnf[P - 1:P, :], xnf, reads=[xnf])
        for c4 in range(8):
            p = pt2[c4 % 2]
            for j in range(4):
                c = c4 * 4 + j
                cx.op("pe", lambda h, c=c, j=j, p=p, xs=xs: h.transpose(out=p[:, j, :P], in_=xs[:P, c * 128:(c + 1) * 128], identity=identb[:P, :P]), [xs, identb], [p])
            cx.op("dve", lambda h, c4=c4, p=p, s=s: h.tensor_tensor(
                out=xnT[:, c4 * 4:c4 * 4 + 4, col0 + s * 128:col0 + s * 128 + P], in0=p[:, :, :P],
                in1=vecs[:, gidx, c4 * 4:c4 * 4 + 4].unsqueeze(2).to_broadcast([128, 4, P]), op=ALU.mult), [p, vecs], [xnT])


WSC = {}
WDONE = {}


def stream_weights(cx, es, W, row0, nrows, col0, ncols, ring, ctr):
    slot = ring[ctr[0] % len(ring)]
    ctr[0] += 1
    nk = nrows // 128
    name = W.tensor.name
    key = (name, row0, nrows, col0, ncols)
    w16 = WSC[name]
    assert nk * ncols == 4096 and tuple(slot.h.shape) == (128, nk, ncols)
    sflat = slot.h[:].rearrange("p a b -> p (a b)")
    if key not in WDONE:
        gid = sum(1 for k in WDONE if k[0] == name)
        WDONE[key] = gid
        cx.dma("pool", slot[:, :nk, :ncols], W[row0:row0 + nrows, col0:col0 + ncols].rearrange("(kc p) n -> p kc n", p=128), slot, writes=[slot])
        cx.dma("act", w16[gid], sflat, slot, reads=[slot])
    else:
        cx.dma("pool", sflat, w16[WDONE[key]], slot, writes=[slot])
    return slot


def layer_a_tile(cx, nc, job, t, A, O, vec, K, Sst, hrstd, prevcol, phases):
    S, TT, C = job.S, job.TT, job.C
    nsub = (TT + 127) // 128
    P = min(TT, 128)
    nch = TT // C
    nst = {128: 6, 8: 2}[C]
    tok0 = t * TT
    ntile = S // TT
    identb, identf, bones, onesb, onesf = K["identb"], K["identf"], K["bones"], K["onesb"], K["onesf"]
    rmask, ident2, scr_p, vecs = K["rmask"], K["ident2"], K["scr_p"], K["vecs"]
    last = (t == ntile - 1)

    with Scope() as et:
      tanhT = cx.sb(et, [128, TT], BF16, "tanhT")
      ahT = cx.sb(et, [128, TT], BF16, "ahT")
      with Scope() as ex:
        xnT = cx.sb(ex, [128, NKC, TT + 1], BF16, "xnT")
        with Scope() as e0:
          if "0" in phases:
            cx.op("dve", lambda h: h.tensor_copy(out=xnT[:, :, 0:1], in_=prevcol[:]), [prevcol], [xnT])
            norm_transpose(cx, e0, job, lambda s, P: job.x[tok0 + s * 128:tok0 + s * 128 + P, :], None, V_G, xnT, 1,
                           vecs, identb, last_out=(job.shift_out if last else None), grow=A["grow"])
            cx.op("dve", lambda h: h.tensor_copy(out=prevcol[:], in_=xnT[:, :, TT:TT + 1]), [xnT], [prevcol])
          cx.end_scope(e0)
        with Scope() as e1:
            mixb = [cx.sb(e1, [128, NKC, TT], BF16, "mix") for _ in range(2)]
            tmpd = [cx.sb(e1, [128, TT], F32, "tmpd") for _ in range(3)]
            pp = [cx.ps(e1, [128, 2, 512], F32, "pp") for _ in range(3)]
            pl = cx.ps(e1, [128, 512], F32, "pl")
            ctr = [0]
            ncg = 0
            e1a = Scope()
            e1a.__enter__()
            w1sb = cx.sb(e1a, [128, NKC, 128], BF16, "w1sb")
            a1sb = cx.sb(e1a, [128, NKC, 128], BF16, "a1sb")
            cx.dma("pool", w1sb[:], A["a_w1"].rearrange("(kc p) n -> p kc n", p=128), w1sb, writes=[w1sb])
            cx.dma("pool", a1sb[:], A["a_a1"].rearrange("(kc p) n -> p kc n", p=128), a1sb, writes=[a1sb])
            ring = stg = None
            for j in ((4, 5, 0, 1, 2, 3) if "1" in phases else ()):
                mix = mixb[j % 2]
                if j == 0:
                    cx.end_scope(e1a)
                    e1a.__exit__(None, None, None)
                    ring = [cx.sb(e1, [128, 16, 256], BF16, "wr") for _ in range(4)]
                    stg = [cx.sb(e1, [128, 2, TT], F32, "stg") for _ in range(2)]
                for c in range(NKC):
                    td = tmpd[c % 3]
                    cx.op("pool" if c % 2 == 0 else "dve", lambda h, c=c, td=td: h.tensor_tensor(out=td[:, :TT], in0=xnT[:, c, 0:TT], in1=xnT[:, c, 1:TT + 1], op=ALU.subtract), [xnT], [td])
                    cx.op("dve", lambda h, c=c, td=td, mix=mix, j=j: h.scalar_tensor_tensor(
                        out=mix[:, c, :], in0=td[:, :TT], scalar=vecs[:, V_MU + j, c:c + 1], in1=xnT[:, c, 1:TT + 1],
                        op0=ALU.mult, op1=ALU.add), [td, xnT, vecs], [mix])
                if j >= 4:
                    wsb = w1sb if j == 4 else a1sb
                    for kc in range(NKC):
                        cx.op("pe", lambda h, kc=kc, wsb=wsb, mix=mix: h.matmul(pl[:, :TT], lhsT=wsb[:, kc, :], rhs=mix[:, kc, :], start=(kc == 0), stop=(kc == NKC - 1)), [wsb, mix], [pl])
                    if j == 4:
                        cx.op("act", lambda h: h.activation(out=tanhT[:, :], in_=pl[:, :TT], func=AF.Tanh), [pl], [tanhT])
                    else:
                        cx.op("act", lambda h: h.activation(out=ahT[:, :], in_=pl[:, :TT], func=AF.Copy), [pl], [ahT])
                    continue
                for cg in range(16):
                    ppt = pp[ncg % 3]
                    sg = stg[ncg % 2]
                    ncg += 1
                    for k2 in range(2):
                        slot = stream_weights(cx, e1, A["a_w_in"], k2 * 2048, 2048, j * D + cg * 256, 256, ring, ctr)
                        for oc in range(2):
                            for kc in range(16):
                                kk = k2 * 16 + kc
                                cx.op("pe", lambda h, ppt=ppt, slot=slot, oc=oc, kc=kc, kk=kk, mix=mix: h.matmul(
                                    ppt[:, oc, :TT], lhsT=slot[:, kc, oc * 128:(oc + 1) * 128], rhs=mix[:, kk, :],
                                    start=(kk == 0), stop=(kk == NKC - 1)), [slot, mix], [ppt])
                    if j == 3:
                        cx.op("act", lambda h, ppt=ppt, sg=sg: h.activation(out=sg[:, :, :], in_=ppt[:, :, :TT], func=AF.Silu), [ppt], [sg])
                    elif cg % 2 == 0:
                        cx.op("act", lambda h, ppt=ppt, sg=sg: h.activation(out=sg[:, :, :], in_=ppt[:, :, :TT], func=AF.Copy), [ppt], [sg])
                    else:
                        cx.op("act", lambda h, ppt=ppt, sg=sg: h.activation(out=sg[:, :, :], in_=ppt[:, :, :TT], func=AF.Copy), [ppt], [sg])
                    cx.dma("sp", scr_p[j, cg * 256:(cg + 1) * 256, 0:TT].rearrange("(o p) t -> p o t", p=128), sg[:, :, :], sg, reads=[sg], writes=[scr_p])
            if ring is None:
                cx.end_scope(e1a)
                e1a.__exit__(None, None, None)
            cx.end_scope(e1)
        cx.end_scope(ex)
      with Scope() as ey:
        oT = cx.sb(ey, [128, NKC, TT], BF16, "oT")
        with Scope() as e2:
            if "2" in phases:
                phase_a2(cx, e2, job, A, K, Sst, tanhT, ahT, oT)
            cx.end_scope(e2)
        with Scope() as e3:
            ring = [cx.sb(e3, [128, 8, 512], BF16, "wr3") for _ in range(4)]
            xres = [cx.sb(e3, [128, 512], F32, "xres") for _ in range(4)]
            hsb = [cx.sb(e3, [128, 512], F32, "hsb") for _ in range(4)]
            junk = cx.sb(e3, [128, 512], BF16, "junk3")
            ssq = cx.sb(e3, [128, 4, 8], F32, "ssq")
            sst = cx.sb(e3, [128, 4], F32, "sst")
            pp = [cx.ps(e3, [128, 4, 512], F32, "pp3") for _ in range(2)]
            ctr = [0]
            n = 0
            for cg in (range(8) if "3" in phases else ()):
                ppt = pp[cg % 2]
                for k4 in range(4):
                    slot = stream_weights(cx, e3, A["a_w_out"], k4 * 1024, 1024, cg * 512, 512, ring, ctr)
                    for s in range(nsub):
                        for kc in range(8):
                            kk = k4 * 8 + kc
                            cx.op("pe", lambda h, ppt=ppt, slot=slot, s=s, kc=kc, kk=kk: h.matmul(
                                ppt[:P, s, :], lhsT=oT[:, kk, s * 128:s * 128 + P], rhs=slot[:, kc, :],
                                start=(kk == 0), stop=(kk == NKC - 1)), [slot, oT], [ppt])
                for s in range(nsub):
                    xr, hb = xres[n % 4], hsb[n % 4]
                    n += 1
                    cx.dma("sp", xr[:P, :], job.x[tok0 + s * 128:tok0 + s * 128 + P, cg * 512:(cg + 1) * 512], xr, writes=[xr])
                    cx.op("dve", lambda h, ppt=ppt, s=s, xr=xr, hb=hb: h.tensor_tensor(out=hb[:P, :], in0=ppt[:P, s, :], in1=xr[:P, :], op=ALU.add), [ppt, xr], [hb])
                    cx.op("act", lambda h, hb=hb, s=s, cg=cg: h.activation(out=junk[:P, :], in_=hb[:P, :], func=AF.Square, accum_out=ssq[:P, s, cg:cg + 1]), [hb], [junk, ssq])
                    cx.dma("sp", job.h_scr[tok0 + s * 128:tok0 + s * 128 + P, cg * 512:(cg + 1) * 512], hb[:P, :], hb, reads=[hb], writes=[job.h_scr])
            cx.op("dve", lambda h: h.tensor_reduce(out=sst[:P, :nsub], in_=ssq[:P, :nsub, :], axis=AX.X, op=ALU.add), [ssq], [sst])
            rms_rstd(cx, (sst, sst[:P, :nsub]), (hrstd, hrstd[:P, t * 4:t * 4 + nsub]), P, 1.0 / D, 1e-6)
            cx.end_scope(e3)
        cx.end_scope(ey)
      with Scope() as ez:
        xnT = cx.sb(ez, [128, NKC, TT + 1], BF16, "hnTa")
        with Scope() as e4:
          if "4" in phases:
            norm_transpose(cx, e4, job, lambda s, P: job.h_scr[tok0 + s * 128:tok0 + s * 128 + P, :], None, V_KVG, xnT, 0,
                           vecs, identb, compute_rstd=False, hrstd=hrstd, t=t)
          cx.end_scope(e4)
        with Scope() as e5:
            ring = [cx.sb(e5, [128, 8, 512], BF16, "wr5") for _ in range(4)]
            pp = [cx.ps(e5, [128, 4, 512], F32, "pp5") for _ in range(2)]
            ksb = [cx.sb(e5, [128, 4, 128], F32, "ksb") for _ in range(3)]
            ksq = cx.sb(e5, [128, 4, 128], F32, "ksq")
            kss = [cx.sb(e5, [128, 4], F32, "kss") for _ in range(2)]
            gkb = K["gkb"]
            ctr = [0]
            n = 0
            for cg in (range(4) if "5" in phases else ()):
                ppt = pp[cg % 2]
                for k4 in range(4):
                    slot = stream_weights(cx, e5, A["w_kv"], k4 * 1024, 1024, cg * 512, 512, ring, ctr)
                    for s in range(nsub):
                        for kc in range(8):
                            kk = k4 * 8 + kc
                            cx.op("pe", lambda h, ppt=ppt, slot=slot, s=s, kc=kc, kk=kk: h.matmul(
                                ppt[:P, s, :], lhsT=xnT[:, kk, s * 128:s * 128 + P], rhs=slot[:, kc, :],
                                start=(kk == 0), stop=(kk == NKC - 1)), [slot, xnT], [ppt])
                for s in range(nsub):
                    kb = ksb[n % 3]
                    ks_ = kss[n % 2]
                    n += 1
                    rows = slice(tok0 + s * 128, tok0 + s * 128 + P)
                    if cg < 2:
                        cx.op("act", lambda h, ppt=ppt, s=s, kb=kb: h.activation(out=kb[:P].rearrange("p a b -> p (a b)"), in_=ppt[:P, s, :], func=AF.Copy), [ppt], [kb])
                        cx.op("dve", lambda h, kb=kb: h.tensor_tensor(out=ksq[:P], in0=kb[:P], in1=kb[:P], op=ALU.mult), [kb], [ksq])
                        cx.op("dve", lambda h, ks_=ks_: h.tensor_reduce(out=ks_[:P, :], in_=ksq[:P], axis=AX.X, op=ALU.add), [ksq], [ks_])
                        rms_rstd(cx, (ks_, ks_[:P, :]), (ks_, ks_[:P, :]), P, 1.0 / 128, 1e-6)
                        cx.op("dve", lambda h, kb=kb, ks_=ks_: h.tensor_tensor(out=kb[:P], in0=kb[:P], in1=ks_[:P, :].unsqueeze(2).to_broadcast([P, 4, 128]), op=ALU.mult), [kb, ks_], [kb])
                        cx.op("dve", lambda h, kb=kb: h.tensor_tensor(out=kb[:P], in0=kb[:P], in1=gkb[:P, :].unsqueeze(1).to_broadcast([P, 4, 128]), op=ALU.mult), [kb, gkb], [kb])
                        cx.dma("sp", job.k_out[rows, cg * 512:(cg + 1) * 512], kb[:P].rearrange("p a b -> p (a b)"), kb, reads=[kb], writes=[job.k_out])
                    else:
                        cx.op("act", lambda h, ppt=ppt, s=s, kb=kb: h.activation(out=kb[:P].rearrange("p a b -> p (a b)"), in_=ppt[:P, s, :], func=AF.Copy), [ppt], [kb])
                        cx.dma("sp", job.v_out[rows, (cg - 2) * 512:(cg - 1) * 512], kb[:P].rearrange("p a b -> p (a b)"), kb, reads=[kb], writes=[job.v_out])
            cx.end_scope(e5)
        cx.end_scope(ez)
      cx.end_scope(et)


def _rr(*gens):
    gens = list(gens)
    while gens:
        for g in list(gens):
            try:
                next(g)
            except StopIteration:
                gens.remove(g)


def phase_a2(cx, es, job, A, K, Sst, tanhT, ahT, oT):
    TT, C = job.TT, job.C
    nch = TT // C
    nst = {128: 6, 8: 2}[C]
    identb, identf, bones = K["identb"], K["identf"], K["bones"]
    rmask, ident2, scr_p, vecs = K["rmask"], K["ident2"], K["scr_p"], K["vecs"]
    w2sb = cx.sb(es, [128, D], BF16, "w2sb")
    a2sb = cx.sb(es, [128, D], BF16, "a2sb")
    cx.dma("pool", w2sb[:], A["a_w2"], w2sb, writes=[w2sb])
    cx.dma("pool", a2sb[:], A["a_a2"], a2sb, writes=[a2sb])
    rin = [cx.sb(es, [128, 4, TT], F32, "rin") for _ in range(2)]

    def f32t(name):
        return cx.sb(es, [128, TT], F32, name)

    def bft(name):
        return cx.sb(es, [128, TT], BF16, name)
    lwp, al, clp, egi, egm, kkr, rn, kk, t1, kp, tmp, mean, m2, var, yc = [f32t(n) for n in (
        "lwp", "al", "clp", "egi", "egm", "kkr", "rn", "kk", "t1", "kp", "tmp", "mean", "m2", "var", "yc")]
    sq, rkb, ysq, ysbb = [bft(n) for n in ("sq", "rkb", "ysq", "ysbb")]
    eg2 = [f32t("eg") for _ in range(2)]
    bon2 = [f32t("bon") for _ in range(2)]
    ysb2 = [f32t("ysb") for _ in range(2)]
    bt2 = [bft("bt") for _ in range(2)]
    kt2 = [bft("kt") for _ in range(2)]
    vbf2 = [bft("vbf") for _ in range(2)]
    at22 = [[bft("at0"), bft("at1")] for _ in range(2)]
    rt22 = [[bft("rt0"), bft("rt1")] for _ in range(2)]
    for par_ in range(2):
        for z in at22[par_] + rt22[par_]:
            cx.op("dve", lambda h, z=z: h.memset(z[:], 0.0), [], [z])
    onesT = f32t("onesT")
    cx.op("dve", lambda h: h.memset(onesT[:], 1.0), [], [onesT])
    tokm = [cx.sb(es, [128, 3, 128], BF16, "tokm") for _ in range(2)]
    UA = [cx.sb(es, [128, 4, 128], BF16, "UA") for _ in range(2)]
    UB = [cx.sb(es, [128, 4, 128], BF16, "UB") for _ in range(2)]
    NTs = [cx.sb(es, [128, 2, 128], BF16, "NTs") for _ in range(2)]
    PPs = [[cx.sb(es, [128, 4, 128], BF16, "PP") for _ in range(3)] for _ in range(2)]
    RRs = [[cx.sb(es, [128, 4, 128], BF16, "RR") for _ in range(3)] for _ in range(2)]
    Sbf = cx.sb(es, [128, 64], BF16, "Sbf")
    Zb = cx.sb(es, [128, 2, 64], BF16, "Zb")
    Ub = cx.sb(es, [128, 2, 64], BF16, "Ub")
    tmpS = cx.sb(es, [128, 64], F32, "tmpS")
    pA = cx.ps(es, [128, 4, 128], F32, "pA")
    pPPs = [cx.ps(es, [128, 4, 128], F32, "pPP") for _ in range(2)]
    pRRs = [cx.ps(es, [128, 4, 128], F32, "pRR") for _ in range(2)]
    pX0, pX1 = pPPs[1], pRRs[1]
    pXv = [pX0.h[:].rearrange("p a b -> p (a b)"), pX1.h[:].rearrange("p a b -> p (a b)")]
    pm1 = cx.ps(es, [128, 512], F32, "pm1")
    pC = pZ = pU = pm1
    pm2 = cx.ps(es, [128, 512], F32, "pm2")
    pY = pS = pm2
    ptr = cx.ps(es, [128, 3, 128], BF16, "ptr")

    def V(idx, hp):
        return vecs[:, idx, hp:hp + 1]

    def make(hp):
        par = hp % 2
        r_in = rin[par]
        eg, bon, ysb, bt, kt, vbf, at2, rt2 = eg2[par], bon2[par], ysb2[par], bt2[par], kt2[par], vbf2[par], at22[par], rt22[par]

        def S1():
            if True:
                pass
                cx.dma("sp", r_in[:, :, :], scr_p[:, hp * 128:(hp + 1) * 128, 0:TT].rearrange("j p t -> p j t"), r_in, reads=[scr_p], writes=[r_in])
                r_ap, k_ap, v_ap, g_ap = r_in[:, 0, :], r_in[:, 1, :], r_in[:, 2, :], r_in[:, 3, :]
                fs = slice(hp * 128, (hp + 1) * 128)
                cx.op("pe", lambda h, fs=fs: h.matmul(pXv[0][:, :TT], lhsT=w2sb[:, fs], rhs=tanhT[:, :], start=True, stop=True), [w2sb, tanhT], [pX0])
                cx.op("pe", lambda h, fs=fs: h.matmul(pXv[1][:, :TT], lhsT=a2sb[:, fs], rhs=ahT[:, :], start=True, stop=True), [a2sb, ahT], [pX1])
                cx.op("act", lambda h, hp=hp: h.activation(out=lwp[:], in_=pXv[0][:, :TT], func=AF.Sigmoid, bias=V(V_W0, hp)), [pX0, vecs], [lwp])
                yield
                cx.op("act", lambda h, hp=hp: h.activation(out=al[:], in_=pXv[1][:, :TT], func=AF.Sigmoid, bias=V(V_A0, hp)), [pX1, vecs], [al])
                for ch in range(nch):
                    cs = slice(ch * C, (ch + 1) * C)
                    cx.op("dve", lambda h, cs=cs: h.tensor_tensor_scan(out=clp[:, cs], data0=onesT[:, cs], data1=lwp[:, cs], initial=0.0, op0=ALU.mult, op1=ALU.add), [onesT, lwp], [clp])
                cx.op("act", lambda h: h.activation(out=eg[:], in_=clp[:], func=AF.Exp, scale=-C0), [clp], [eg])
                yield
                cx.op("act", lambda h: h.activation(out=egi[:], in_=clp[:], func=AF.Exp, scale=C0), [clp], [egi])
                cx.op("dve", lambda h: h.tensor_tensor(out=tmp[:], in0=clp[:], in1=lwp[:], op=ALU.subtract), [clp, lwp], [tmp])
                cx.op("act", lambda h: h.activation(out=egm[:], in_=tmp[:], func=AF.Exp, scale=-C0), [tmp], [egm])
                yield
                cx.op("dve", lambda h, hp=hp, k_ap=k_ap: h.tensor_scalar(out=kkr[:], in0=k_ap, scalar1=V(V_KK, hp), scalar2=None, op0=ALU.mult), [r_in, vecs], [kkr])
                cx.op("act", lambda h: h.activation(out=sq[:], in_=kkr[:], func=AF.Square), [kkr], [sq])
                cx.op("pe", lambda h: h.matmul(pXv[0][:, :TT], lhsT=bones[:], rhs=sq[:], start=True, stop=True), [bones, sq], [pX0])
                yield
                cx.op("dve", lambda h: h.tensor_scalar(out=rn[:], in0=pXv[0][:, :TT], scalar1=1e-24, scalar2=None, op0=ALU.max), [pX0], [rn])
                cx.op("act", lambda h: h.activation(out=rn[:], in_=rn[:], func=AF.Ln), [rn], [rn])
                cx.op("act", lambda h: h.activation(out=rn[:], in_=rn[:], func=AF.Exp, scale=-0.5), [rn], [rn])
                yield
                cx.op("dve", lambda h: h.tensor_tensor(out=kk[:], in0=kkr[:], in1=rn[:], op=ALU.mult), [kkr, rn], [kk])
                cx.op("dve", lambda h, hp=hp: h.tensor_scalar(out=t1[:], in0=al[:], scalar1=-1.0, scalar2=V(V_KA, hp), op0=ALU.add, op1=ALU.mult), [al, vecs], [t1])
                cx.op("dve", lambda h, k_ap=k_ap: h.scalar_tensor_tensor(out=kp[:], in0=t1[:], scalar=1.0, in1=k_ap, op0=ALU.add, op1=ALU.mult), [t1, r_in], [kp])
                yield
                for e in range(2):
                    po = slice(64 * e, 64 * e + 64)
                    cx.op("dve", lambda h, e=e, po=po: h.scalar_tensor_tensor(out=at2[e][po, :], in0=kk[po, :], scalar=-1.0, in1=egm[po, :], op0=ALU.mult, op1=ALU.mult), [kk, egm], [at2[e]])
                cx.op("pool", lambda h: h.tensor_tensor(out=tmp[:], in0=kk[:], in1=al[:], op=ALU.mult), [kk, al], [tmp])
                cx.op("pool", lambda h: h.tensor_tensor(out=bt[:], in0=tmp[:], in1=egi[:], op=ALU.mult), [tmp, egi], [bt])
                yield
                cx.op("pool", lambda h: h.tensor_tensor(out=kt[:], in0=kp[:], in1=egi[:], op=ALU.mult), [kp, egi], [kt])
                for e in range(2):
                    po = slice(64 * e, 64 * e + 64)
                    cx.op("pool", lambda h, e=e, po=po, r_in=r_in: h.tensor_tensor(out=rt2[e][po, :], in0=r_in[po, 0, :], in1=eg[po, :], op=ALU.mult), [r_in, eg], [rt2[e]])
                cx.op("dve", lambda h, hp=hp, r_ap=r_ap: h.scalar_tensor_tensor(out=rkb[:], in0=r_ap, scalar=V(V_RK, hp), in1=kp[:], op0=ALU.mult, op1=ALU.mult), [r_in, vecs, kp], [rkb])
                yield
                cx.op("pe", lambda h: h.matmul(pXv[1][:, :TT], lhsT=bones[:], rhs=rkb[:], start=True, stop=True), [bones, rkb], [pX1])
                cx.op("dve", lambda h, v_ap=v_ap: h.tensor_tensor(out=bon[:], in0=pXv[1][:, :TT], in1=v_ap, op=ALU.mult), [pX1, r_in], [bon])
                cx.op("act", lambda h, v_ap=v_ap: h.activation(out=vbf[:], in_=v_ap, func=AF.Copy), [r_in], [vbf])
                yield
            yield

        if True:
            def s2(ch, idx):
                cs = slice(ch * C, (ch + 1) * C)
                tk = tokm[idx]
                ua, ub, nts = UA[idx], UB[idx], NTs[idx]
                for i, srcT in enumerate((vbf, bt, kt)):
                    cx.op("pe", lambda h, i=i, srcT=srcT, cs=cs: h.transpose(out=ptr[:C, i, :], in_=srcT[:, cs], identity=identb[:]), [srcT, identb], [ptr])
                cx.op("act", lambda h, tk=tk: h.activation(out=tk[:C], in_=ptr[:C], func=AF.Copy), [ptr], [tk])
                for e in range(2):
                    cx.op("pe", lambda h, e=e, cs=cs: h.matmul(pA[:C, e, :C], lhsT=bt[:, cs], rhs=at2[e][:, cs], start=True, stop=True), [bt, at2[e]], [pA])
                    cx.op("pe", lambda h, e=e, cs=cs: h.matmul(pA[:C, 2 + e, :C], lhsT=kt[:, cs], rhs=at2[e][:, cs], start=True, stop=True), [kt, at2[e]], [pA])
                cx.op("dve", lambda h, ua=ua: h.tensor_tensor(out=ua[:C, :, :C], in0=pA[:C, :, :C], in1=rmask[:C, 0, :, :C], op=ALU.mult), [pA, rmask], [ua])
                for e in range(2):
                    cx.op("pe", lambda h, e=e, cs=cs: h.matmul(pA[:C, e, :C], lhsT=bt[:, cs], rhs=rt2[e][:, cs], start=True, stop=True), [bt, rt2[e]], [pA])
                    cx.op("pe", lambda h, e=e, cs=cs: h.matmul(pA[:C, 2 + e, :C], lhsT=kt[:, cs], rhs=rt2[e][:, cs], start=True, stop=True), [kt, rt2[e]], [pA])
                cx.op("dve", lambda h, ub=ub: h.tensor_tensor(out=ub[:C, :, :C], in0=pA[:C, :, :C], in1=rmask[:C, 1, :, :C], op=ALU.mult), [pA, rmask], [ub])
                for e in range(2):
                    cx.op("pe", lambda h, e=e, cs=cs: h.matmul(pm1[:C, e * 128:e * 128 + C], lhsT=at2[e][:, cs], rhs=bt[:, cs], start=True, stop=True), [at2[e], bt], [pC])
                cx.op("dve", lambda h, nts=nts: h.tensor_tensor(out=nts[:C, :, :C], in0=pm1[:C, 0:256].rearrange("p (e c) -> p e c", e=2)[:, :, :C], in1=rmask[:C, 2, 0:2, :C], op=ALU.mult), [pC, rmask], [nts])

            def s3(idx, st, res):
                ua, nts = UA[idx], NTs[idx]
                pPP, pRR, PP, RR = pPPs[0], pRRs[0], PPs[st], RRs[st]
                rr = RR[0]
                cx.op("dve", lambda h: h.tensor_tensor(out=rr[:C, 0:2, :C], in0=ua[:C, 0:2, :C], in1=ident2[:C, 0:2, :C], op=ALU.add), [ua, ident2], [rr])
                cx.op("dve", lambda h: h.tensor_tensor(out=rr[:C, 2:4, :C], in0=nts[:C, :, :C], in1=ident2[:C, 0:2, :C], op=ALU.add), [nts, ident2], [rr])
                ppc = PP[0]
                cx.op("act", lambda h: h.activation(out=ppc[:C, 0:2, :C], in_=ua[:C, 0:2, :C], func=AF.Copy), [ua], [ppc])
                cx.op("act", lambda h: h.activation(out=ppc[:C, 2:4, :C], in_=nts[:C, :, :C], func=AF.Copy), [nts], [ppc])
                yield
                for i in range(1, nst + 1):
                    lastst = (i == nst)
                    ppn = PP[i % 3]
                    rrn = RR[i % 3]
                    for e in range(2):
                        cx.op("pe", lambda h, e=e: h.matmul(pPP[:C, e, :C], lhsT=ppc[:C, 2 + e, :C], rhs=ppc[:C, e, :C], start=True, stop=True), [ppc], [pPP])
                        if not lastst:
                            cx.op("pe", lambda h, e=e: h.matmul(pPP[:C, 2 + e, :C], lhsT=ppc[:C, e, :C], rhs=ppc[:C, 2 + e, :C], start=True, stop=True), [ppc], [pPP])
                    nq = 2 if lastst else 4
                    cx.op("act", lambda h: h.activation(out=ppn[:C, 0:nq, :C], in_=pPP[:C, 0:nq, :C], func=AF.Copy), [pPP], [ppn])
                    yield
                    for e in range(2):
                        cx.op("pe", lambda h, e=e: h.matmul(pRR[:C, e, :C], lhsT=rr[:C, 2 + e, :C], rhs=ppn[:C, e, :C], start=True, stop=True), [rr, ppn], [pRR])
                        if not lastst:
                            cx.op("pe", lambda h, e=e: h.matmul(pRR[:C, 2 + e, :C], lhsT=ppn[:C, e, :C], rhs=rr[:C, 2 + e, :C], start=True, stop=True), [rr, ppn], [pRR])
                    cx.op("dve", lambda h: h.tensor_tensor(out=rrn[:C, 0:nq, :C], in0=pRR[:C, 0:nq, :C], in1=rr[:C, 0:nq, :C], op=ALU.add), [pRR, rr], [rrn])
                    ppc, rr = ppn, rrn
                    yield
                res.append(rr)

            def s4(ch, idx, rr):
                cs = slice(ch * C, (ch + 1) * C)
                tk = tokm[idx]
                ua, ub = UA[idx], UB[idx]
                cx.op("act", lambda h, hp=hp: h.activation(out=Sbf[:], in_=Sst[:, hp, :], func=AF.Copy), [Sst], [Sbf])
                for e in range(2):
                    po = slice(64 * e, 64 * e + 64)
                    zc = slice(256 + e * 64, 256 + e * 64 + 64)
                    cx.op("pe", lambda h, e=e, po=po, zc=zc, cs=cs: h.matmul(pm1[:C, zc], lhsT=at2[e][:, cs], rhs=Sbf[:, :], start=True, stop=False), [at2[e], Sbf], [pZ])
                    cx.op("pe", lambda h, e=e, po=po, zc=zc, ua=ua, tk=tk: h.matmul(pm1[:C, zc], lhsT=ua[:C, 2 + e, :C], rhs=tk[:C, 0, po], start=False, stop=True), [ua, tk], [pZ])
                cx.op("act", lambda h: h.activation(out=Zb[:C].rearrange("p e v -> p (e v)"), in_=pm1[:C, 256:384], func=AF.Copy), [pZ], [Zb])
                yield
                for e in range(2):
                    uc = slice(384 + e * 64, 384 + e * 64 + 64)
                    cx.op("pe", lambda h, e=e, uc=uc, rr=rr: h.matmul(pm1[:C, uc], lhsT=rr[:C, e, :C], rhs=Zb[:C, e, :], start=True, stop=True), [rr, Zb], [pU])
                cx.op("dve", lambda h: h.tensor_copy(out=Ub[:C].rearrange("p e v -> p (e v)"), in_=pm1[:C, 384:512]), [pU], [Ub])
                yield
                for e in range(2):
                    po = slice(64 * e, 64 * e + 64)
                    cx.op("pe", lambda h, e=e, po=po, cs=cs: h.matmul(pm2[po, 0:C], lhsT=Sbf[:, :], rhs=rt2[e][:, cs], start=True, stop=False), [Sbf, rt2[e]], [pY])
                    cx.op("pe", lambda h, e=e, po=po, ub=ub: h.matmul(pm2[po, 0:C], lhsT=Ub[:C, e, :], rhs=ub[:C, e, :C], start=False, stop=False), [Ub, ub], [pY])
                    cx.op("pe", lambda h, e=e, po=po, ub=ub, tk=tk: h.matmul(pm2[po, 0:C], lhsT=tk[:C, 0, po], rhs=ub[:C, 2 + e, :C], start=False, stop=True), [tk, ub], [pY])
                cx.op("act", lambda h, cs=cs: h.activation(out=ysb[:, cs], in_=pm2[:, 0:C], func=AF.Copy), [pY], [ysb])
                for e in range(2):
                    po = slice(64 * e, 64 * e + 64)
                    cx.op("pe", lambda h, e=e, po=po, tk=tk: h.matmul(pm2[po, 128:192], lhsT=tk[:C, 1, po], rhs=Ub[:C, e, :], start=True, stop=False), [tk, Ub], [pS])
                    cx.op("pe", lambda h, e=e, po=po, tk=tk: h.matmul(pm2[po, 128:192], lhsT=tk[:C, 2, po], rhs=tk[:C, 0, po], start=False, stop=True), [tk], [pS])
                cx.op("dve", lambda h, hp=hp: h.tensor_tensor(out=tmpS[:], in0=pm2[:, 128:192], in1=Sst[:, hp, :], op=ALU.add), [pS, Sst], [tmpS])
                gcol = (ch + 1) * C - 1
                cx.op("dve", lambda h, hp=hp, gcol=gcol: h.tensor_scalar(out=Sst[:, hp, :], in0=tmpS[:], scalar1=eg[:, gcol:gcol + 1], scalar2=None, op0=ALU.mult), [tmpS, eg], [Sst])


        def chunks():
            for c0 in range(0, nch, 2):
                chs = [c for c in (c0, c0 + 1) if c < nch]
                for idx, ch in enumerate(chs):
                    s2(ch, idx)
                    yield
                res = [[] for _ in chs]
                for idx in range(len(chs)):
                    yield from s3(idx, idx, res[idx])
                for idx, ch in enumerate(chs):
                    yield from s4(ch, idx, res[idx][0])
                    yield

        def S5():
            g_ap = r_in[:, 3, :]
            if True:
                cx.op("act", lambda h: h.activation(out=ysq[:], in_=ysb[:], func=AF.Square), [ysb], [ysq])
                cx.op("act", lambda h: h.activation(out=ysbb[:], in_=ysb[:], func=AF.Copy), [ysb], [ysbb])
                cx.op("pe", lambda h: h.matmul(pXv[0][:, :TT], lhsT=bones[:], rhs=ysbb[:], start=True, stop=True), [bones, ysbb], [pX0])
                yield
                cx.op("pe", lambda h: h.matmul(pXv[1][:, :TT], lhsT=bones[:], rhs=ysq[:], start=True, stop=True), [bones, ysq], [pX1])
                cx.op("dve", lambda h: h.tensor_scalar(out=mean[:], in0=pXv[0][:, :TT], scalar1=1.0 / 64, scalar2=None, op0=ALU.mult), [pX0], [mean])
                cx.op("dve", lambda h: h.tensor_tensor(out=m2[:], in0=mean[:], in1=mean[:], op=ALU.mult), [mean], [m2])
                yield
                cx.op("dve", lambda h: h.scalar_tensor_tensor(out=var[:], in0=pXv[1][:, :TT], scalar=1.0 / 64, in1=m2[:], op0=ALU.mult, op1=ALU.subtract), [pX1, m2], [var])
                cx.op("dve", lambda h: h.tensor_scalar(out=var[:], in0=var[:], scalar1=64e-5, scalar2=None, op0=ALU.add), [var], [var])
                cx.op("act", lambda h: h.activation(out=var[:], in_=var[:], func=AF.Ln), [var], [var])
                yield
                cx.op("act", lambda h: h.activation(out=var[:], in_=var[:], func=AF.Exp, scale=-0.5), [var], [var])
                cx.op("pool", lambda h: h.tensor_tensor(out=yc[:], in0=ysb[:], in1=mean[:], op=ALU.subtract), [ysb, mean], [yc])
                cx.op("dve", lambda h: h.tensor_tensor(out=yc[:], in0=yc[:], in1=var[:], op=ALU.mult), [yc, var], [yc])
                yield
                cx.op("act", lambda h, hp=hp: h.activation(out=yc[:], in_=yc[:], func=AF.Identity, scale=V(V_LG, hp), bias=V(V_LB, hp)), [yc, vecs], [yc])
                cx.op("pool", lambda h: h.tensor_tensor(out=yc[:], in0=yc[:], in1=bon[:], op=ALU.add), [yc, bon], [yc])
                cx.op("dve", lambda h, hp=hp, g_ap=g_ap: h.tensor_tensor(out=oT[:, hp, :], in0=yc[:], in1=g_ap, op=ALU.mult), [yc, r_in], [oT])
                yield
            yield
        return S1, chunks, S5

    gens = [make(hp) for hp in range(NKC)]
    _rr(gens[0][0]())
    for hp in range(NKC):
        def tail(hp=hp):
            if hp >= 1:
                yield from gens[hp - 1][2]()
            if hp + 1 < NKC:
                yield from gens[hp + 1][0]()
        _rr(gens[hp][1](), tail())
    _rr(gens[NKC - 1][2]())


def layer_b(cx, nc, job, A, O, vec, K, hrstd, phases):
    S, TT = job.S, job.TT
    ntile = S // TT
    nsub = (TT + 127) // 128
    P = min(TT, 128)
    identb, onesb, onesf = K["identb"], K["onesb"], K["onesf"]
    amask, gqs, vecs = K["amask"], K["gqs"], K["vecs"]
    nkt_cache = job.ncache
    nkt_own = (S + 127) // 128
    nkt = nkt_cache + nkt_own
    ktscr = K["ktscr"]
    with Scope() as eb:
        with Scope() as e0:
            kst = [cx.sb(e0, [128, 1024], BF16, "kst") for _ in range(2)]
            kts = [cx.sb(e0, [128, 8, 128], BF16, "kts") for _ in range(2)]
            ptk = [cx.ps(e0, [128, 8, 128], BF16, "ptk") for _ in range(2)]
            for kt in (range(nkt) if "6" in phases else ()):
                if kt < nkt_cache:
                    ksrc, rows, pr = A["ck"], slice(kt * 128, kt * 128 + 128), 128
                else:
                    o = kt - nkt_cache
                    pr = min(128, S - o * 128)
                    ksrc, rows = job.k_out.h, slice(o * 128, o * 128 + pr)
                ks_ = kst[kt % 2]
                pk = ptk[kt % 2]
                kb = kts[kt % 2]
                cx.dma("pool", ks_[:pr, :], ksrc[rows, :], ks_, writes=[ks_])
                for hd in range(8):
                    cx.op("pe", lambda h, hd=hd, ks_=ks_, pk=pk, pr=pr: h.transpose(out=pk[:, hd, :pr], in_=ks_[:pr, hd * 128:(hd + 1) * 128], identity=identb[:pr, :pr]), [ks_, identb], [pk])
                if kt % 2 == 0:
                    cx.op("dve", lambda h, pk=pk, kb=kb, pr=pr: h.tensor_copy(out=kb[:, :, :pr], in_=pk[:, :, :pr]), [pk], [kb])
                else:
                    cx.op("act", lambda h, pk=pk, kb=kb, pr=pr: h.activation(out=kb[:, :, :pr], in_=pk[:, :, :pr], func=AF.Copy), [pk], [kb])
                cx.dma("sp", ktscr[:, :, kt * 128:kt * 128 + pr].rearrange("h c t -> c h t"), kb[:, :, :pr], kb, reads=[kb], writes=[ktscr])
            cx.end_scope(e0)
        for t in range(ntile):
            tok0 = t * TT
            with Scope() as et:
                hnT = cx.sb(et, [128, NKC, TT], BF16, "hnT")
                oT = cx.sb(et, [128, NKC, TT], BF16, "oTb")
                with Scope() as e1:
                    if "7" in phases:
                        norm_transpose(cx, e1, job, lambda s, P: job.h_scr[tok0 + s * 128:tok0 + s * 128 + P, :], None, V_BG, hnT, 0,
                                       vecs, identb, compute_rstd=False, hrstd=hrstd, t=t)
                    cx.end_scope(e1)
                with Scope() as e2:
                    ring = [cx.sb(e2, [128, 8, 512], BF16, "wrb") for _ in range(3)]
                    KTh = [cx.sb(e2, [128, nkt * 128], BF16, "KTh") for _ in range(2)]
                    Vh = [cx.sb(e2, [128, nkt, 128], BF16, "Vh") for _ in range(2)]
                    qT = cx.sb(e2, [128, 12, TT], BF16, "qT")
                    sgT = cx.sb(e2, [128, 4, TT], BF16, "sgT")
                    qf = [cx.sb(e2, [128, TT], F32, "qf") for _ in range(4)]
                    sqf = [cx.sb(e2, [128, TT], BF16, "sqf") for _ in range(4)]
                    rq = [cx.sb(e2, [128, TT], F32, "rq") for _ in range(2)]
                    pe_ = [cx.sb(e2, [128, 4, P], BF16, "pexp") for _ in range(4)]
                    rden = cx.sb(e2, [128, 4, P], F32, "rden")
                    of = cx.sb(e2, [128, 4, P], F32, "of")
                    ppq = cx.ps(e2, [128, 4, 512], F32, "ppq")
                    pw = [cx.ps(e2, [128, 512], F32, "pw") for _ in range(2)]
                    pnum = cx.ps(e2, [128, 512], F32, "pnum")
                    pden = cx.ps(e2, [128, 512], F32, "pden")
                    ctr = [0]
                    nw = 0
                    nq = 0
                    npe = 0
                    nkeys = nkt_cache * 128 + S
                    for kvh in (range(8) if "8" in phases else ()):
                        kth, vh = KTh[kvh % 2], Vh[kvh % 2]
                        cx.dma("sp", kth[:, :nkeys], ktscr[kvh, :, :nkeys], kth, reads=[ktscr], writes=[kth])
                        if nkt_cache:
                            cx.dma("pool", vh[:, :nkt_cache, :], A["cv"].rearrange("(kt p) (h c) -> p kt h c", p=128, c=128)[:, :, kvh, :], vh, writes=[vh])
                        if S >= 128:
                            cx.dma("pool", vh[:, nkt_cache:, :], job.v_out.h.rearrange("(kt p) (h c) -> p kt h c", p=128, c=128)[:, :, kvh, :], vh, writes=[vh])
                        else:
                            cx.dma("pool", vh[:S, nkt_cache, :], job.v_out.h[:, kvh * 128:(kvh + 1) * 128], vh, writes=[vh])
                        for g in range(4):
                            col0 = (g * D + kvh * 512) if g < 3 else (3 * D + kvh * 512)
                            for k4 in range(4):
                                slot = stream_weights(cx, e2, A["b_w_in"], k4 * 1024, 1024, col0, 512, ring, ctr)
                                for oc in range(4):
                                    for kc in range(8):
                                        kk = k4 * 8 + kc
                                        cx.op("pe", lambda h, slot=slot, oc=oc, kc=kc, kk=kk: h.matmul(
                                            ppq[:, oc, :TT], lhsT=slot[:, kc, oc * 128:(oc + 1) * 128], rhs=hnT[:, kk, :],
                                            start=(kk == 0), stop=(kk == NKC - 1)), [slot, hnT], [ppq])
                            if g == 3:
                                cx.op("act", lambda h: h.activation(out=sgT[:, :, :], in_=ppq[:, :, :TT], func=AF.Silu), [ppq], [sgT])
                                continue
                            for oc in range(4):
                                q_, s_ = qf[oc], sqf[oc]
                                cx.op("act", lambda h, oc=oc, q_=q_: h.activation(out=q_[:, :], in_=ppq[:, oc, :TT], func=AF.Copy), [ppq], [q_])
                                cx.op("act", lambda h, oc=oc, s_=s_: h.activation(out=s_[:, :], in_=ppq[:, oc, :TT], func=AF.Square), [ppq], [s_])
                            for oc in range(4):
                                q_, s_, r_ = qf[oc], sqf[oc], rq[nq % 2]
                                nq += 1
                                pwt = pw[nw % 2]
                                nw += 1
                                cx.op("pe", lambda h, s_=s_, pwt=pwt: h.matmul(pwt[:, :TT], lhsT=onesb[:], rhs=s_[:, :], start=True, stop=True), [onesb, s_], [pwt])
                                cx.op("dve", lambda h, r_=r_, pwt=pwt: h.tensor_scalar(out=r_[:, :], in0=pwt[:, :TT], scalar1=1.0 / 128, scalar2=1e-6, op0=ALU.mult, op1=ALU.add), [pwt], [r_])
                                cx.op("act", lambda h, r_=r_: h.activation(out=r_[:, :], in_=r_[:, :], func=AF.Ln), [r_], [r_])
                                cx.op("act", lambda h, r_=r_: h.activation(out=r_[:, :], in_=r_[:, :], func=AF.Exp, scale=-0.5), [r_], [r_])
                                cx.op("dve", lambda h, q_=q_, r_=r_, g=g, oc=oc: h.scalar_tensor_tensor(out=qT[:, g * 4 + oc, :], in0=q_[:, :], scalar=gqs[:, g:g + 1], in1=r_[:, :], op0=ALU.mult, op1=ALU.mult), [q_, r_, gqs], [qT])
                        for s in range(nsub):
                            qa = nkt_cache + t * max(TT // 128, 1) + s
                            units = []
                            for g in range(3):
                                for dl in range(NDELTA[g]):
                                    ktile = qa - dl
                                    if ktile >= 0:
                                        units.append((g, dl, ktile))
                            qs = slice(s * 128, s * 128 + P)
                            for ui, (g, dl, ktile) in enumerate(units):
                                kw = 128
                                if ktile >= nkt_cache:
                                    kw = min(128, S - (ktile - nkt_cache) * 128)
                                pwt = pw[nw % 2]
                                nw += 1
                                pb = pe_[npe % 4]
                                npe += 1
                                mi = MOFF[g] + dl
                                cx.op("pe", lambda h, pwt=pwt, ktile=ktile, kw=kw, g=g, qs=qs, kth=kth: h.matmul(
                                    pwt[:kw, 0:4 * P].rearrange("p (a b) -> p a b", a=4), lhsT=kth[:, ktile * 128:ktile * 128 + kw],
                                    rhs=qT[:, g * 4:g * 4 + 4, qs], start=True, stop=True), [kth, qT], [pwt])
                                cx.op("act", lambda h, pwt=pwt, pb=pb, kw=kw: h.activation(out=pb[:kw].rearrange("p a b -> p (a b)"), in_=pwt[:kw, 0:4 * P], func=AF.Exp), [pwt], [pb])
                                meng = "dve"
                                cx.op(meng, lambda h, pb=pb, kw=kw, mi=mi: h.tensor_tensor(out=pb[:kw], in0=pb[:kw], in1=amask[:kw, mi, :P].unsqueeze(1).to_broadcast([kw, 4, P]), op=ALU.mult), [pb, amask], [pb])
                                first, lastu = (ui == 0), (ui == len(units) - 1)
                                cx.op("pe", lambda h, pb=pb, kw=kw, ktile=ktile, vh=vh, first=first, lastu=lastu: h.matmul(
                                    pnum[:, 0:4 * P], lhsT=vh[:kw, ktile, :], rhs=pb[:kw].rearrange("p a b -> p (a b)"),
                                    start=first, stop=lastu), [vh, pb], [pnum])
                                cx.op("pe", lambda h, pb=pb, kw=kw, first=first, lastu=lastu: h.matmul(
                                    pden[:, 0:4 * P], lhsT=onesb[:kw, :], rhs=pb[:kw].rearrange("p a b -> p (a b)"),
                                    start=first, stop=lastu), [onesb, pb], [pden])
                            cx.op("act", lambda h: h.activation(out=rden[:].rearrange("p a b -> p (a b)"), in_=pden[:, 0:4 * P], func=AF.Ln), [pden], [rden])
                            cx.op("act", lambda h: h.activation(out=rden[:].rearrange("p a b -> p (a b)"), in_=rden[:].rearrange("p a b -> p (a b)"), func=AF.Exp, scale=-1.0), [rden], [rden])
                            cx.op("dve", lambda h: h.tensor_tensor(out=of[:].rearrange("p a b -> p (a b)"), in0=pnum[:, 0:4 * P], in1=rden[:].rearrange("p a b -> p (a b)"), op=ALU.mult), [pnum, rden], [of])
                            cx.op("dve", lambda h, kvh=kvh, qs=qs: h.tensor_tensor(out=oT[:, kvh * 4:kvh * 4 + 4, qs], in0=of[:], in1=sgT[:, :, qs], op=ALU.mult), [of, sgT], [oT])
                    cx.end_scope(e2)
                with Scope() as e3:
                    ring = [cx.sb(e3, [128, 8, 512], BF16, "wr7") for _ in range(4)]
                    xres = [cx.sb(e3, [128, 512], F32, "hres") for _ in range(4)]
                    hsb = [cx.sb(e3, [128, 512], F32, "ysb") for _ in range(4)]
                    pp = [cx.ps(e3, [128, 4, 512], F32, "pp7") for _ in range(2)]
                    ctr = [0]
                    n = 0
                    for cg in (range(8) if "9" in phases else ()):
                        ppt = pp[cg % 2]
                        for k4 in range(4):
                            slot = stream_weights(cx, e3, A["b_w_out"], k4 * 1024, 1024, cg * 512, 512, ring, ctr)
                            for s in range(nsub):
                                for kc in range(8):
                                    kk = k4 * 8 + kc
                                    cx.op("pe", lambda h, ppt=ppt, slot=slot, s=s, kc=kc, kk=kk: h.matmul(
                                        ppt[:P, s, :], lhsT=oT[:, kk, s * 128:s * 128 + P], rhs=slot[:, kc, :],
                                        start=(kk == 0), stop=(kk == NKC - 1)), [slot, oT], [ppt])
                        for s in range(nsub):
                            xr, hb = xres[n % 4], hsb[n % 4]
                            n += 1
                            rows = slice(tok0 + s * 128, tok0 + s * 128 + P)
                            cx.dma("sp", xr[:P, :], job.h_scr[rows, cg * 512:(cg + 1) * 512], xr, reads=[job.h_scr], writes=[xr])
                            cx.op("dve", lambda h, ppt=ppt, s=s, xr=xr, hb=hb: h.tensor_tensor(out=hb[:P, :], in0=ppt[:P, s, :], in1=xr[:P, :], op=ALU.add), [ppt, xr], [hb])
                            cx.dma("sp", job.y[rows, cg * 512:(cg + 1) * 512], hb[:P, :], hb, reads=[hb], writes=[job.y])
                    cx.end_scope(e3)
                cx.end_scope(et)
        cx.end_scope(eb)


def _consts():
    bf = ml_dtypes.bfloat16
    identf = np.eye(128, dtype=np.float32)
    bones = np.zeros((128, 128), np.float32)
    bones[:64, :64] = 1
    bones[64:, 64:] = 1
    i = np.arange(128)
    su = (i[:, None] < i[None, :]).astype(np.float32)
    iu = (i[:, None] <= i[None, :]).astype(np.float32)
    sl = (i[None, :] < i[:, None]).astype(np.float32)
    rmask = np.stack([np.stack([m] * 4, 1) for m in (su, iu, sl)], 1)
    am = np.zeros((128, 24, 128), np.float32)
    for g in range(3):
        for dl in range(NDELTA[g]):
            d = 128 * dl + i[None, :] - i[:, None]
            ok = (d >= 0) & (d % DILS[g] == 0) & (d // DILS[g] <= 128)
            am[:, MOFF[g] + dl, :] = ok
    return dict(identf=identf, bones=bones, rmask=np.ascontiguousarray(rmask), amask=am)


def _fm(v):
    return np.ascontiguousarray(np.asarray(v, np.float32).reshape(NKC, 128).T)


_CACHE = {}


def kernel(x_prompt, x_sample, state_wkv, state_shift, cache_k, cache_v,
           a_norm_g, a_mu, a_w_in, a_w0, a_w1, a_w2, a_a0, a_a1, a_a2,
           a_k_k, a_k_a, a_r_k, a_lnx_g, a_lnx_b, a_w_out,
           kv_norm_g, w_kv, k_norm_g, b_norm_g, b_w_in, q_norm_g, b_w_out, _SP=None, _NC=8, _PH="AB0123456789", _DP=True, _DS=True):
    f = lambda a: np.ascontiguousarray(np.asarray(a, np.float32))
    B, S, _ = x_prompt.shape
    SP = _SP or S
    key = (SP, _PH, _DP, _DS)
    if key not in _CACHE:
        _CACHE[key] = build_program(SP, phases=_PH, do_prompt=_DP, do_sample=_DS)
    nc = _CACHE[key]
    cs = _consts()
    vlist = [a_norm_g[0]] + [a_mu[0, j] for j in range(6)] + [a_w0[0], a_a0[0], a_k_k[0], a_k_a[0],
             np.asarray(a_r_k[0]).reshape(-1), a_lnx_g[0], a_lnx_b[0], kv_norm_g, b_norm_g[0]]
    vecs = np.ascontiguousarray(np.stack([_fm(v) for v in vlist], 1))
    shared = dict(
        a_w_in=f(a_w_in[0]), a_w1=f(a_w1[0]), a_w2=f(a_w2[0]), a_a1=f(a_a1[0]), a_a2=f(a_a2[0]),
        a_w_out=f(a_w_out[0]), w_kv=f(w_kv), b_w_in=f(b_w_in[0]), b_w_out=f(b_w_out[0]),
        vecs=vecs, grow=f(a_norm_g[0]), gq=np.ascontiguousarray(f(q_norm_g[0]).T),
        gkb=np.ascontiguousarray(np.broadcast_to(f(k_norm_g)[None, :], (128, 128))),
        identf=cs["identf"], bones=cs["bones"], rmask=cs["rmask"], amask=cs["amask"])
    in_maps = []
    for c in range(_NC):
        m = dict(shared)
        m["xp"] = f(x_prompt[c % B, :SP])
        m["xs"] = f(x_sample[c])
        m["swkv"] = f(state_wkv[0, c]).reshape(D, 64)
        m["sshift"] = _fm(state_shift[0, c])
        m["ck"] = f(cache_k[c]).reshape(-1, 1024)
        m["cv"] = f(cache_v[c]).reshape(-1, 1024)
        in_maps.append(m)
    res = run_bass_kernel_spmd(nc, in_maps, core_ids=list(range(_NC)))
    R = list(res.results)
    while len(R) < 8:
        R.append(R[0])
    y_p = np.stack([R[b]["yp"] for b in range(B)])
    y_s = np.stack([R[c]["ys"] for c in range(8)])
    wkv_p = np.stack([R[b]["wkvp"] for b in range(B)])[None]
    sh_p = np.stack([R[b]["shiftp"] for b in range(B)])[None]
    k_p = np.stack([R[b]["kp"].reshape(SP, 8, 128) for b in range(B)])
    v_p = np.stack([R[b]["vp"].reshape(SP, 8, 128) for b in range(B)])
    wkv_s = np.stack([R[c]["wkvs"] for c in range(8)])[None]
    sh_s = np.stack([R[c]["shifts"] for c in range(8)])[None]
    k_s = np.stack([R[c]["ks"].reshape(8, 8, 128) for c in range(8)])
    v_s = np.stack([R[c]["vs"].reshape(8, 8, 128) for c in range(8)])
    return (y_p, y_s, wkv_p, sh_p, k_p, v_p, wkv_s, sh_s, k_s, v_s)
```
